# Optimizing a Trainium2 kernel written in Bass

```python
import jax, jax.numpy as jnp
from jax import lax
import numpy as np

D_MODEL = 1024
BATCH = 8
SEQ = 8192
DEPTH = 2
DEC_BATCH = 8
DEC_SEQ = 64
PAST_LEN = 4096

CHUNK = 64
PLE_DIM = 256
EPS = 1e-6
NEG_INF = -1e30
Q_BLOCK = 128
MLA_HEADS = 8
MLA_NOPE = 64
MLA_ROPE = 32
MLA_V = 64
MLA_Q_RANK = 256
MLA_KV_RANK = 128
MLA_WIDTH = MLA_HEADS * MLA_V
ROPE_THETA = 10000.0
POOL_WINDOWS = (2, 4, 8, 16)
POOL_GROUP_DIM = 64
POOL_WIDTH = 4 * POOL_GROUP_DIM
POOL_HIST = 15
CA_HEADS = 4
CA_HEAD_DIM = 64
CA_WIDTH = CA_HEADS * CA_HEAD_DIM
CA_LEFT_CHUNKS = 8
CA_WINDOW = CA_LEFT_CHUNKS * CHUNK
CA_MAX_REL = 128
N_BRANCH = 3
IN_SIZES = (MLA_Q_RANK, MLA_KV_RANK + MLA_ROPE, MLA_WIDTH, POOL_WIDTH, POOL_WIDTH, 3 * CA_WIDTH, CA_WIDTH, N_BRANCH * D_MODEL)
IN_TOTAL = MLA_Q_RANK + MLA_KV_RANK + MLA_ROPE + MLA_WIDTH + 2 * POOL_WIDTH + 4 * CA_WIDTH + N_BRANCH * D_MODEL

kernel_name = "hybrid_stream_mla_pool_chunkattn_step"


def rmsnorm(x, g):
    xf = x.astype(jnp.float32)
    y = xf * lax.rsqrt(jnp.mean(xf * xf, axis=-1, keepdims=True) + EPS)
    return (y * g.astype(jnp.float32)).astype(x.dtype)


def rope(x, pos):
    half = x.shape[-1] // 2
    inv = ROPE_THETA ** (-jnp.arange(half, dtype=jnp.float32) / half)
    ang = pos.astype(jnp.float32)[:, None] * inv[None, :]
    ang = ang.reshape((ang.shape[0],) + (1,) * (x.ndim - 3) + (half,))
    cos, sin = jnp.cos(ang), jnp.sin(ang)
    xf = x.astype(jnp.float32)
    x1, x2 = xf[..., :half], xf[..., half:]
    return jnp.concatenate([x1 * cos - x2 * sin, x1 * sin + x2 * cos], axis=-1).astype(x.dtype)


def split_cols(z):
    outs, off = [], 0
    for n in IN_SIZES:
        outs.append(z[..., off:off + n])
        off += n
    return outs


def mla_attend(q_nope, q_rope, q_pos, ckv_all, krope_all, k_pos, w_ukv, kn_nope):
    B, Lq = q_nope.shape[:2]
    Lk = ckv_all.shape[1]
    kv = (ckv_all @ w_ukv).reshape(B, Lk, MLA_HEADS, MLA_NOPE + MLA_V)
    k_nope = rmsnorm(kv[..., :MLA_NOPE], kn_nope)
    v = kv[..., MLA_NOPE:]
    k_chunk = k_pos // CHUNK
    scale = (MLA_NOPE + MLA_ROPE) ** -0.5

    def block(args):
        qn, qr, qp = args
        s = jnp.einsum("bqhd,bkhd->bhqk", qn, k_nope) + jnp.einsum("bqhr,bkr->bhqk", qr, krope_all)
        s = s.astype(jnp.float32) * scale
        mask = k_chunk[None, :] <= (qp // CHUNK)[:, None]
        s = jnp.where(mask[None, None], s, NEG_INF)
        pr = jax.nn.softmax(s, axis=-1).astype(v.dtype)
        return jnp.einsum("bhqk,bkhd->bqhd", pr, v)

    if Lq > Q_BLOCK and Lq % Q_BLOCK == 0:
        nb = Lq // Q_BLOCK
        qn_b = q_nope.reshape(B, nb, Q_BLOCK, MLA_HEADS, MLA_NOPE).swapaxes(0, 1)
        qr_b = q_rope.reshape(B, nb, Q_BLOCK, MLA_HEADS, MLA_ROPE).swapaxes(0, 1)
        qp_b = q_pos.reshape(nb, Q_BLOCK)
        out = lax.map(block, (qn_b, qr_b, qp_b)).swapaxes(0, 1)
    else:
        out = block((q_nope, q_rope, q_pos))
    return out.reshape(B, Lq, MLA_WIDTH)


def pool_mix(u, hist, pos, pool_w, pool_scale):
    B, L, _ = u.shape
    P = POOL_HIST
    up = jnp.concatenate([hist.astype(u.dtype), u], axis=1)
    upf = up.astype(jnp.float32)
    cs = jnp.concatenate([jnp.zeros((B, 1, POOL_WIDTH), jnp.float32), jnp.cumsum(upf, axis=1)], axis=1)
    means = []
    for g, w in enumerate(POOL_WINDOWS):
        c0, c1 = g * POOL_GROUP_DIM, (g + 1) * POOL_GROUP_DIM
        tot = cs[:, P + 1:P + 1 + L, c0:c1] - cs[:, P + 1 - w:P + 1 - w + L, c0:c1]
        cnt = jnp.minimum(pos + 1, w).astype(jnp.float32)[None, :, None]
        means.append(tot / cnt)
    pooled = (jnp.concatenate(means, axis=-1) - upf[:, P:]).astype(u.dtype)
    y = jnp.einsum("blgc,gcd->blgd", pooled.reshape(B, L, len(POOL_WINDOWS), POOL_GROUP_DIM), pool_w)
    y = y.reshape(B, L, POOL_WIDTH) * pool_scale
    return y, up[:, -P:]


def band_chunks(t, n_chunks):
    pad = [(0, 0), (CA_WINDOW, 0)] + [(0, 0)] * (t.ndim - 2)
    tc = jnp.pad(t, pad).reshape((t.shape[0], n_chunks + CA_LEFT_CHUNKS, CHUNK) + t.shape[2:])
    return jnp.concatenate([tc[:, i:i + n_chunks] for i in range(CA_LEFT_CHUNKS + 1)], axis=2)


def ca_attend(q, k, v, q_pos, k_pos, rel_table):
    s = jnp.einsum("bnqhd,bnkhd->bnhqk", q, k).astype(jnp.float32) * CA_HEAD_DIM ** -0.5
    rel = jnp.clip(k_pos[:, None, :] - q_pos[:, :, None], -CA_MAX_REL, CA_MAX_REL) + CA_MAX_REL
    bias = jnp.transpose(rel_table.astype(jnp.float32)[:, rel], (1, 0, 2, 3))
    qc = (q_pos // CHUNK)[:, :, None]
    kc = (k_pos // CHUNK)[:, None, :]
    mask = (k_pos[:, None, :] >= 0) & (kc <= qc) & (kc >= qc - CA_LEFT_CHUNKS)
    s = jnp.where(mask[None, :, None], s + bias[None], NEG_INF)
    pr = jax.nn.softmax(s, axis=-1).astype(v.dtype)
    return jnp.einsum("bnhqk,bnkhd->bnqhd", pr, v)


def trunk_layer(x, ple, hist_ckv, hist_krope, hist_ck, hist_cv, hist_pool,
                norm_in, w_in, mla_q_norm, mla_w_uq, mla_kv_norm, mla_w_ukv, mla_qn_nope, mla_qn_rope,
                mla_kn_nope, mla_kn_rope, w_o_mla, pool_w, pool_scale, w_o_pool, ca_qn, ca_kn, ca_rel_bias,
                w_o_ca, w_out, ple_norm, w_ple_gate, w_ple_proj):
    B, L, _ = x.shape
    past = hist_ckv.shape[1]
    pos = past + jnp.arange(L)
    h = rmsnorm(x, norm_in)
    hq, hkv, g_a, u_b, g_b, qkv_c, g_c, g_mix = split_cols(h @ w_in)

    cq = rmsnorm(hq, mla_q_norm)
    q = (cq @ mla_w_uq).reshape(B, L, MLA_HEADS, MLA_NOPE + MLA_ROPE)
    q_nope = rmsnorm(q[..., :MLA_NOPE], mla_qn_nope)
    q_rope = rope(rmsnorm(q[..., MLA_NOPE:], mla_qn_rope), pos)
    ckv = rmsnorm(hkv[..., :MLA_KV_RANK], mla_kv_norm)
    krope = rope(rmsnorm(hkv[..., MLA_KV_RANK:], mla_kn_rope), pos)
    ckv_all = jnp.concatenate([hist_ckv.astype(ckv.dtype), ckv], axis=1)
    krope_all = jnp.concatenate([hist_krope.astype(krope.dtype), krope], axis=1)
    o_a = mla_attend(q_nope, q_rope, pos, ckv_all, krope_all, jnp.arange(past + L), mla_w_ukv, mla_kn_nope)
    o_a = o_a * jax.nn.silu(g_a)

    o_b, new_pool = pool_mix(u_b, hist_pool, pos, pool_w, pool_scale)
    o_b = o_b * jax.nn.silu(g_b)

    qkv = qkv_c.reshape(B, L, 3, CA_HEADS, CA_HEAD_DIM)
    qc = rmsnorm(qkv[:, :, 0], ca_qn)
    kc = rmsnorm(qkv[:, :, 1], ca_kn)
    vc = qkv[:, :, 2]
    if hist_ck is None:
        nc = L // CHUNK
        q_b = qc.reshape(B, nc, CHUNK, CA_HEADS, CA_HEAD_DIM)
        k_b = band_chunks(kc, nc)
        v_b = band_chunks(vc, nc)
        q_pos = pos.reshape(nc, CHUNK)
        kp = jnp.arange(-CA_WINDOW, L).reshape(nc + CA_LEFT_CHUNKS, CHUNK)
        k_pos = jnp.concatenate([kp[i:i + nc] for i in range(CA_LEFT_CHUNKS + 1)], axis=1)
        keep = min(CA_WINDOW, L)
        new_ck, new_cv = kc[:, L - keep:], vc[:, L - keep:]
    else:
        lc = hist_ck.shape[1]
        q_b = qc[:, None]
        k_b = jnp.concatenate([hist_ck.astype(kc.dtype), kc], axis=1)[:, None]
        v_b = jnp.concatenate([hist_cv.astype(vc.dtype), vc], axis=1)[:, None]
        q_pos = pos[None]
        k_pos = jnp.arange(past - lc, past + L)[None]
        new_ck, new_cv = kc, vc
    o_c = ca_attend(q_b, k_b, v_b, q_pos, k_pos, ca_rel_bias).reshape(B, L, CA_WIDTH)
    o_c = o_c * jax.nn.silu(g_c)

    gates = jax.nn.sigmoid(g_mix)
    m = (gates[..., :D_MODEL] * (o_a @ w_o_mla)
         + gates[..., D_MODEL:2 * D_MODEL] * (o_b @ w_o_pool)
         + gates[..., 2 * D_MODEL:] * (o_c @ w_o_ca))
    x = x + m @ w_out

    x = x + jax.nn.sigmoid(rmsnorm(x, ple_norm) @ w_ple_gate) * (ple @ w_ple_proj)
    return x, ckv, krope, new_ck, new_cv, new_pool


def setup_inputs(seed: int = 0) -> dict:
    key = jax.random.key(seed)
    ks = list(jax.random.split(key, 32))

    def nrm(shape, scale=1.0):
        return jax.random.normal(ks.pop(), shape, jnp.float32) * scale

    def gain(n):
        return 1.0 + nrm((DEPTH, n), 0.05)

    ca_keep = min(CA_WINDOW, PAST_LEN)
    return {
        "x_prompt": nrm((BATCH, SEQ, D_MODEL)),
        "x_sample": nrm((DEC_BATCH, DEC_SEQ, D_MODEL)),
        "cache_mla_ckv": nrm((DEPTH, DEC_BATCH, PAST_LEN, MLA_KV_RANK)),
        "cache_mla_krope": nrm((DEPTH, DEC_BATCH, PAST_LEN, MLA_ROPE)),
        "cache_ca_k": nrm((DEPTH, DEC_BATCH, ca_keep, CA_HEADS, CA_HEAD_DIM)),
        "cache_ca_v": nrm((DEPTH, DEC_BATCH, ca_keep, CA_HEADS, CA_HEAD_DIM)),
        "state_pool": nrm((DEPTH, DEC_BATCH, POOL_HIST, POOL_WIDTH)),
        "p_prompt": nrm((DEPTH, BATCH, SEQ, PLE_DIM)),
        "p_sample": nrm((DEPTH, DEC_BATCH, DEC_SEQ, PLE_DIM)),
        "norm_in": gain(D_MODEL),
        "w_in": nrm((DEPTH, D_MODEL, IN_TOTAL), D_MODEL ** -0.5),
        "mla_q_norm": gain(MLA_Q_RANK),
        "mla_w_uq": nrm((DEPTH, MLA_Q_RANK, MLA_HEADS * (MLA_NOPE + MLA_ROPE)), MLA_Q_RANK ** -0.5),
        "mla_kv_norm": gain(MLA_KV_RANK),
        "mla_w_ukv": nrm((DEPTH, MLA_KV_RANK, MLA_HEADS * (MLA_NOPE + MLA_V)), MLA_KV_RANK ** -0.5),
        "mla_qn_nope": gain(MLA_NOPE),
        "mla_qn_rope": gain(MLA_ROPE),
        "mla_kn_nope": gain(MLA_NOPE),
        "mla_kn_rope": gain(MLA_ROPE),
        "w_o_mla": nrm((DEPTH, MLA_WIDTH, D_MODEL), MLA_WIDTH ** -0.5),
        "pool_w": nrm((DEPTH, len(POOL_WINDOWS), POOL_GROUP_DIM, POOL_GROUP_DIM), POOL_GROUP_DIM ** -0.5),
        "pool_scale": 1.0 + nrm((DEPTH, POOL_WIDTH), 0.1),
        "w_o_pool": nrm((DEPTH, POOL_WIDTH, D_MODEL), POOL_WIDTH ** -0.5),
        "ca_qn": gain(CA_HEAD_DIM),
        "ca_kn": gain(CA_HEAD_DIM),
        "ca_rel_bias": nrm((DEPTH, CA_HEADS, 2 * CA_MAX_REL + 1), 0.1),
        "w_o_ca": nrm((DEPTH, CA_WIDTH, D_MODEL), CA_WIDTH ** -0.5),
        "w_out": nrm((DEPTH, D_MODEL, D_MODEL), D_MODEL ** -0.5),
        "ple_norm": gain(D_MODEL),
        "w_ple_gate": nrm((DEPTH, D_MODEL, D_MODEL), D_MODEL ** -0.5),
        "w_ple_proj": nrm((DEPTH, PLE_DIM, D_MODEL), PLE_DIM ** -0.5),
    }


def reference(x_prompt, x_sample, cache_mla_ckv, cache_mla_krope, cache_ca_k, cache_ca_v, state_pool,
              p_prompt, p_sample, norm_in, w_in, mla_q_norm, mla_w_uq, mla_kv_norm, mla_w_ukv,
              mla_qn_nope, mla_qn_rope, mla_kn_nope, mla_kn_rope, w_o_mla, pool_w, pool_scale, w_o_pool,
              ca_qn, ca_kn, ca_rel_bias, w_o_ca, w_out, ple_norm, w_ple_gate, w_ple_proj):
    xp, xs = x_prompt, x_sample
    B = xp.shape[0]
    st_p = [[], [], [], [], []]
    st_s = [[], [], [], [], []]
    for i in range(DEPTH):
        lw = (norm_in[i], w_in[i], mla_q_norm[i], mla_w_uq[i], mla_kv_norm[i], mla_w_ukv[i],
              mla_qn_nope[i], mla_qn_rope[i], mla_kn_nope[i], mla_kn_rope[i], w_o_mla[i],
              pool_w[i], pool_scale[i], w_o_pool[i], ca_qn[i], ca_kn[i], ca_rel_bias[i], w_o_ca[i],
              w_out[i], ple_norm[i], w_ple_gate[i], w_ple_proj[i])
        xp, ckv_p, kr_p, ck_p, cv_p, pl_p = trunk_layer(
            xp, p_prompt[i], jnp.zeros((B, 0, MLA_KV_RANK), xp.dtype), jnp.zeros((B, 0, MLA_ROPE), xp.dtype),
            None, None, jnp.zeros((B, POOL_HIST, POOL_WIDTH), xp.dtype), *lw)
        xs, ckv_s, kr_s, ck_s, cv_s, pl_s = trunk_layer(
            xs, p_sample[i], cache_mla_ckv[i], cache_mla_krope[i], cache_ca_k[i], cache_ca_v[i],
            state_pool[i], *lw)
        for lst, t in zip(st_p, (ckv_p, kr_p, ck_p, cv_p, pl_p)):
            lst.append(t)
        for lst, t in zip(st_s, (ckv_s, kr_s, ck_s, cv_s, pl_s)):
            lst.append(t)
    return (xp, xs,
            jnp.stack(st_p[0]), jnp.stack(st_p[1]), jnp.stack(st_p[2]), jnp.stack(st_p[3]), jnp.stack(st_p[4]),
            jnp.stack(st_s[0]), jnp.stack(st_s[1]), jnp.stack(st_s[2]), jnp.stack(st_s[3]), jnp.stack(st_s[4]))
```

```python
import numpy as np
import concourse.bass as bass
import concourse.mybir as mybir
from concourse.bass_utils import run_bass_kernel_spmd

F32 = mybir.dt.float32
BF16 = mybir.dt.bfloat16
AF = mybir.ActivationFunctionType
ALU = mybir.AluOpType
AX = mybir.AxisListType

D = 1024
EPS = 1e-6
NL = 2
NH = 8
CA_H = 4
SLAB = 8320
NSLAB = 14
NSLABB = 56


class Buf:
    def __init__(self, name):
        self.name = name
        self.w = None
        self.r = {}
        self.dsem = None
        self.dcnt = 0


class T:
    def __init__(self, ap, bufs):
        self.ap = ap
        self.bufs = list(bufs)

    def __getitem__(self, k):
        return T(self.ap[k], self.bufs)

    def re(self, s, **kw):
        return T(self.ap.rearrange(s, **kw), self.bufs)

    def bc(self, axis, shape):
        return T(self.ap.unsqueeze(axis).to_broadcast(shape), self.bufs)

    def bit(self, dt):
        return T(self.ap.bitcast(dt), self.bufs)


def _ap(x):
    return x.ap if isinstance(x, T) else x


class Sched:
    ENG = ["pe", "act", "dve", "pool", "sp"]

    def __init__(self, nc):
        self.nc = nc
        self.ops = {e: [] for e in self.ENG}
        self.nm = {e: 0 for e in self.ENG}
        self.sem = {e: nc.alloc_semaphore("s_" + e) for e in self.ENG}
        self.waited = {e: {} for e in self.ENG}
        self.pend = {e: ([], []) for e in self.ENG}
        self.dmarecs = {}
        self.nsem = 5

    def _need(self, e, rec):
        if rec is None:
            return
        key, sem, val = rec
        if key == "pe" and e == "pe":
            return
        if self.waited[e].get(key, 0) >= val:
            return
        self.waited[e][key] = val
        self.ops[e].append(("wait", sem, val))

    def op(self, e, fn, r=(), w=(), mark=True):
        rb = [b for t in r if isinstance(t, T) for b in t.bufs]
        wb = [b for t in w if isinstance(t, T) for b in t.bufs]
        for b in rb:
            self._need(e, b.w)
        for b in wb:
            self._need(e, b.w)
            for rec in list(b.r.values()):
                self._need(e, rec)
        pr, pw = self.pend[e]
        if mark:
            self.nm[e] += 1
            rec = (e, self.sem[e], self.nm[e])
            self.ops[e].append(("op", fn, self.sem[e]))
            for b in pw + wb:
                b.w = rec
                b.r = {}
            for b in pr + rb:
                if b.w is not rec:
                    b.r[e] = rec
            self.pend[e] = ([], [])
        else:
            self.ops[e].append(("op", fn, None))
            pr.extend(rb)
            pw.extend(wb)

    def dma(self, out, in_, q=None):
        e = q if q is not None else ("sp" if isinstance(out, T) else "pool")
        sb = out if isinstance(out, T) else in_
        rb = in_.bufs if isinstance(in_, T) else []
        wb = out.bufs if isinstance(out, T) else []
        for b in rb:
            self._need(e, b.w)
        for b in wb:
            self._need(e, b.w)
            for rec in list(b.r.values()):
                self._need(e, rec)
        b0 = sb.bufs[0]
        if b0.dsem is None:
            b0.dsem = {}
            b0.dcnt = {}
        if e not in b0.dsem:
            b0.dsem[e] = self.nc.alloc_semaphore("d%s_%s" % (e, b0.name))
            b0.dcnt[e] = 0
            self.nsem += 1
        b0.dcnt[e] += 16
        key = "d%s:%s" % (e, b0.name)
        rec = (key, b0.dsem[e], b0.dcnt[e])
        self.ops[e].append(("dma", _ap(out), _ap(in_), b0.dsem[e]))
        for b in wb:
            b.w = rec
            b.r = {}
        for b in rb:
            b.r[key] = rec
        self.dmarecs[key] = rec

    def dma_fence(self):
        for e in ("sp", "pool"):
            for rec in list(self.dmarecs.values()):
                self._need(e, rec)

    def replay(self, e, eng):
        for it in self.ops[e]:
            if it[0] == "wait":
                eng.wait_ge(it[1], it[2])
            elif it[0] == "op":
                ins = it[1](eng)
                if it[2] is not None:
                    ins.then_inc(it[2], 1)
            else:
                eng.dma_start(out=it[1], in_=it[2]).then_inc(it[3], 16)


class K:
    def __init__(self, LP, PAST, LS=64):
        self.LP, self.PAST, self.LS = LP, PAST, LS
        nc = self.nc = bass.Bass("TRN2", target_bir_lowering=False)
        self.S = Sched(nc)
        import os
        self.pstop = float(os.environ.get("PSTOP", "99"))
        self.nbuf = 0
        self.din = {}
        self.dout = {}
        self._decl_io()
        self._alloc()

    def _in(self, name, shape):
        self.din[name] = self.nc.dram_tensor(name, list(shape), F32, kind="ExternalInput").ap()
        return self.din[name]

    def _out(self, name, shape):
        self.dout[name] = self.nc.dram_tensor(name, list(shape), F32, kind="ExternalOutput").ap()
        return self.dout[name]

    def _scr(self, name, shape, dt=BF16):
        return self.nc.dram_tensor(name, list(shape), dt, kind="Internal").ap()

    def _decl_io(self):
        LP, PAST, LS = self.LP, self.PAST, self.LS
        i = self._in
        i("xp", [LP, D]); i("xs", [LS, D])
        i("c_ckv", [NL, PAST, 128]); i("c_kr", [NL, PAST, 32])
        i("c_cak", [NL, 512, 256]); i("c_cav", [NL, 512, 256])
        i("st_pool", [NL, 15, 256])
        i("pp", [NL, LP, 256]); i("ps", [NL, LS, 256])
        i("w_in", [NL, D, 5536]); i("w_uq", [NL, 256, 768]); i("w_ukv", [NL, 128, 1024])
        i("w_o_mla", [NL, 512, D]); i("pool_w", [NL, 4, 64, 64]); i("w_o_pool", [NL, 256, D])
        i("w_o_ca", [NL, 256, D]); i("w_out", [NL, D, D]); i("w_pg", [NL, D, D]); i("w_pp", [NL, 256, D])
        i("gcol", [NL, 128, 18])
        i("grow", [NL, 1, 128 + 32 + 96 + 64 + 512])
        i("plsc", [NL, 64, 4])
        i("biasT", [NL, 128, 20, 128])
        i("ident", [128, 128])
        i("bands", [128, 12, 128])
        i("ropeP", [LP, 64]); i("ropeS", [LS, 64])
        o = self._out
        o("y_p", [LP, D]); o("y_s", [LS, D])
        o("ckv_p", [NL, LP, 128]); o("kr_p", [NL, LP, 32])
        o("cak_p", [NL, 512, 256]); o("cav_p", [NL, 512, 256]); o("pool_p", [NL, 15, 256])
        o("ckv_s", [NL, LS, 128]); o("kr_s", [NL, LS, 32])
        o("cak_s", [NL, LS, 256]); o("cav_s", [NL, LS, 256]); o("pool_s", [NL, 15, 256])
        s = self._scr
        self.nktP = LP // 128
        self.nktS = PAST // 128 + 1
        self.x1p = s("x1p", [LP, D], F32); self.x1s = s("x1s", [LS, D], F32)
        self.QTp = s("QTp", [NH, 96, LP]); self.KTp = s("KTp", [NH, 96, LP])
        self.VAp = s("VAp", [NH, 128, self.nktP, 65])
        self.OTp = s("OTp", [D, LP])
        LK = self.nktS * 128
        self.QTs = s("QTs", [NH, 96, LS]); self.KTs = s("KTs", [NH, 96, LK])
        self.VAs = s("VAs", [NH, 128, self.nktS, 65])
        self.OTs = s("OTs", [D, LS])

    def sb(self, name, shape, dt=F32, bufs=None):
        self.nbuf += 1
        ap = self.nc.alloc_sbuf_tensor("sb_" + name, list(shape), dt).ap()
        return T(ap, bufs if bufs is not None else [Buf(name)])

    def _alloc(self):
        nc = self.nc
        self.pp_t = [nc.alloc_psum_tensor("pp%d" % i, [128, 1024], F32).ap() for i in range(4)]
        self.pbuf = [Buf("pb%d" % i) for i in range(8)]
        self.bk_i = 0
        self.bk2_i = 0
        self.bk_lim = 8
        self.arena = nc.alloc_sbuf_tensor("arena", [128, NSLAB * SLAB // 2], BF16).ap()
        self.slab = [Buf("slab%d" % i) for i in range(NSLAB)]
        self.arenaB = nc.alloc_sbuf_tensor("arenaB", [128, NSLABB * 512], BF16).ap()
        self.slabB = [Buf("slabB%d" % i) for i in range(NSLABB)]
        self.ab_cur = 0
        sb = self.sb
        self.identf = sb("identf", [128, 128]); self.identb = sb("identb", [128, 128], BF16)
        self.ones_f = sb("ones_f", [128, 64])
        self.bandf = self.av(3, 1, [128, 12 * 128], F32)
        self.bands = sb("bands", [128, 12, 128], BF16)
        self.gcol = sb("gcol", [128, 18])
        self.grow = sb("grow", [128, 832])
        self.plsc = sb("plsc", [64, 4])
        self.biasT = sb("biasTb", [128, 20, 128], BF16)
        self.invn1 = sb("invn1", [128, 11]); self.invn2 = sb("invn2", [128, 24])
        self.w_uq = sb("w_uq", [128, 2, 768], BF16); self.w_ukv = sb("w_ukv", [128, 1024], BF16)
        self.pool_w = sb("pool_w", [64, 4, 64], BF16); self.w_pp = sb("w_ppb", [128, 2, 1024], BF16)
        self.stg = [sb("stg%d" % i, [128, 1024]) for i in range(2)]
        self.stg_i = 0

    def ab(self, name, shape, dt=F32):
        esz = 4 if dt == F32 else 2
        nb = int(np.prod(shape[1:])) * esz
        ns = (nb + 1023) // 1024
        t = self.av(self.ab_cur, ns, shape, dt, arena="B")
        self.ab_cur += ns
        assert self.ab_cur <= NSLABB, (name, self.ab_cur)
        return t

    def av(self, s0, n, shape, dt=BF16, off=0, arena="A"):
        SLAB_ = SLAB if arena == "A" else 1024
        ar = self.arena if arena == "A" else self.arenaB
        slabs = self.slab if arena == "A" else self.slabB
        base = ar[:, s0 * SLAB_ // 2 + off // 2: (s0 + n) * SLAB_ // 2]
        esz = 4 if dt == F32 else 2
        nel = int(np.prod(shape[1:]))
        assert off + nel * esz <= n * SLAB_, (shape, n)
        ap = base[:, 0:nel * esz // 2]
        if dt == F32:
            ap = ap.bitcast(F32)
        ap = ap[0:shape[0], :]
        if len(shape) == 3:
            ap = ap.rearrange("p (a b) -> p a b", b=shape[2])
        elif len(shape) == 4:
            ap = ap.rearrange("p (a b c) -> p a b c", b=shape[2], c=shape[3])
        return T(ap, slabs[s0:s0 + n])

    def bk(self):
        i = self.bk_i % self.bk_lim
        self.bk_i = (i + 1) % self.bk_lim
        return T(self.pp_t[i // 2][:, (i % 2) * 512:(i % 2 + 1) * 512], [self.pbuf[i]])

    def bank(self, i):
        return T(self.pp_t[i // 2][:, (i % 2) * 512:(i % 2 + 1) * 512], [self.pbuf[i]])

    def bk2(self):
        i = self.bk2_i % (self.bk_lim // 2)
        self.bk2_i = (i + 1) % (self.bk_lim // 2)
        return T(self.pp_t[i], [self.pbuf[2 * i], self.pbuf[2 * i + 1]])

    def act(self, out, in_, func, scale=1.0, bias=0.0, accum=None):
        r = [in_] + [x for x in (scale, bias) if isinstance(x, T)]
        w = [out] + ([accum] if accum is not None else [])
        kw = dict(out=out.ap, in_=in_.ap, func=func, bias=_ap(bias), scale=_ap(scale))
        if accum is not None:
            kw["accum_out"] = accum.ap
        self.S.op("act", lambda e: e.activation(**kw), r, w)

    def tt(self, eng, out, a, b, op):
        self.S.op(eng, lambda e: e.tensor_tensor(out=out.ap, in0=a.ap, in1=b.ap, op=op), [a, b], [out])

    def ts(self, eng, out, a, s1, op0, s2=None, op1=None):
        r = [a] + [x for x in (s1, s2) if isinstance(x, T)]
        if op1 is None:
            fn = lambda e: e.tensor_scalar(out=out.ap, in0=a.ap, scalar1=_ap(s1), scalar2=None, op0=op0)
        else:
            fn = lambda e: e.tensor_scalar(out=out.ap, in0=a.ap, scalar1=_ap(s1), scalar2=_ap(s2), op0=op0, op1=op1)
        self.S.op(eng, fn, r, [out])

    def stt(self, eng, out, a, s, b, op0, op1):
        r = [a, b] + ([s] if isinstance(s, T) else [])
        self.S.op(eng, lambda e: e.scalar_tensor_tensor(out=out.ap, in0=a.ap, scalar=_ap(s), in1=b.ap, op0=op0, op1=op1), r, [out])

    def cp(self, eng, out, in_):
        if eng == "act":
            self.act(out, in_, AF.Copy)
        else:
            self.S.op(eng, lambda e: e.tensor_copy(out=out.ap, in_=in_.ap), [in_], [out])

    def red(self, eng, out, in_):
        self.S.op(eng, lambda e: e.tensor_reduce(out=out.ap, in_=in_.ap, axis=AX.X, op=ALU.add), [in_], [out])

    def memset(self, eng, out, val):
        self.S.op(eng, lambda e: e.memset(out.ap, val), [], [out])

    def recip(self, out, in_):
        self.S.op("dve", lambda e: e.reciprocal(out=out.ap, in_=in_.ap), [in_], [out])

    def mm(self, out, lhsT, rhs, start=True, stop=True, mark=None):
        if mark is None:
            mark = stop
        self.S.op("pe", lambda e: e.matmul(out.ap, lhsT=lhsT.ap, rhs=rhs.ap, start=start, stop=stop), [lhsT, rhs], [out], mark=mark)

    def tr(self, out, in_, P, mark=True):
        idn = self.identb[0:P, 0:P]
        self.S.op("pe", lambda e: e.transpose(out=out.ap, in_=in_.ap, identity=idn.ap), [in_, idn], [out], mark=mark)

    def rstd(self, st_in, invn, tmp, out):
        self.tt("dve", tmp, st_in, invn, ALU.mult)
        self.act(tmp, tmp, AF.Ln, bias=self.eps_t[0:tmp.ap.shape[0], :])
        self.act(out, tmp, AF.Exp, scale=-0.5)

    def load_w(self, dst, src, scale=None, eng="pool"):
        n = dst.ap.shape[-1]
        rows = dst.ap.shape[0]
        for c0 in range(0, n, 1024):
            c1 = min(n, c0 + 1024)
            st = self.stg[self.stg_i]
            self.stg_i ^= 1
            self.S.dma(st[0:rows, 0:c1 - c0], src[:, c0:c1])
            if scale is not None:
                self.act(dst[:, c0:c1], st[0:rows, 0:c1 - c0], AF.Copy, scale=scale)
            else:
                self.cp(eng, dst[:, c0:c1], st[0:rows, 0:c1 - c0])

    def setup_consts(self):
        S = self.S
        d = self.din
        S.dma(self.identf, d["ident"])
        self.cp("dve", self.identb, self.identf)
        self.memset("dve", self.ones_f, 1.0)
        self.eps_t = self.sb("eps_t", [128, 1])
        self.memset("dve", self.eps_t, EPS)
        self.one_t = self.sb("one_t", [128, 1])
        self.memset("dve", self.one_t, 1.0)
        S.dma(self.bandf, d["bands"].rearrange("p a b -> p (a b)"))
        self.cp("dve", self.bands.re("p a b -> p (a b)"), self.bandf)
        for c0, c1, v in ((0, 1, 1 / 256), (1, 2, 1 / 128), (2, 3, 1 / 32), (3, 11, 1 / 64)):
            self.memset("dve", self.invn1[:, c0:c1], v)
        for c0, c1, v in ((0, 8, 1 / 64), (8, 16, 1 / 32), (16, 24, 1 / 64)):
            self.memset("dve", self.invn2[:, c0:c1], v)
        self.invD = self.sb("invD", [128, 1])
        self.memset("dve", self.invD, 1.0 / D)

    def load_layer_small(self, l):
        S = self.S
        d = self.din
        S.dma(self.gcol, d["gcol"][l])
        S.dma(self.grow, d["grow"][l].partition_broadcast(128))
        S.dma(self.plsc, d["plsc"][l])
        self.ts("dve", self.grow[:, 320:576], self.grow[:, 320:576], 0.125, ALU.mult)
        bst = self.av(3, 2, [128, 20 * 128], F32)
        S.dma(bst, d["biasT"][l].rearrange("p a b -> p (a b)"))
        self.cp("pool", self.biasT.re("p a b -> p (a b)"), bst)
        for k in range(2):
            self.load_w(self.w_uq[:, k, :], d["w_uq"][l][k * 128:(k + 1) * 128, :], scale=self.gcol[:, 16 + k:17 + k])
            self.load_w(self.w_pp[:, k, :], d["w_pp"][l][k * 128:(k + 1) * 128, :])
        self.load_w(self.w_ukv, d["w_ukv"][l])
        for g in range(4):
            self.load_w(self.pool_w[:, g, :], d["pool_w"][l][g])

    def load_p1_weights(self, l):
        w_in = self.din["w_in"][l]
        self.w_tm = self.av(0, 3, [128, 8, 1440])
        pieces = [(0, 256, 0), (928, 1184, 256), (1440, 1952, 512), (1952, 2208, 1024), (256, 416, 1280)]
        import os
        KK = int(os.environ.get("KK", "8")); NP_ = int(os.environ.get("KNP", "5"))
        for k in range(KK):
            for (a, b, o) in pieces[:NP_]:
                self.load_w(self.w_tm[:, k, o:o + b - a], w_in[k * 128:(k + 1) * 128, a:b], scale=self.gcol[:, k:k + 1])

    def load_p3_weights(self, l):
        d = self.din
        w_in = d["w_in"][l]
        self.w_fm = self.av(0, 8, [128, 8, 4096])
        self.w_o = self.av(8, 2, [128, 8, 1024])
        self.w_out = self.av(10, 2, [128, 8, 1024])
        self.w_pg = self.av(12, 2, [128, 8, 1024])
        pieces = [(416, 928, 0), (1184, 1440, 512), (2208, 2464, 768), (2464, 3488, 1024), (3488, 4512, 2048), (4512, 5536, 3072)]
        for k in range(8):
            for (a, b, o) in pieces:
                self.load_w(self.w_fm[:, k, o:o + b - a], w_in[k * 128:(k + 1) * 128, a:b], scale=self.gcol[:, k:k + 1])
            self.load_w(self.w_out[:, k, :], d["w_out"][l][k * 128:(k + 1) * 128, :], eng="dve")
            self.load_w(self.w_pg[:, k, :], d["w_pg"][l][k * 128:(k + 1) * 128, :], scale=self.gcol[:, 8 + k:9 + k])
        for k in range(4):
            self.load_w(self.w_o[:, k, :], d["w_o_mla"][l][k * 128:(k + 1) * 128, :], eng="dve")
        for k in range(2):
            self.load_w(self.w_o[:, 4 + k, :], d["w_o_pool"][l][k * 128:(k + 1) * 128, :])
            self.load_w(self.w_o[:, 6 + k, :], d["w_o_ca"][l][k * 128:(k + 1) * 128, :])

    def alloc_p1(self):
        av = self.av
        o = [0]

        def A(shape, dt=BF16, slab=None):
            esz = 4 if dt == F32 else 2
            nb = int(np.prod(shape[1:])) * esz
            ns = (nb + SLAB - 1) // SLAB
            t = av(o[0], ns, shape, dt)
            o[0] += ns
            return t

        o[0] = 3
        self.p1 = p = {}
        p["xt"] = [A([128, 1024], F32), A([128, 1024], F32)]
        p["kv_s"] = A([128, 1024], F32)
        p["zA"] = A([128, 512], F32)
        p["zB"] = A([128, 512], F32)
        p["zC"] = A([128, 416], F32)
        p["q_s"] = A([128, 768], F32)
        p["sq"] = A([128, 768], F32)
        p["hT"] = A([128, 8, 128])
        assert o[0] <= 12
        def sb(name, shape, dt=F32):
            nb = int(np.prod(shape[1:])) * (4 if dt == F32 else 2)
            return self.ab(name, shape, dt) if nb >= 512 else self.sb(name, shape, dt)
        self.ab_cur = 0
        p["hb"] = sb("p1_hb", [128, 1024], BF16)
        p["junk"] = sb("p1_junk", [128, 1024], BF16)
        p["st0"] = sb("p1_st0", [128, 4])
        p["st1"] = sb("p1_st1", [128, 11]); p["tm1"] = sb("p1_tm1", [128, 11]); p["rs1"] = sb("p1_rs1", [128, 11])
        p["st2"] = sb("p1_st2", [128, 24]); p["tm2"] = sb("p1_tm2", [128, 24]); p["rs2"] = sb("p1_rs2", [128, 24])
        p["cq"] = sb("p1_cq", [128, 256], BF16); p["cqT"] = sb("p1_cqT", [128, 2, 128], BF16)
        p["ckv_s"] = sb("p1_ckv", [128, 128]); p["ckvb"] = sb("p1_ckvb", [128, 128], BF16); p["ckvT"] = sb("p1_ckvT", [128, 128], BF16)
        p["kr"] = sb("p1_kr", [128, 32]); p["kt1"] = sb("p1_kt1", [128, 32]); p["kt2"] = sb("p1_kt2", [128, 32]); p["kro"] = sb("p1_kro", [128, 32])
        p["rp"] = [sb("p1_rp%d" % i, [128, 64]) for i in range(3)]
        p["qkn"] = sb("p1_qkn", [128, 512]); p["qkb"] = sb("p1_qkb", [128, 512], BF16)
        p["qa"] = sb("p1_qa", [128, 8, 96], BF16); p["ka"] = sb("p1_ka", [128, 8, 96], BF16)
        p["va"] = self.sb("p1_va", [128, 8, 65], BF16)
        p["trp"] = sb("p1_trp", [128, 8, 32]); p["t1"] = sb("p1_t1", [128, 8, 32]); p["t2"] = sb("p1_t2", [128, 8, 32])
        p["qT_s"] = sb("p1_qTs", [96, 8, 128], BF16); p["kT_s"] = sb("p1_kTs", [96, 8, 128], BF16)
        p["ub"] = [sb("p1_ub%d" % i, [128, 256], BF16) for i in range(3)]
        p["ubf"] = sb("p1_ubf", [128, 256])
        p["pldT"] = sb("p1_pldT", [64, 4, 128], BF16); p["yT"] = sb("p1_yT", [64, 4, 128], BF16)
        p["QcT"] = [sb("p1_QcT%d" % i, [64, 4, 128], BF16) for i in range(2)]
        p["KcT"] = [sb("p1_KcT%d" % i, [64, 4, 128], BF16) for i in range(6)]
        p["Vc"] = [self.sb("p1_Vc%d" % i, [128, 4, 65], BF16) for i in range(6)]
        p["PTc"] = sb("p1_PTc", [128, 5, 4, 128], BF16)
        p["rd"] = sb("p1_rd", [128, 512]); p["bc_s"] = sb("p1_bcs", [64, 512]); p["ocT"] = sb("p1_ocT", [64, 4, 128], BF16)
        p["cst"] = sb("p1_cst", [128, 256]); p["cstb"] = sb("p1_cstb", [128, 256], BF16)
        for t in p["Vc"]:
            self.memset("pool", t, 1.0)
        self.memset("pool", p["va"], 1.0)

    def kv_build(self, P, tok0, t, KT, VA, ckvb, kro):
        p = self.p1
        self.tr(self.bkb()[:, 0:P], ckvb[0:P, :], P)
        ps = self.last_bkb
        self.cp("dve", p["ckvT"][:, 0:P], ps[:, 0:P])
        pk = self.bk2()
        for c in range(2):
            self.mm(pk[0:P, c * 512:(c + 1) * 512], p["ckvT"][:, 0:P], self.w_ukv[:, c * 512:(c + 1) * 512])
        kv = p["kv_s"]
        self.cp("act", kv[0:P, 0:512], pk[0:P, 0:512])
        self.cp("dve", kv[0:P, 512:1024], pk[0:P, 512:1024])
        kv3 = kv.re("p (h d) -> p h d", d=128)
        sq3 = p["sq"].re("p (h d) -> p h d", d=96)
        self.act(sq3[0:P, :, 0:64], kv3[0:P, :, 0:64], AF.Square)
        self.red("dve", p["st2"][0:P, 16:24], sq3[0:P, :, 0:64])
        self.rstd(p["st2"][0:P, 16:24], self.invn2[0:P, 16:24], p["tm2"][0:P, 16:24], p["rs2"][0:P, 16:24])
        gkn = self.grow[0:P, 256:320]
        ka = p["ka"]
        self.tt("dve", sq3[0:P, :, 0:64], kv3[0:P, :, 0:64], gkn.bc(1, [P, 8, 64]), ALU.mult)
        self.tt("dve", ka[0:P, :, 0:64], sq3[0:P, :, 0:64], p["rs2"][0:P, 16:24].bc(2, [P, 8, 64]), ALU.mult)
        self.cp("dve", ka[0:P, :, 64:96], kro[0:P, :].bc(1, [P, 8, 32]))
        self.cp("act", p["va"][0:P, :, 0:64], kv3[0:P, :, 64:128])
        pt = self.bkb()
        pt3 = pt.re("p (h t) -> p h t", t=128)
        for h in range(8):
            self.tr(pt3[0:96, h, 0:P], ka[0:P, h, :], P, mark=(h == 7))
        self.cp("act", p["kT_s"][:, :, 0:P], pt3[0:96, :, 0:P])
        self.S.dma(KT[:, :, tok0:tok0 + P].rearrange("h d t -> d h t"), p["kT_s"][:, :, 0:P])
        self.S.dma(VA[:, 0:P, t, :].rearrange("h p c -> p h c"), p["va"][0:P, :, :])

    def bkb(self):
        b = self.bk()
        self.last_bkb = b.bit(BF16)
        return self.last_bkb

    def p1_tile(self, l, seq, t):
        p = self.p1
        S = self.S
        P = seq["P"]
        tok0 = t * 128
        kt = seq["kt0"] + t
        xt = p["xt"][t % 2]
        rp = p["rp"][t % 3]
        QcT = p["QcT"][t % 2]
        S.dma(xt[0:P, :], seq["x"][tok0:tok0 + P, :])
        S.dma(rp[0:P, :], seq["rope"][tok0:tok0 + P, :])
        st0 = p["st0"]
        self.act(p["junk"][0:P, :], xt[0:P, :], AF.Square, accum=st0[0:P, 0:1])
        self.rstd(st0[0:P, 0:1], self.invD[0:P, :], st0[0:P, 1:2], st0[0:P, 2:3])
        self.act(p["hb"][0:P, :], xt[0:P, :], AF.Copy, scale=st0[0:P, 2:3])
        pt = self.bkb()
        pt3 = pt.re("p (k t) -> p k t", t=128)
        for k in range(8):
            self.tr(pt3[:, k, 0:P], p["hb"][0:P, k * 128:(k + 1) * 128], P, mark=(k == 7))
        hT = p["hT"]
        self.cp("dve", hT[:, :, 0:P], pt3[:, :, 0:P])
        yield
        zs = [p["zA"], p["zB"], p["zC"]]
        for c, wc in enumerate((512, 512, 416)):
            pz = self.bk()
            for k in range(8):
                self.mm(pz[0:P, 0:wc], hT[:, k, 0:P], self.w_tm[:, k, c * 512:c * 512 + wc], start=(k == 0), stop=(k == 7))
            self.cp("act" if c != 1 else "dve", zs[c][0:P, 0:wc], pz[0:P, 0:wc])
        zA, zB, zC = zs
        st1, rs1 = p["st1"], p["rs1"]
        sq = p["sq"]
        self.act(sq[0:P, 0:256], zA[0:P, 0:256], AF.Square, accum=st1[0:P, 0:1])
        self.act(sq[0:P, 256:384], zC[0:P, 256:384], AF.Square, accum=st1[0:P, 1:2])
        self.act(sq[0:P, 384:416], zC[0:P, 384:416], AF.Square, accum=st1[0:P, 2:3])
        self.act(sq[0:P, 0:512], zB[0:P, :], AF.Square)
        self.red("dve", st1[0:P, 3:11], sq[0:P, 0:512].re("p (h d) -> p h d", d=64))
        self.rstd(st1[0:P, :], self.invn1[0:P, :], p["tm1"][0:P, :], rs1[0:P, :])
        yield
        self.act(p["cq"][0:P, :], zA[0:P, 0:256], AF.Copy, scale=rs1[0:P, 0:1])
        ckv_s = p["ckv_s"]
        self.stt("dve", ckv_s[0:P, :], zC[0:P, 256:384], rs1[0:P, 1:2], self.grow[0:P, 0:128], ALU.mult, ALU.mult)
        S.dma(seq["o_ckv"][l][tok0:tok0 + P, :], ckv_s[0:P, :])
        self.cp("act", p["ckvb"][0:P, :], ckv_s[0:P, :])
        kr, kt1, kt2, kro = p["kr"], p["kt1"], p["kt2"], p["kro"]
        self.stt("dve", kr[0:P, :], zC[0:P, 384:416], rs1[0:P, 2:3], self.grow[0:P, 128:160], ALU.mult, ALU.mult)
        self.tt("dve", kt1[0:P, :], kr[0:P, :], rp[0:P, 0:32], ALU.mult)
        self.tt("dve", kt2[0:P, 0:16], kr[0:P, 16:32], rp[0:P, 32:48], ALU.mult)
        self.tt("dve", kt2[0:P, 16:32], kr[0:P, 0:16], rp[0:P, 48:64], ALU.mult)
        self.tt("dve", kro[0:P, :], kt1[0:P, :], kt2[0:P, :], ALU.add)
        S.dma(seq["o_kr"][l][tok0:tok0 + P, :], kro[0:P, :])
        ub = p["ub"][t % 3]
        ubp = p["ub"][(t - 1) % 3]
        self.cp("act", ub[0:P, :], zA[0:P, 256:512])
        if t == seq["nt"] - 1:
            S.dma(seq["o_pool"][l], zA[P - 15:P, 256:512])
        qkn, qkb = p["qkn"], p["qkb"]
        zB3 = zB.re("p (h d) -> p h d", d=64)
        qk3 = qkn.re("p (h d) -> p h d", d=64)
        self.tt("dve", qk3[0:P], zB3[0:P], self.grow[0:P, 320:832].re("p (h d) -> p h d", d=64), ALU.mult)
        self.tt("dve", qk3[0:P], qk3[0:P], rs1[0:P, 3:11].bc(2, [P, 8, 64]), ALU.mult)
        self.cp("act", qkb[0:P, :], qkn[0:P, :])
        lo = max(0, seq["nt"] * 128 - 512) if not seq["hist"] else 0
        if tok0 >= lo:
            S.dma(seq["o_cak"][l][tok0 - lo:tok0 - lo + P, :], qkn[0:P, 256:512])
            S.dma(seq["o_cav"][l][tok0 - lo:tok0 - lo + P, :], zC[0:P, 0:256])
        u_cur = seq["ca0"] + t
        slot = u_cur % 6
        self.cp("act", p["Vc"][slot][0:P, :, 0:64], zC[0:P, 0:256].re("p (h d) -> p h d", d=64))
        yield
        pt = self.bkb()
        pt3 = pt.re("p (k t) -> p k t", t=128)
        for k in range(2):
            self.tr(pt3[:, k, 0:P], p["cq"][0:P, k * 128:(k + 1) * 128], P, mark=(k == 1))
        self.cp("dve", p["cqT"][:, :, 0:P], pt3[:, 0:2, 0:P])
        pq = self.bk2()
        for c, (c0, c1) in enumerate(((0, 512), (512, 768))):
            for k in range(2):
                self.mm(pq[0:P, c * 512:c * 512 + c1 - c0], p["cqT"][:, k, 0:P], self.w_uq[:, k, c0:c1], start=(k == 0), stop=(k == 1))
        q_s = p["q_s"]
        self.cp("act", q_s[0:P, 0:512], pq[0:P, 0:512])
        self.cp("dve", q_s[0:P, 512:768], pq[0:P, 512:768])
        self.act(sq[0:P, 0:768], q_s[0:P, :], AF.Square)
        sq3 = sq.re("p (h d) -> p h d", d=96)
        st2, rs2 = p["st2"], p["rs2"]
        self.red("dve", st2[0:P, 0:8], sq3[0:P, :, 0:64])
        self.red("dve", st2[0:P, 8:16], sq3[0:P, :, 64:96])
        self.rstd(st2[0:P, 0:16], self.invn2[0:P, 0:16], p["tm2"][0:P, 0:16], rs2[0:P, 0:16])
        q3 = q_s.re("p (h d) -> p h d", d=96)
        gq96 = self.grow[0:P, 160:256]
        self.tt("dve", q3[0:P, :, :], q3[0:P, :, :], gq96.bc(1, [P, 8, 96]), ALU.mult)
        qa = p["qa"]
        self.tt("dve", qa[0:P, :, 0:64], q3[0:P, :, 0:64], rs2[0:P, 0:8].bc(2, [P, 8, 64]), ALU.mult)
        trp, t1, t2 = p["trp"], p["t1"], p["t2"]
        self.tt("dve", trp[0:P], q3[0:P, :, 64:96], rs2[0:P, 8:16].bc(2, [P, 8, 32]), ALU.mult)
        self.tt("dve", t1[0:P], trp[0:P], rp[0:P, 0:32].bc(1, [P, 8, 32]), ALU.mult)
        self.tt("dve", t2[0:P, :, 0:16], trp[0:P, :, 16:32], rp[0:P, 32:48].bc(1, [P, 8, 16]), ALU.mult)
        self.tt("dve", t2[0:P, :, 16:32], trp[0:P, :, 0:16], rp[0:P, 48:64].bc(1, [P, 8, 16]), ALU.mult)
        self.tt("dve", qa[0:P, :, 64:96], t1[0:P], t2[0:P], ALU.add)
        pt = self.bkb()
        pt3 = pt.re("p (h t) -> p h t", t=128)
        for h in range(8):
            self.tr(pt3[0:96, h, 0:P], qa[0:P, h, :], P, mark=(h == 7))
        self.cp("act", p["qT_s"][:, :, 0:P], pt3[0:96, :, 0:P])
        S.dma(seq["QT"][:, :, tok0:tok0 + P].rearrange("h d t -> d h t"), p["qT_s"][:, :, 0:P])
        yield
        self.kv_build(P, kt * 128, kt, seq["KT"], seq["VA"], p["ckvb"], kro)
        yield
        first = (t == 0 and not seq["hist"])
        pp_ = self.bk()
        pp3 = pp_.re("p (g t) -> p g t", t=128)
        for g in range(4):
            bc_ = self.bands[0:P, (8 + g) if first else g, 0:P]
            self.mm(pp3[0:64, g, 0:P], ub[0:P, g * 64:(g + 1) * 64], bc_, start=True, stop=first)
            if not first:
                self.mm(pp3[0:64, g, 0:P], ubp[:, g * 64:(g + 1) * 64], self.bands[:, 4 + g, 0:P], start=False, stop=True)
        self.cp("dve", p["pldT"][:, :, 0:P], pp3[0:64, :, 0:P])
        py = self.bk()
        py3 = py.re("p (g t) -> p g t", t=128)
        for g in range(4):
            self.mm(py3[0:64, g, 0:P], self.pool_w[:, g, :], p["pldT"][:, g, 0:P])
        self.tt("dve", p["yT"][:, :, 0:P], py3[0:64, :, 0:P], self.plsc.bc(2, [64, 4, P]), ALU.mult)
        S.dma(seq["OT"][512:768, tok0:tok0 + P].rearrange("(g d) t -> d g t", d=64), p["yT"][:, :, 0:P])
        yield
        pt = self.bkb()
        pt3 = pt.re("p (h t) -> p h t", t=128)
        for h in range(8):
            self.tr(pt3[0:64, h, 0:P], qkb[0:P, h * 64:(h + 1) * 64], P, mark=(h == 7))
        self.cp("act", QcT[:, :, 0:P], pt3[0:64, 0:4, 0:P])
        self.cp("act", p["KcT"][slot][:, :, 0:P], pt3[0:64, 4:8, 0:P])
        yield
        us = [u for u in range(u_cur - 4, u_cur + 1) if u >= 0]
        PTc = p["PTc"]
        for ui, u in enumerate(us):
            nk = P if u == u_cur else 128
            du = u - u_cur + 4
            pS = self.bk()
            pS3 = pS.re("p (h t) -> p h t", t=128)
            for h in range(4):
                self.mm(pS3[0:nk, h, 0:P], p["KcT"][u % 6][:, h, 0:nk], QcT[:, h, 0:P], start=True, stop=False, mark=False)
                self.mm(pS3[0:nk, h, 0:P], self.identb[0:nk, 0:nk], self.biasT[0:nk, h * 5 + du, 0:P], start=False, stop=True, mark=(h == 3))
            self.act(PTc[0:nk, ui, :, 0:P], pS3[0:nk, :, 0:P], AF.Exp)
        po = self.bk()
        po3 = po.re("p (h t) -> p h t", t=128)
        for h in range(4):
            for ui, u in enumerate(us):
                nk = P if u == u_cur else 128
                self.mm(po3[0:65, h, 0:P], p["Vc"][u % 6][0:nk, h, :], PTc[0:nk, ui, h, 0:P], start=(ui == 0), stop=(ui == len(us) - 1),
                        mark=(ui == len(us) - 1 and h == 3))
        rd = p["rd"]
        rd3 = rd.re("p (h t) -> p h t", t=128)
        self.act(rd3[64:65, :, 0:P], po3[64:65, :, 0:P], AF.Ln)
        self.act(rd3[64:65, :, 0:P], rd3[64:65, :, 0:P], AF.Exp, scale=-1.0)
        pb = self.bk()
        pb3 = pb.re("p (h t) -> p h t", t=128)
        for h in range(4):
            self.mm(pb3[0:64, h, 0:P], self.ones_f[64:65, 0:64], rd3[64:65, h, 0:P], mark=(h == 3))
        bcs3 = p["bc_s"].re("p (h t) -> p h t", t=128)
        self.cp("act", bcs3[:, :, 0:P], pb3[0:64, :, 0:P])
        self.tt("dve", p["ocT"][:, :, 0:P], po3[0:64, :, 0:P], bcs3[:, :, 0:P], ALU.mult)
        S.dma(seq["OT"][768:1024, tok0:tok0 + P].rearrange("(h d) t -> d h t", d=64), p["ocT"][:, :, 0:P])

    def run_interleaved(self, gens, lag):
        active = []
        idx = 0
        while idx < len(gens) or active:
            if idx < len(gens) and (not active or active[-1][1] >= lag):
                active.append([gens[idx], 0])
                idx += 1
            for a in list(active):
                try:
                    next(a[0])
                    a[1] += 1
                except StopIteration:
                    active.remove(a)

    def p1_hist(self, l, seq):
        p = self.p1
        S = self.S
        d = self.din
        nh = self.PAST // 128
        for t in range(nh):
            S.dma(p["ckv_s"], d["c_ckv"][l][t * 128:(t + 1) * 128, :])
            S.dma(p["kro"], d["c_kr"][l][t * 128:(t + 1) * 128, :])
            self.cp("act", p["ckvb"], p["ckv_s"])
            self.kv_build(128, t * 128, t, seq["KT"], seq["VA"], p["ckvb"], p["kro"])
        for u in range(4):
            S.dma(p["cst"], d["c_cak"][l][u * 128:(u + 1) * 128, :])
            self.cp("act", p["cstb"], p["cst"])
            pt = self.bkb()
            pt3 = pt.re("p (h t) -> p h t", t=128)
            for h in range(4):
                self.tr(pt3[0:64, h, :], p["cstb"][:, h * 64:(h + 1) * 64], 128, mark=(h == 3))
            self.cp("act", p["KcT"][u % 6], pt3[0:64, 0:4, :])
            S.dma(p["cst"], d["c_cav"][l][u * 128:(u + 1) * 128, :])
            self.cp("dve", p["Vc"][u % 6][:, :, 0:64], p["cst"].re("p (h d) -> p h d", d=64))
        self.memset("dve", p["ubf"], 0.0)
        S.dma(p["ubf"][113:128, :], d["st_pool"][l])
        self.cp("dve", p["ub"][2], p["ubf"])

    def alloc_p2(self):
        self.p2 = p = {}
        p["KT"] = [self.av(0, 2, [96, 8192]), self.av(5, 2, [96, 8192])]
        p["QT"] = [self.av(2, 2, [96, 8192]), self.av(7, 2, [96, 8192])]
        p["VA"] = [self.av(4, 1, [128, 64, 65]), self.av(9, 1, [128, 64, 65])]
        p["PT"] = [self.av(10, 1, [128, 512], off=i * 1024) for i in range(4)]
        for i, t in enumerate(p["PT"]):
            t.bufs = [self.slab[10]] if False else [Buf("p2pt%d" % i)]
        def sb(name, shape, dt=F32):
            nb = int(np.prod(shape[1:])) * (4 if dt == F32 else 2)
            return self.ab(name, shape, dt) if nb >= 512 else self.sb(name, shape, dt)
        self.ab_cur = 0
        p["rd"] = sb("p2_rd", [128, 512]); p["bc_s"] = [sb("p2_bcs%d" % i, [64, 512]) for i in range(2)]
        p["oT"] = [sb("p2_oT%d" % i, [64, 512], BF16) for i in range(2)]
        self.pt_i = 0
        self.p2_o = 0
        self.p2_s = 0
        self.p2slabdep = self.slab[10]

    def p2_seq(self, seq, causal):
        p = self.p2
        S = self.S
        NQ = seq["nq"]
        NKT = seq["nkt_total"]
        nk_last = seq["nk_last"]
        sc = 96 ** -0.5
        for i in range(4):
            self.memset("pool", T(p["PT"][i].ap, p["PT"][i].bufs + [self.p2slabdep]), 0.0)
        NK = (NKT - 1) * 128 + nk_last

        def load_head(h):
            s = h % 2
            KT, QT, VA = p["KT"][s], p["QT"][s], p["VA"][s]
            S.dma(KT[:, 0:NK], seq["KT"][h][:, 0:NK])
            S.dma(QT[:, 0:NQ], seq["QT"][h][:, 0:NQ])
            if nk_last == 128:
                S.dma(VA[:, 0:NKT, :], seq["VA"][h][:, 0:NKT, :])
            else:
                S.dma(VA[:, 0:NKT - 1, :], seq["VA"][h][:, 0:NKT - 1, :])
                S.dma(VA[0:nk_last, NKT - 1, :], seq["VA"][h][0:nk_last, NKT - 1, :])

        blocks = []
        for h in range(NH):
            s = h % 2
            KT, QT, VA = p["KT"][s], p["QT"][s], p["VA"][s]
            gi = 0
            first_of_head = True
            for q0 in range(0, NQ, 512):
                W = min(512, NQ - q0)
                kts = list(range(0, (q0 + W) // 128)) if causal else list(range(NKT))
                grp = dict(pO=None)
                for ki, kt in enumerate(kts):
                    nk = nk_last if kt == NKT - 1 else 128
                    diag = causal and kt * 128 >= q0
                    c0 = kt * 128 - q0 if diag else 0
                    blk = {}

                    def A(h=h, kt=kt, nk=nk, diag=diag, c0=c0, W=W, q0=q0, KT=KT, QT=QT, blk=blk, pre=first_of_head):
                        pS = self.bank(2 + self.p2_s % 5)
                        self.p2_s += 1
                        self.mm(pS[0:nk, c0:W], KT[:, kt * 128:kt * 128 + nk], QT[:, q0 + c0:q0 + W])
                        PT = p["PT"][self.pt_i]
                        self.pt_i = (self.pt_i + 1) % 4
                        self.act(PT[0:nk, c0:W], pS[0:nk, c0:W], AF.Exp, scale=sc)
                        if diag:
                            self.memset("pool", PT[64:128, c0:c0 + 64], 0.0)
                        blk["PT"] = PT

                    def B(h=h, kt=kt, nk=nk, c0=c0, W=W, q0=q0, VA=VA, blk=blk, ki=ki, nkts=len(kts), grp=grp, gi=gi, pre=first_of_head):
                        if pre and h + 1 < NH:
                            load_head(h + 1)
                        if ki == 0:
                            grp["pO"] = self.bank(self.p2_o % 2)
                            self.p2_o += 1
                        pO = grp["pO"]
                        self.mm(pO[0:65, c0:W], VA[0:nk, kt, :], blk["PT"][0:nk, c0:W], start=(ki == 0), stop=(ki == nkts - 1), mark=True)
                        if ki == nkts - 1:
                            rd = p["rd"]
                            self.act(rd[64:65, 0:W], pO[64:65, 0:W], AF.Ln)
                            self.act(rd[64:65, 0:W], rd[64:65, 0:W], AF.Exp, scale=-1.0)
                            pb = self.bank(7)
                            self.mm(pb[0:64, 0:W], self.ones_f[64:65, 0:64], rd[64:65, 0:W])
                            bcs = p["bc_s"][gi % 2]
                            oT = p["oT"][gi % 2]
                            self.cp("act", bcs[:, 0:W], pb[0:64, 0:W])
                            self.tt("dve", oT[:, 0:W], pO[0:64, 0:W], bcs[:, 0:W], ALU.mult)
                            S.dma(seq["OT"][h * 64:(h + 1) * 64, q0:q0 + W], oT[:, 0:W])

                    blocks.append((A, B))
                    first_of_head = False
                gi += 1
        LA = 3
        load_head(0)
        for i in range(len(blocks) + LA):
            if i < len(blocks):
                blocks[i][0]()
            if i - LA >= 0:
                blocks[i - LA][1]()
        for i in range(4):
            self.memset("pool", T(p["PT"][i].ap, p["PT"][i].bufs + [self.p2slabdep]), 0.0)

    def alloc_p3(self):
        def sb(name, shape, dt=F32):
            nb = int(np.prod(shape[1:])) * (4 if dt == F32 else 2)
            return self.ab(name, shape, dt) if nb >= 512 else self.sb(name, shape, dt)
        self.ab_cur = 0
        self.p3 = p = {}
        G = 256
        p["xt"] = [sb("p3_xt%d" % i, [128, 1024]) for i in range(2)]
        p["xr"] = [sb("p3_xr%d" % i, [128, 1024]) for i in range(2)]
        p["hb"] = sb("p3_hb", [128, 1024], BF16); p["junk"] = sb("p3_junk", [128, 1024], BF16)
        p["st"] = sb("p3_st", [128, 8])
        p["hT"] = [sb("p3_hT%d" % i, [128, 8, G], BF16) for i in range(2)]
        p["oT"] = sb("p3_oT", [128, 8, G], BF16)
        p["ed"] = [sb("p3_ed%d" % i, [128, G]) for i in range(4)]
        p["tt"] = [sb("p3_tt%d" % i, [128, G]) for i in range(4)]
        p["mT"] = sb("p3_mT", [128, 8, G], BF16)
        p["hxT"] = sb("p3_hxT", [128, 8, 128], BF16)
        p["pt"] = [sb("p3_pt%d" % i, [128, 256]) for i in range(2)]; p["pb"] = sb("p3_pb", [128, 256], BF16); p["pT"] = sb("p3_pT", [128, 2, 128], BF16)
        p["sig"] = sb("p3_sig", [128, 1024])
        self.ed_i = 0
        self.tt_i = 0
        self.gd_i = 0

    def p3_head(self, l, seq, g0, tiles, gi):
        p = self.p3
        S = self.S
        hT, st = p["hT"][gi % 2], p["st"]
        for j, (tok0, P) in enumerate(tiles):
            S.dma(p["xt"][j % 2][0:P, :], seq["x"][tok0:tok0 + P, :])
        for j, (tok0, P) in enumerate(tiles):
            xt = p["xt"][j % 2]
            jo = tok0 - g0
            self.act(p["junk"][0:P, :], xt[0:P, :], AF.Square, accum=st[0:P, 0:1])
            self.rstd(st[0:P, 0:1], self.invD[0:P, :], st[0:P, 1:2], st[0:P, 2:3])
            self.act(p["hb"][0:P, :], xt[0:P, :], AF.Copy, scale=st[0:P, 2:3])
            pt = self.bkb()
            pt3 = pt.re("p (k t) -> p k t", t=128)
            for k in range(8):
                self.tr(pt3[:, k, 0:P], p["hb"][0:P, k * 128:(k + 1) * 128], P, mark=(k == 7))
            self.cp("dve", hT[:, :, jo:jo + P], pt3[:, :, 0:P])

    def p3_group(self, l, seq, g0, tiles, gi=0, hook=None):
        p = self.p3
        S = self.S
        W = sum(P for _, P in tiles)
        hT, oT, mT, st = p["hT"][gi % 2], p["oT"], p["mT"], p["st"]
        S.dma(oT[:, :, 0:W], seq["OT"][:, g0:g0 + W].rearrange("(c p) t -> p c t", p=128))
        for j, (tok0, P) in enumerate(tiles):
            S.dma(p["xr"][j % 2][0:P, :], seq["x"][tok0:tok0 + P, :])
            S.dma(p["pt"][j % 2][0:P, :], seq["p"][l][tok0:tok0 + P, :])

        def ntt():
            t = p["tt"][self.tt_i]
            self.tt_i = (self.tt_i + 1) % 4
            return t

        def gate_items(items):
            for i in range(0, len(items), 2):
                grp = items[i:i + 2]
                st_ = []
                for it in grp:
                    ppd = it["pre"]() if it.get("pre") else None
                    pg = self.bk()
                    for k in range(8):
                        self.mm(pg[:, 0:W], self.w_fm[:, k, it["col0"]:it["col0"] + 128], hT[:, k, 0:W], start=(k == 0), stop=(k == 7))
                    ed = p["ed"][self.ed_i]
                    self.ed_i = (self.ed_i + 1) % 4
                    st_.append((pg, ppd, ed))
                for (pg, ppd, ed) in st_:
                    self.act(ed[:, 0:W], pg[:, 0:W], AF.Exp, scale=-1.0)
                for (pg, ppd, ed) in st_:
                    self.act(ed[:, 0:W], ed[:, 0:W], AF.Ln, bias=self.one_t[:, :])
                for (pg, ppd, ed) in st_:
                    self.act(ed[:, 0:W], ed[:, 0:W], AF.Exp, scale=-1.0)
                for it, (pg, ppd, ed) in zip(grp, st_):
                    it["post"](pg, ppd, ed)

        def silu_post(c):
            def f(pg, ppd, ed):
                t = ntt()
                self.tt("dve", t[:, 0:W], pg[:, 0:W], ed[:, 0:W], ALU.mult)
                self.tt("dve", oT[:, c, 0:W], t[:, 0:W], oT[:, c, 0:W], ALU.mult)
            return f

        gate_items([dict(col0=c * 128, post=silu_post(c)) for c in range(8)])
        accs = {}

        def mk_pre(oc, kc0, nkc):
            def f():
                ppd = self.bk()
                for kc in range(nkc):
                    self.mm(ppd[:, 0:W], self.w_o[:, kc0 + kc, oc * 128:(oc + 1) * 128], oT[:, kc0 + kc, 0:W], start=(kc == 0), stop=(kc == nkc - 1))
                return ppd
            return f

        def mk_post(oc, br):
            def f(pg, ppd, ed):
                t = ntt()
                self.tt("dve", t[:, 0:W], ppd[:, 0:W], ed[:, 0:W], ALU.mult)
                if br == 0:
                    accs[oc] = t
                elif br == 1:
                    self.tt("dve", accs[oc][:, 0:W], accs[oc][:, 0:W], t[:, 0:W], ALU.add)
                else:
                    self.tt("dve", mT[:, oc, 0:W], accs[oc][:, 0:W], t[:, 0:W], ALU.add)
            return f

        items = []
        for oc in range(8):
            for br, (kc0, nkc) in enumerate(((0, 4), (4, 2), (6, 2))):
                items.append(dict(col0=1024 + br * 1024 + oc * 128, pre=mk_pre(oc, kc0, nkc), post=mk_post(oc, br)))
        gate_items(items)
        if hook is not None:
            hook()
        for j, (tok0, P) in enumerate(tiles):
            jo = tok0 - g0
            xr, sig = p["xr"][j % 2], p["sig"]
            x2 = xr
            ptl = p["pt"][j % 2]
            self.cp("act", p["pb"][0:P, :], ptl[0:P, :])
            pt = self.bkb()
            pt3 = pt.re("p (k t) -> p k t", t=128)
            for k in range(2):
                self.tr(pt3[:, k, 0:P], p["pb"][0:P, k * 128:(k + 1) * 128], P, mark=(k == 1))
            self.cp("dve", p["pT"][:, :, 0:P], pt3[:, 0:2, 0:P])
            ppp = T(self.pp_t[3], [self.pbuf[6], self.pbuf[7]])
            for c in range(2):
                for k in range(2):
                    self.mm(ppp[0:P, c * 512:(c + 1) * 512], p["pT"][:, k, 0:P], self.w_pp[:, k, c * 512:(c + 1) * 512], start=(k == 0), stop=(k == 1))
            px = self.bk2()
            for c in range(2):
                for k in range(8):
                    self.mm(px[0:P, c * 512:(c + 1) * 512], mT[:, k, jo:jo + P], self.w_out[:, k, c * 512:(c + 1) * 512], start=(k == 0), stop=(k == 7))
            for c in range(2):
                self.tt("dve", x2[0:P, c * 512:(c + 1) * 512], px[0:P, c * 512:(c + 1) * 512], xr[0:P, c * 512:(c + 1) * 512], ALU.add)
            self.act(p["junk"][0:P, :], x2[0:P, :], AF.Square, accum=st[0:P, 4:5])
            self.rstd(st[0:P, 4:5], self.invD[0:P, :], st[0:P, 5:6], st[0:P, 6:7])
            self.act(p["hb"][0:P, :], x2[0:P, :], AF.Copy, scale=st[0:P, 6:7])
            pt = self.bkb()
            pt3 = pt.re("p (k t) -> p k t", t=128)
            for k in range(8):
                self.tr(pt3[:, k, 0:P], p["hb"][0:P, k * 128:(k + 1) * 128], P, mark=(k == 7))
            self.cp("dve", p["hxT"][:, :, 0:P], pt3[:, :, 0:P])
            pgt = self.bk2()
            for c in range(2):
                for k in range(8):
                    self.mm(pgt[0:P, c * 512:(c + 1) * 512], p["hxT"][:, k, 0:P], self.w_pg[:, k, c * 512:(c + 1) * 512], start=(k == 0), stop=(k == 7))
            for c in range(2):
                self.act(sig[0:P, c * 512:(c + 1) * 512], pgt[0:P, c * 512:(c + 1) * 512], AF.Exp, scale=-1.0)
            for c in range(2):
                self.act(sig[0:P, c * 512:(c + 1) * 512], sig[0:P, c * 512:(c + 1) * 512], AF.Ln, bias=self.one_t[0:P, :])
            for c in range(2):
                self.act(sig[0:P, c * 512:(c + 1) * 512], sig[0:P, c * 512:(c + 1) * 512], AF.Exp, scale=-1.0)
            for c in range(2):
                self.tt("dve", sig[0:P, c * 512:(c + 1) * 512], ppp[0:P, c * 512:(c + 1) * 512], sig[0:P, c * 512:(c + 1) * 512], ALU.mult)
            self.tt("dve", x2[0:P, :], x2[0:P, :], sig[0:P, :], ALU.add)
            S.dma(seq["xo"][tok0:tok0 + P, :], x2[0:P, :])

    def build(self):
        S = self.S
        d, o = self.din, self.dout
        LP, LS, PAST = self.LP, self.LS, self.PAST
        self.phase_log = []
        plog = lambda name: self.phase_log.append((name, dict(S.nm)))
        self.plog = plog
        self.setup_consts()
        self.alloc_p1()
        self.alloc_p2()
        self.alloc_p3()
        import os
        STOP = int(os.environ.get("KSTOP", "99"))
        for l in range(NL):
            if STOP < 99 and l > 0:
                break
            seqP = dict(P=128, nt=LP // 128, kt0=0, ca0=0, hist=False,
                        x=(d["xp"] if l == 0 else self.x1p), xo=(self.x1p if l < NL - 1 else o["y_p"]),
                        rope=d["ropeP"], p=d["pp"], QT=self.QTp, KT=self.KTp, VA=self.VAp, OT=self.OTp,
                        o_ckv=o["ckv_p"], o_kr=o["kr_p"], o_cak=o["cak_p"], o_cav=o["cav_p"], o_pool=[o["pool_p"][i] for i in range(NL)],
                        nq=LP, nkt_total=LP // 128, nk_last=128)
            seqS = dict(P=LS, nt=1, kt0=PAST // 128, ca0=4, hist=True,
                        x=(d["xs"] if l == 0 else self.x1s), xo=(self.x1s if l < NL - 1 else o["y_s"]),
                        rope=d["ropeS"], p=d["ps"], QT=self.QTs, KT=self.KTs, VA=self.VAs, OT=self.OTs,
                        o_ckv=o["ckv_s"], o_kr=o["kr_s"], o_cak=o["cak_s"], o_cav=o["cav_s"], o_pool=[o["pool_s"][i] for i in range(NL)],
                        nq=LS, nkt_total=PAST // 128 + 1, nk_last=LS)
            S.dma_fence()
            plog("L%d start" % l)
            self.load_layer_small(l)
            if STOP <= 0:
                break
            self.load_p1_weights(l)
            if STOP <= 1:
                break
            plog("L%d p1w done" % l)
            self.run_interleaved([self.p1_tile(l, seqP, t) for t in range(seqP["nt"])], 4)
            plog("L%d p1 prompt done" % l)
            if STOP <= 2:
                break
            self.p1_hist(l, seqS)
            self.run_interleaved([self.p1_tile(l, seqS, 0)], 4)
            S.dma_fence()
            if STOP <= 3:
                break
            plog("L%d p1 sample done" % l)
            self.p2_seq(seqP, True)
            plog("L%d p2 prompt done" % l)
            self.p2_seq(seqS, False)
            plog("L%d p2 sample done" % l)
            S.dma_fence()
            if STOP <= 4:
                break
            self.bk_lim = 6
            self.load_p3_weights(l)
            if STOP <= 5:
                break
            plog("L%d p3w done" % l)
            G = 256
            groups = [(g0, [(g0 + j * 128, 128) for j in range(G // 128)]) for g0 in range(0, LP, G)]
            groups.append(None)
            self.p3_head(l, seqP, groups[0][0], groups[0][1], 0)
            for gi in range(len(groups) - 1):
                g0, tl = groups[gi]
                if groups[gi + 1] is not None:
                    nxt = groups[gi + 1]
                    hook = (lambda nxt=nxt, gi=gi: self.p3_head(l, seqP, nxt[0], nxt[1], gi + 1))
                else:
                    hook = (lambda gi=gi: self.p3_head(l, seqS, 0, [(0, LS)], gi + 1))
                self.p3_group(l, seqP, g0, tl, gi, hook)
            plog("L%d p3 prompt done" % l)
            self.p3_group(l, seqS, 0, [(0, LS)], len(groups) - 1, None)
            plog("L%d p3 sample done" % l)
            self.bk_lim = 8
        S.dma_fence()
        nc = self.nc
        with nc.Block() as block:
            @block.tensor
            def _(e):
                S.replay("pe", e)

            @block.scalar
            def _(e):
                S.replay("act", e)

            @block.vector
            def _(e):
                S.replay("dve", e)

            @block.gpsimd
            def _(e):
                S.replay("pool", e)

            @block.sync
            def _(e):
                S.replay("sp", e)
        return nc


def _rope_table(pos):
    half = 16
    inv = 10000.0 ** (-np.arange(half, dtype=np.float64) / half)
    ang = (pos.astype(np.float32)[:, None] * inv.astype(np.float32)[None, :]).astype(np.float32)
    c = np.cos(ang.astype(np.float64)); s = np.sin(ang.astype(np.float64))
    return np.concatenate([c, c, -s, s], axis=1).astype(np.float32)


def _bands():
    out = np.zeros((3, 4, 128, 128), np.float32)
    i = np.arange(128)[:, None]; j = np.arange(128)[None, :]
    for g, w in enumerate((2, 4, 8, 16)):
        inw = ((j - i) >= 0) & ((j - i) < w)
        out[0, g] = inw / w - (i == j)
        out[1, g] = (((j + 128 - i) >= 0) & ((j + 128 - i) < w)) / w
        cnt = np.minimum(j + 1, w)
        out[2, g] = inw / cnt - (i == j)
    return np.ascontiguousarray(out.reshape(12, 128, 128).transpose(1, 0, 2))


def _bias_idx():
    idx = np.zeros((5, 128, 128), np.int64); msk = np.zeros((5, 128, 128), bool)
    r = np.arange(128)[:, None]; c = np.arange(128)[None, :]
    for du in range(5):
        rel = (du - 4) * 128 + r - c
        idx[du] = np.clip(rel, -128, 128) + 128
        kc = ((du - 4) * 128 + r) // 64
        qc = c // 64
        msk[du] = (kc > qc) | (kc < qc - 8)
    return idx, msk


def _prep(inp, c, LP, PAST, LS):
    f = np.float32
    m = {}
    m["xp"] = inp["x_prompt"][c]; m["xs"] = inp["x_sample"][c]
    m["c_ckv"] = inp["cache_mla_ckv"][:, c]; m["c_kr"] = inp["cache_mla_krope"][:, c]
    m["c_cak"] = inp["cache_ca_k"][:, c].reshape(NL, 512, 256); m["c_cav"] = inp["cache_ca_v"][:, c].reshape(NL, 512, 256)
    m["st_pool"] = inp["state_pool"][:, c]
    m["pp"] = inp["p_prompt"][:, c]; m["ps"] = inp["p_sample"][:, c]
    return {k: np.ascontiguousarray(v, dtype=f) for k, v in m.items()}


def _shared(inp, LP, PAST, LS):
    f = np.float32
    m = {}
    for k_, n in (("w_in", "w_in"), ("w_uq", "mla_w_uq"), ("w_ukv", "mla_w_ukv"), ("w_o_mla", "w_o_mla"), ("pool_w", "pool_w"),
                  ("w_o_pool", "w_o_pool"), ("w_o_ca", "w_o_ca"), ("w_out", "w_out"), ("w_pg", "w_ple_gate"), ("w_pp", "w_ple_proj")):
        m[k_] = inp[n]
    gcol = np.concatenate([inp["norm_in"].reshape(NL, 8, 128).transpose(0, 2, 1), inp["ple_norm"].reshape(NL, 8, 128).transpose(0, 2, 1),
                           inp["mla_q_norm"].reshape(NL, 2, 128).transpose(0, 2, 1)], axis=2)
    m["gcol"] = gcol
    caq = np.tile(inp["ca_qn"][:, None, :], (1, 4, 1)).reshape(NL, 256)
    cak = np.tile(inp["ca_kn"][:, None, :], (1, 4, 1)).reshape(NL, 256)
    m["grow"] = np.concatenate([inp["mla_kv_norm"], inp["mla_kn_rope"], inp["mla_qn_nope"], inp["mla_qn_rope"], inp["mla_kn_nope"],
                                caq, cak], axis=1)[:, None, :]
    m["plsc"] = inp["pool_scale"].reshape(NL, 4, 64).transpose(0, 2, 1)
    idx, msk = _bias_idx()
    tab = inp["ca_rel_bias"]
    bt = tab[:, :, idx]
    bt = np.where(msk[None, None], f(-30000.0), bt)
    m["biasT"] = bt.transpose(0, 3, 1, 2, 4).reshape(NL, 128, 20, 128)
    m["ident"] = np.eye(128, dtype=f)
    m["bands"] = _bands()
    m["ropeP"] = _rope_table(np.arange(LP)); m["ropeS"] = _rope_table(PAST + np.arange(LS))
    return {k: np.ascontiguousarray(v, dtype=f) for k, v in m.items()}


_CACHE = {}


def run(inp, ncores, LP, PAST, LS=64):
    key = (LP, PAST, LS)
    if key not in _CACHE:
        _CACHE[key] = K(LP, PAST, LS).build()
    nc = _CACHE[key]
    sh = _shared(inp, LP, PAST, LS)
    in_maps = []
    for c in range(ncores):
        mm_ = dict(sh)
        mm_.update(_prep(inp, c, LP, PAST, LS))
        in_maps.append(mm_)
    res = run_bass_kernel_spmd(nc, in_maps, core_ids=list(range(ncores)))
    R = res.results
    st = lambda k, ax: np.stack([r[k] for r in R], axis=ax)
    y_p = st("y_p", 0); y_s = st("y_s", 0)
    outs = (y_p, y_s, st("ckv_p", 1), st("kr_p", 1),
            st("cak_p", 1).reshape(NL, ncores, -1, 4, 64), st("cav_p", 1).reshape(NL, ncores, -1, 4, 64), st("pool_p", 1),
            st("ckv_s", 1), st("kr_s", 1), st("cak_s", 1).reshape(NL, ncores, -1, 4, 64), st("cav_s", 1).reshape(NL, ncores, -1, 4, 64),
            st("pool_s", 1))
    return tuple(np.ascontiguousarray(o, dtype=np.float32) for o in outs)


def kernel(**inputs):
    inp = {k: np.asarray(v) for k, v in inputs.items()}
    return run(inp, 8, 8192, 4096, 64)
```

```python
import numpy as np
import concourse.bass as bass
import concourse.mybir as mybir
from concourse.bass_utils import run_bass_kernel_spmd

F32 = mybir.dt.float32
BF16 = mybir.dt.bfloat16
AF = mybir.ActivationFunctionType
ALU = mybir.AluOpType
AX = mybir.AxisListType

D = 1024
EPS = 1e-6
NL = 2
NH = 8
CA_H = 4
SLAB = 8320
NSLAB = 14
NSLABB = 56


class Buf:
    def __init__(self, name):
        self.name = name
        self.w = None
        self.r = {}
        self.dsem = None
        self.dcnt = 0


class T:
    def __init__(self, ap, bufs):
        self.ap = ap
        self.bufs = list(bufs)

    def __getitem__(self, k):
        return T(self.ap[k], self.bufs)

    def re(self, s, **kw):
        return T(self.ap.rearrange(s, **kw), self.bufs)

    def bc(self, axis, shape):
        return T(self.ap.unsqueeze(axis).to_broadcast(shape), self.bufs)

    def bit(self, dt):
        return T(self.ap.bitcast(dt), self.bufs)


def _ap(x):
    return x.ap if isinstance(x, T) else x


class Sched:
    ENG = ["pe", "act", "dve", "pool", "sp"]

    def __init__(self, nc):
        self.nc = nc
        self.ops = {e: [] for e in self.ENG}
        self.nm = {e: 0 for e in self.ENG}
        self.sem = {e: nc.alloc_semaphore("s_" + e) for e in self.ENG}
        self.waited = {e: {} for e in self.ENG}
        self.pend = {e: ([], []) for e in self.ENG}
        self.dmarecs = {}
        self.nsem = 5

    def _need(self, e, rec):
        if rec is None:
            return
        key, sem, val = rec
        if key == "pe" and e == "pe":
            return
        if self.waited[e].get(key, 0) >= val:
            return
        self.waited[e][key] = val
        self.ops[e].append(("wait", sem, val))

    def op(self, e, fn, r=(), w=(), mark=True):
        rb = [b for t in r if isinstance(t, T) for b in t.bufs]
        wb = [b for t in w if isinstance(t, T) for b in t.bufs]
        for b in rb:
            self._need(e, b.w)
        for b in wb:
            self._need(e, b.w)
            for rec in list(b.r.values()):
                self._need(e, rec)
        pr, pw = self.pend[e]
        if mark:
            self.nm[e] += 1
            rec = (e, self.sem[e], self.nm[e])
            self.ops[e].append(("op", fn, self.sem[e]))
            for b in pw + wb:
                b.w = rec
                b.r = {}
            for b in pr + rb:
                if b.w is not rec:
                    b.r[e] = rec
            self.pend[e] = ([], [])
        else:
            self.ops[e].append(("op", fn, None))
            pr.extend(rb)
            pw.extend(wb)

    def dma(self, out, in_, q=None):
        e = q if q is not None else ("sp" if isinstance(out, T) else "pool")
        sb = out if isinstance(out, T) else in_
        rb = in_.bufs if isinstance(in_, T) else []
        wb = out.bufs if isinstance(out, T) else []
        for b in rb:
            self._need(e, b.w)
        for b in wb:
            self._need(e, b.w)
            for rec in list(b.r.values()):
                self._need(e, rec)
        b0 = sb.bufs[0]
        if b0.dsem is None:
            b0.dsem = {}
            b0.dcnt = {}
        if e not in b0.dsem:
            b0.dsem[e] = self.nc.alloc_semaphore("d%s_%s" % (e, b0.name))
            b0.dcnt[e] = 0
            self.nsem += 1
        b0.dcnt[e] += 16
        key = "d%s:%s" % (e, b0.name)
        rec = (key, b0.dsem[e], b0.dcnt[e])
        self.ops[e].append(("dma", _ap(out), _ap(in_), b0.dsem[e]))
        for b in wb:
            b.w = rec
            b.r = {}
        for b in rb:
            b.r[key] = rec
        self.dmarecs[key] = rec

    def dma_fence(self):
        for e in ("sp", "pool"):
            for rec in list(self.dmarecs.values()):
                self._need(e, rec)

    def replay(self, e, eng):
        for it in self.ops[e]:
            if it[0] == "wait":
                eng.wait_ge(it[1], it[2])
            elif it[0] == "op":
                ins = it[1](eng)
                if it[2] is not None:
                    ins.then_inc(it[2], 1)
            else:
                eng.dma_start(out=it[1], in_=it[2]).then_inc(it[3], 16)


class K:
    def __init__(self, LP, PAST, LS=64):
        self.LP, self.PAST, self.LS = LP, PAST, LS
        nc = self.nc = bass.Bass("TRN2", target_bir_lowering=False)
        self.S = Sched(nc)
        import os
        self.pstop = float(os.environ.get("PSTOP", "99"))
        self.nbuf = 0
        self.din = {}
        self.dout = {}
        self._decl_io()
        self._alloc()

    def _in(self, name, shape):
        self.din[name] = self.nc.dram_tensor(name, list(shape), F32, kind="ExternalInput").ap()
        return self.din[name]

    def _out(self, name, shape):
        self.dout[name] = self.nc.dram_tensor(name, list(shape), F32, kind="ExternalOutput").ap()
        return self.dout[name]

    def _scr(self, name, shape, dt=BF16):
        return self.nc.dram_tensor(name, list(shape), dt, kind="Internal").ap()

    def _decl_io(self):
        LP, PAST, LS = self.LP, self.PAST, self.LS
        i = self._in
        i("xp", [LP, D]); i("xs", [LS, D])
        i("c_ckv", [NL, PAST, 128]); i("c_kr", [NL, PAST, 32])
        i("c_cak", [NL, 512, 256]); i("c_cav", [NL, 512, 256])
        i("st_pool", [NL, 15, 256])
        i("pp", [NL, LP, 256]); i("ps", [NL, LS, 256])
        i("w_in", [NL, D, 5536]); i("w_uq", [NL, 256, 768]); i("w_ukv", [NL, 128, 1024])
        i("w_o_mla", [NL, 512, D]); i("pool_w", [NL, 4, 64, 64]); i("w_o_pool", [NL, 256, D])
        i("w_o_ca", [NL, 256, D]); i("w_out", [NL, D, D]); i("w_pg", [NL, D, D]); i("w_pp", [NL, 256, D])
        i("gcol", [NL, 128, 18])
        i("grow", [NL, 1, 128 + 32 + 96 + 64 + 512])
        i("plsc", [NL, 64, 4])
        i("biasT", [NL, 128, 20, 128])
        i("ident", [128, 128])
        i("bands", [128, 12, 128])
        i("ropeP", [LP, 64]); i("ropeS", [LS, 64])
        o = self._out
        o("y_p", [LP, D]); o("y_s", [LS, D])
        o("ckv_p", [NL, LP, 128]); o("kr_p", [NL, LP, 32])
        o("cak_p", [NL, 512, 256]); o("cav_p", [NL, 512, 256]); o("pool_p", [NL, 15, 256])
        o("ckv_s", [NL, LS, 128]); o("kr_s", [NL, LS, 32])
        o("cak_s", [NL, LS, 256]); o("cav_s", [NL, LS, 256]); o("pool_s", [NL, 15, 256])
        s = self._scr
        self.nktP = LP // 128
        self.nktS = PAST // 128 + 1
        self.x1p = s("x1p", [LP, D], F32); self.x1s = s("x1s", [LS, D], F32)
        self.QTp = s("QTp", [NH, 96, LP]); self.KTp = s("KTp", [NH, 96, LP])
        self.VAp = s("VAp", [NH, 128, self.nktP, 65])
        self.OTp = s("OTp", [D, LP])
        LK = self.nktS * 128
        self.QTs = s("QTs", [NH, 96, LS]); self.KTs = s("KTs", [NH, 96, LK])
        self.VAs = s("VAs", [NH, 128, self.nktS, 65])
        self.OTs = s("OTs", [D, LS])

    def sb(self, name, shape, dt=F32, bufs=None):
        self.nbuf += 1
        ap = self.nc.alloc_sbuf_tensor("sb_" + name, list(shape), dt).ap()
        return T(ap, bufs if bufs is not None else [Buf(name)])

    def _alloc(self):
        nc = self.nc
        self.pp_t = [nc.alloc_psum_tensor("pp%d" % i, [128, 1024], F32).ap() for i in range(4)]
        self.pbuf = [Buf("pb%d" % i) for i in range(8)]
        self.bk_i = 0
        self.bk2_i = 0
        self.arena = nc.alloc_sbuf_tensor("arena", [128, NSLAB * SLAB // 2], BF16).ap()
        self.slab = [Buf("slab%d" % i) for i in range(NSLAB)]
        self.arenaB = nc.alloc_sbuf_tensor("arenaB", [128, NSLABB * 512], BF16).ap()
        self.slabB = [Buf("slabB%d" % i) for i in range(NSLABB)]
        self.ab_cur = 0
        sb = self.sb
        self.identf = sb("identf", [128, 128]); self.identb = sb("identb", [128, 128], BF16)
        self.ones_f = sb("ones_f", [128, 64])
        self.bandf = self.av(3, 1, [128, 12 * 128], F32)
        self.bands = sb("bands", [128, 12, 128], BF16)
        self.gcol = sb("gcol", [128, 18])
        self.grow = sb("grow", [128, 832])
        self.plsc = sb("plsc", [64, 4])
        self.biasT = sb("biasTb", [128, 20, 128], BF16)
        self.invn1 = sb("invn1", [128, 11]); self.invn2 = sb("invn2", [128, 24])
        self.w_uq = sb("w_uq", [128, 2, 768], BF16); self.w_ukv = sb("w_ukv", [128, 1024], BF16)
        self.pool_w = sb("pool_w", [64, 4, 64], BF16); self.w_pp = sb("w_ppb", [128, 2, 1024], BF16)
        self.stg = [sb("stg%d" % i, [128, 1024]) for i in range(2)]
        self.stg += [self.av(NSLABB - 8 + 4 * i, 4, [128, 1024], F32, arena="B") for i in range(2)]
        self.stg_i = 0

    def ab(self, name, shape, dt=F32):
        esz = 4 if dt == F32 else 2
        nb = int(np.prod(shape[1:])) * esz
        ns = (nb + 1023) // 1024
        t = self.av(self.ab_cur, ns, shape, dt, arena="B")
        self.ab_cur += ns
        assert self.ab_cur <= NSLABB, (name, self.ab_cur)
        return t

    def av(self, s0, n, shape, dt=BF16, off=0, arena="A"):
        SLAB_ = SLAB if arena == "A" else 1024
        ar = self.arena if arena == "A" else self.arenaB
        slabs = self.slab if arena == "A" else self.slabB
        base = ar[:, s0 * SLAB_ // 2 + off // 2: (s0 + n) * SLAB_ // 2]
        esz = 4 if dt == F32 else 2
        nel = int(np.prod(shape[1:]))
        assert off + nel * esz <= n * SLAB_, (shape, n)
        ap = base[:, 0:nel * esz // 2]
        if dt == F32:
            ap = ap.bitcast(F32)
        ap = ap[0:shape[0], :]
        if len(shape) == 3:
            ap = ap.rearrange("p (a b) -> p a b", b=shape[2])
        elif len(shape) == 4:
            ap = ap.rearrange("p (a b c) -> p a b c", b=shape[2], c=shape[3])
        return T(ap, slabs[s0:s0 + n])

    def bk(self):
        i = self.bk_i
        self.bk_i = (i + 1) % 8
        return T(self.pp_t[i // 2][:, (i % 2) * 512:(i % 2 + 1) * 512], [self.pbuf[i]])

    def bank(self, i):
        return T(self.pp_t[i // 2][:, (i % 2) * 512:(i % 2 + 1) * 512], [self.pbuf[i]])

    def bk2(self):
        i = self.bk2_i
        self.bk2_i = (i + 1) % 4
        return T(self.pp_t[i], [self.pbuf[2 * i], self.pbuf[2 * i + 1]])

    def act(self, out, in_, func, scale=1.0, bias=0.0, accum=None):
        r = [in_] + [x for x in (scale, bias) if isinstance(x, T)]
        w = [out] + ([accum] if accum is not None else [])
        kw = dict(out=out.ap, in_=in_.ap, func=func, bias=_ap(bias), scale=_ap(scale))
        if accum is not None:
            kw["accum_out"] = accum.ap
        self.S.op("act", lambda e: e.activation(**kw), r, w)

    def tt(self, eng, out, a, b, op):
        self.S.op(eng, lambda e: e.tensor_tensor(out=out.ap, in0=a.ap, in1=b.ap, op=op), [a, b], [out])

    def ts(self, eng, out, a, s1, op0, s2=None, op1=None):
        r = [a] + [x for x in (s1, s2) if isinstance(x, T)]
        if op1 is None:
            fn = lambda e: e.tensor_scalar(out=out.ap, in0=a.ap, scalar1=_ap(s1), scalar2=None, op0=op0)
        else:
            fn = lambda e: e.tensor_scalar(out=out.ap, in0=a.ap, scalar1=_ap(s1), scalar2=_ap(s2), op0=op0, op1=op1)
        self.S.op(eng, fn, r, [out])

    def stt(self, eng, out, a, s, b, op0, op1):
        r = [a, b] + ([s] if isinstance(s, T) else [])
        self.S.op(eng, lambda e: e.scalar_tensor_tensor(out=out.ap, in0=a.ap, scalar=_ap(s), in1=b.ap, op0=op0, op1=op1), r, [out])

    def cp(self, eng, out, in_):
        if eng == "act":
            self.act(out, in_, AF.Copy)
        else:
            self.S.op(eng, lambda e: e.tensor_copy(out=out.ap, in_=in_.ap), [in_], [out])

    def red(self, eng, out, in_):
        self.S.op(eng, lambda e: e.tensor_reduce(out=out.ap, in_=in_.ap, axis=AX.X, op=ALU.add), [in_], [out])

    def memset(self, eng, out, val):
        self.S.op(eng, lambda e: e.memset(out.ap, val), [], [out])

    def recip(self, out, in_):
        self.S.op("dve", lambda e: e.reciprocal(out=out.ap, in_=in_.ap), [in_], [out])

    def mm(self, out, lhsT, rhs, start=True, stop=True, mark=None):
        if mark is None:
            mark = stop
        self.S.op("pe", lambda e: e.matmul(out.ap, lhsT=lhsT.ap, rhs=rhs.ap, start=start, stop=stop), [lhsT, rhs], [out], mark=mark)

    def tr(self, out, in_, P, mark=True):
        idn = self.identb[0:P, 0:P]
        self.S.op("pe", lambda e: e.transpose(out=out.ap, in_=in_.ap, identity=idn.ap), [in_, idn], [out], mark=mark)

    def rstd(self, st_in, invn, tmp, out):
        self.tt("dve", tmp, st_in, invn, ALU.mult)
        self.act(tmp, tmp, AF.Ln, bias=self.eps_t[0:tmp.ap.shape[0], :])
        self.act(out, tmp, AF.Exp, scale=-0.5)

    def load_w(self, dst, src, scale=None, eng="pool"):
        n = dst.ap.shape[-1]
        rows = dst.ap.shape[0]
        for c0 in range(0, n, 1024):
            c1 = min(n, c0 + 1024)
            st = self.stg[self.stg_i]
            self.stg_i = (self.stg_i + 1) % len(self.stg)
            self.S.dma(st[0:rows, 0:c1 - c0], src[:, c0:c1])
            if scale is not None:
                self.act(dst[:, c0:c1], st[0:rows, 0:c1 - c0], AF.Copy, scale=scale)
            else:
                self.cp(eng, dst[:, c0:c1], st[0:rows, 0:c1 - c0])

    def setup_consts(self):
        S = self.S
        d = self.din
        S.dma(self.identf, d["ident"])
        self.cp("dve", self.identb, self.identf)
        self.memset("dve", self.ones_f, 1.0)
        self.eps_t = self.sb("eps_t", [128, 1])
        self.memset("dve", self.eps_t, EPS)
        self.one_t = self.sb("one_t", [128, 1])
        self.memset("dve", self.one_t, 1.0)
        S.dma(self.bandf, d["bands"].rearrange("p a b -> p (a b)"))
        self.cp("dve", self.bands.re("p a b -> p (a b)"), self.bandf)
        for c0, c1, v in ((0, 1, 1 / 256), (1, 2, 1 / 128), (2, 3, 1 / 32), (3, 11, 1 / 64)):
            self.memset("dve", self.invn1[:, c0:c1], v)
        for c0, c1, v in ((0, 8, 1 / 64), (8, 16, 1 / 32), (16, 24, 1 / 64)):
            self.memset("dve", self.invn2[:, c0:c1], v)
        self.invD = self.sb("invD", [128, 1])
        self.memset("dve", self.invD, 1.0 / D)

    def load_layer_small(self, l):
        S = self.S
        d = self.din
        S.dma(self.gcol, d["gcol"][l])
        S.dma(self.grow, d["grow"][l].partition_broadcast(128))
        S.dma(self.plsc, d["plsc"][l])
        self.ts("dve", self.grow[:, 320:576], self.grow[:, 320:576], 0.125, ALU.mult)
        bst = self.av(3, 2, [128, 20 * 128], F32)
        S.dma(bst, d["biasT"][l].rearrange("p a b -> p (a b)"))
        self.cp("pool", self.biasT.re("p a b -> p (a b)"), bst)
        for k in range(2):
            self.load_w(self.w_uq[:, k, :], d["w_uq"][l][k * 128:(k + 1) * 128, :], scale=self.gcol[:, 16 + k:17 + k])
            self.load_w(self.w_pp[:, k, :], d["w_pp"][l][k * 128:(k + 1) * 128, :])
        self.load_w(self.w_ukv, d["w_ukv"][l])
        for g in range(4):
            self.load_w(self.pool_w[:, g, :], d["pool_w"][l][g])

    def load_p1_weights(self, l):
        w_in = self.din["w_in"][l]
        self.w_tm = self.av(0, 3, [128, 8, 1440])
        pieces = [(0, 256, 0), (928, 1184, 256), (1440, 1952, 512), (1952, 2208, 1024), (256, 416, 1280)]
        import os
        KK = int(os.environ.get("KK", "8")); NP_ = int(os.environ.get("KNP", "5"))
        for k in range(KK):
            for (a, b, o) in pieces[:NP_]:
                self.load_w(self.w_tm[:, k, o:o + b - a], w_in[k * 128:(k + 1) * 128, a:b], scale=self.gcol[:, k:k + 1])

    def load_p3_weights(self, l):
        d = self.din
        w_in = d["w_in"][l]
        self.w_fm = self.av(0, 8, [128, 8, 4096])
        self.w_o = self.av(8, 2, [128, 8, 1024])
        self.w_out = self.av(10, 2, [128, 8, 1024])
        self.w_pg = self.av(12, 2, [128, 8, 1024])
        pieces = [(416, 928, 0), (1184, 1440, 512), (2208, 2464, 768), (2464, 3488, 1024), (3488, 4512, 2048), (4512, 5536, 3072)]
        for k in range(8):
            for (a, b, o) in pieces:
                self.load_w(self.w_fm[:, k, o:o + b - a], w_in[k * 128:(k + 1) * 128, a:b], scale=self.gcol[:, k:k + 1])
            self.load_w(self.w_out[:, k, :], d["w_out"][l][k * 128:(k + 1) * 128, :], eng="dve")
            self.load_w(self.w_pg[:, k, :], d["w_pg"][l][k * 128:(k + 1) * 128, :], scale=self.gcol[:, 8 + k:9 + k])
        for k in range(4):
            self.load_w(self.w_o[:, k, :], d["w_o_mla"][l][k * 128:(k + 1) * 128, :], eng="dve")
        for k in range(2):
            self.load_w(self.w_o[:, 4 + k, :], d["w_o_pool"][l][k * 128:(k + 1) * 128, :])
            self.load_w(self.w_o[:, 6 + k, :], d["w_o_ca"][l][k * 128:(k + 1) * 128, :])

    def alloc_p1(self):
        av = self.av
        o = [0]

        def A(shape, dt=BF16, slab=None):
            esz = 4 if dt == F32 else 2
            nb = int(np.prod(shape[1:])) * esz
            ns = (nb + SLAB - 1) // SLAB
            t = av(o[0], ns, shape, dt)
            o[0] += ns
            return t

        o[0] = 3
        self.p1 = p = {}
        p["xt"] = [A([128, 1024], F32), A([128, 1024], F32)]
        p["kv_s"] = A([128, 1024], F32)
        p["zA"] = A([128, 512], F32)
        p["zB"] = A([128, 512], F32)
        p["zC"] = A([128, 416], F32)
        p["q_s"] = A([128, 768], F32)
        p["sq"] = A([128, 768], F32)
        p["hT"] = A([128, 8, 128])
        assert o[0] <= 12
        def sb(name, shape, dt=F32):
            nb = int(np.prod(shape[1:])) * (4 if dt == F32 else 2)
            return self.ab(name, shape, dt) if nb >= 512 else self.sb(name, shape, dt)
        self.ab_cur = 0
        p["hb"] = sb("p1_hb", [128, 1024], BF16)
        p["junk"] = sb("p1_junk", [128, 1024], BF16)
        p["st0"] = sb("p1_st0", [128, 4])
        p["st1"] = sb("p1_st1", [128, 11]); p["tm1"] = sb("p1_tm1", [128, 11]); p["rs1"] = sb("p1_rs1", [128, 11])
        p["st2"] = sb("p1_st2", [128, 24]); p["tm2"] = sb("p1_tm2", [128, 24]); p["rs2"] = sb("p1_rs2", [128, 24])
        p["cq"] = sb("p1_cq", [128, 256], BF16); p["cqT"] = sb("p1_cqT", [128, 2, 128], BF16)
        p["ckv_s"] = sb("p1_ckv", [128, 128]); p["ckvb"] = sb("p1_ckvb", [128, 128], BF16); p["ckvT"] = sb("p1_ckvT", [128, 128], BF16)
        p["kr"] = sb("p1_kr", [128, 32]); p["kt1"] = sb("p1_kt1", [128, 32]); p["kt2"] = sb("p1_kt2", [128, 32]); p["kro"] = sb("p1_kro", [128, 32])
        p["rp"] = [sb("p1_rp%d" % i, [128, 64]) for i in range(3)]
        p["qkn"] = sb("p1_qkn", [128, 512]); p["qkb"] = sb("p1_qkb", [128, 512], BF16)
        p["qa"] = sb("p1_qa", [128, 8, 96], BF16); p["ka"] = sb("p1_ka", [128, 8, 96], BF16)
        p["va"] = self.sb("p1_va", [128, 8, 65], BF16)
        p["trp"] = sb("p1_trp", [128, 8, 32]); p["t1"] = sb("p1_t1", [128, 8, 32]); p["t2"] = sb("p1_t2", [128, 8, 32])
        p["qT_s"] = sb("p1_qTs", [96, 8, 128], BF16); p["kT_s"] = sb("p1_kTs", [96, 8, 128], BF16)
        p["ub"] = [sb("p1_ub%d" % i, [128, 256], BF16) for i in range(3)]
        p["ubf"] = sb("p1_ubf", [128, 256])
        p["pldT"] = sb("p1_pldT", [64, 4, 128], BF16); p["yT"] = sb("p1_yT", [64, 4, 128], BF16)
        p["QcT"] = [sb("p1_QcT%d" % i, [64, 4, 128], BF16) for i in range(2)]
        p["KcT"] = [sb("p1_KcT%d" % i, [64, 4, 128], BF16) for i in range(6)]
        p["Vc"] = [self.sb("p1_Vc%d" % i, [128, 4, 65], BF16) for i in range(6)]
        p["PTc"] = sb("p1_PTc", [128, 5, 4, 128], BF16)
        p["rd"] = sb("p1_rd", [128, 512]); p["bc_s"] = sb("p1_bcs", [64, 512]); p["ocT"] = sb("p1_ocT", [64, 4, 128], BF16)
        p["cst"] = sb("p1_cst", [128, 256]); p["cstb"] = sb("p1_cstb", [128, 256], BF16)
        for t in p["Vc"]:
            self.memset("pool", t, 1.0)
        self.memset("pool", p["va"], 1.0)

    def kv_build(self, P, tok0, t, KT, VA, ckvb, kro):
        p = self.p1
        self.tr(self.bkb()[:, 0:P], ckvb[0:P, :], P)
        ps = self.last_bkb
        self.cp("dve", p["ckvT"][:, 0:P], ps[:, 0:P])
        pk = self.bk2()
        for c in range(2):
            self.mm(pk[0:P, c * 512:(c + 1) * 512], p["ckvT"][:, 0:P], self.w_ukv[:, c * 512:(c + 1) * 512])
        kv = p["kv_s"]
        self.cp("act", kv[0:P, 0:512], pk[0:P, 0:512])
        self.cp("dve", kv[0:P, 512:1024], pk[0:P, 512:1024])
        kv3 = kv.re("p (h d) -> p h d", d=128)
        sq3 = p["sq"].re("p (h d) -> p h d", d=96)
        self.act(sq3[0:P, :, 0:64], kv3[0:P, :, 0:64], AF.Square)
        self.red("dve", p["st2"][0:P, 16:24], sq3[0:P, :, 0:64])
        self.rstd(p["st2"][0:P, 16:24], self.invn2[0:P, 16:24], p["tm2"][0:P, 16:24], p["rs2"][0:P, 16:24])
        gkn = self.grow[0:P, 256:320]
        ka = p["ka"]
        self.tt("dve", sq3[0:P, :, 0:64], kv3[0:P, :, 0:64], gkn.bc(1, [P, 8, 64]), ALU.mult)
        self.tt("dve", ka[0:P, :, 0:64], sq3[0:P, :, 0:64], p["rs2"][0:P, 16:24].bc(2, [P, 8, 64]), ALU.mult)
        self.cp("dve", ka[0:P, :, 64:96], kro[0:P, :].bc(1, [P, 8, 32]))
        self.cp("act", p["va"][0:P, :, 0:64], kv3[0:P, :, 64:128])
        pt = self.bkb()
        pt3 = pt.re("p (h t) -> p h t", t=128)
        for h in range(8):
            self.tr(pt3[0:96, h, 0:P], ka[0:P, h, :], P, mark=(h == 7))
        self.cp("act", p["kT_s"][:, :, 0:P], pt3[0:96, :, 0:P])
        self.S.dma(KT[:, :, tok0:tok0 + P].rearrange("h d t -> d h t"), p["kT_s"][:, :, 0:P])
        self.S.dma(VA[:, 0:P, t, :].rearrange("h p c -> p h c"), p["va"][0:P, :, :])

    def bkb(self):
        b = self.bk()
        self.last_bkb = b.bit(BF16)
        return self.last_bkb

    def p1_tile(self, l, seq, t):
        p = self.p1
        S = self.S
        P = seq["P"]
        tok0 = t * 128
        kt = seq["kt0"] + t
        xt = p["xt"][t % 2]
        rp = p["rp"][t % 3]
        QcT = p["QcT"][t % 2]
        S.dma(xt[0:P, :], seq["x"][tok0:tok0 + P, :])
        S.dma(rp[0:P, :], seq["rope"][tok0:tok0 + P, :])
        st0 = p["st0"]
        self.act(p["junk"][0:P, :], xt[0:P, :], AF.Square, accum=st0[0:P, 0:1])
        self.rstd(st0[0:P, 0:1], self.invD[0:P, :], st0[0:P, 1:2], st0[0:P, 2:3])
        self.act(p["hb"][0:P, :], xt[0:P, :], AF.Copy, scale=st0[0:P, 2:3])
        pt = self.bkb()
        pt3 = pt.re("p (k t) -> p k t", t=128)
        for k in range(8):
            self.tr(pt3[:, k, 0:P], p["hb"][0:P, k * 128:(k + 1) * 128], P, mark=(k == 7))
        hT = p["hT"]
        self.cp("dve", hT[:, :, 0:P], pt3[:, :, 0:P])
        yield
        zs = [p["zA"], p["zB"], p["zC"]]
        for c, wc in enumerate((512, 512, 416)):
            pz = self.bk()
            for k in range(8):
                self.mm(pz[0:P, 0:wc], hT[:, k, 0:P], self.w_tm[:, k, c * 512:c * 512 + wc], start=(k == 0), stop=(k == 7))
            self.cp("act" if c != 1 else "dve", zs[c][0:P, 0:wc], pz[0:P, 0:wc])
        zA, zB, zC = zs
        st1, rs1 = p["st1"], p["rs1"]
        sq = p["sq"]
        self.act(sq[0:P, 0:256], zA[0:P, 0:256], AF.Square, accum=st1[0:P, 0:1])
        self.act(sq[0:P, 256:384], zC[0:P, 256:384], AF.Square, accum=st1[0:P, 1:2])
        self.act(sq[0:P, 384:416], zC[0:P, 384:416], AF.Square, accum=st1[0:P, 2:3])
        self.act(sq[0:P, 0:512], zB[0:P, :], AF.Square)
        self.red("dve", st1[0:P, 3:11], sq[0:P, 0:512].re("p (h d) -> p h d", d=64))
        self.rstd(st1[0:P, :], self.invn1[0:P, :], p["tm1"][0:P, :], rs1[0:P, :])
        yield
        self.act(p["cq"][0:P, :], zA[0:P, 0:256], AF.Copy, scale=rs1[0:P, 0:1])
        ckv_s = p["ckv_s"]
        self.stt("dve", ckv_s[0:P, :], zC[0:P, 256:384], rs1[0:P, 1:2], self.grow[0:P, 0:128], ALU.mult, ALU.mult)
        S.dma(seq["o_ckv"][l][tok0:tok0 + P, :], ckv_s[0:P, :])
        self.cp("act", p["ckvb"][0:P, :], ckv_s[0:P, :])
        kr, kt1, kt2, kro = p["kr"], p["kt1"], p["kt2"], p["kro"]
        self.stt("dve", kr[0:P, :], zC[0:P, 384:416], rs1[0:P, 2:3], self.grow[0:P, 128:160], ALU.mult, ALU.mult)
        self.tt("dve", kt1[0:P, :], kr[0:P, :], rp[0:P, 0:32], ALU.mult)
        self.tt("dve", kt2[0:P, 0:16], kr[0:P, 16:32], rp[0:P, 32:48], ALU.mult)
        self.tt("dve", kt2[0:P, 16:32], kr[0:P, 0:16], rp[0:P, 48:64], ALU.mult)
        self.tt("dve", kro[0:P, :], kt1[0:P, :], kt2[0:P, :], ALU.add)
        S.dma(seq["o_kr"][l][tok0:tok0 + P, :], kro[0:P, :])
        ub = p["ub"][t % 3]
        ubp = p["ub"][(t - 1) % 3]
        self.cp("act", ub[0:P, :], zA[0:P, 256:512])
        if t == seq["nt"] - 1:
            S.dma(seq["o_pool"][l], zA[P - 15:P, 256:512])
        qkn, qkb = p["qkn"], p["qkb"]
        zB3 = zB.re("p (h d) -> p h d", d=64)
        qk3 = qkn.re("p (h d) -> p h d", d=64)
        self.tt("dve", qk3[0:P], zB3[0:P], self.grow[0:P, 320:832].re("p (h d) -> p h d", d=64), ALU.mult)
        self.tt("dve", qk3[0:P], qk3[0:P], rs1[0:P, 3:11].bc(2, [P, 8, 64]), ALU.mult)
        self.cp("act", qkb[0:P, :], qkn[0:P, :])
        lo = max(0, seq["nt"] * 128 - 512) if not seq["hist"] else 0
        if tok0 >= lo:
            S.dma(seq["o_cak"][l][tok0 - lo:tok0 - lo + P, :], qkn[0:P, 256:512])
            S.dma(seq["o_cav"][l][tok0 - lo:tok0 - lo + P, :], zC[0:P, 0:256])
        u_cur = seq["ca0"] + t
        slot = u_cur % 6
        self.cp("act", p["Vc"][slot][0:P, :, 0:64], zC[0:P, 0:256].re("p (h d) -> p h d", d=64))
        yield
        pt = self.bkb()
        pt3 = pt.re("p (k t) -> p k t", t=128)
        for k in range(2):
            self.tr(pt3[:, k, 0:P], p["cq"][0:P, k * 128:(k + 1) * 128], P, mark=(k == 1))
        self.cp("dve", p["cqT"][:, :, 0:P], pt3[:, 0:2, 0:P])
        pq = self.bk2()
        for c, (c0, c1) in enumerate(((0, 512), (512, 768))):
            for k in range(2):
                self.mm(pq[0:P, c * 512:c * 512 + c1 - c0], p["cqT"][:, k, 0:P], self.w_uq[:, k, c0:c1], start=(k == 0), stop=(k == 1))
        q_s = p["q_s"]
        self.cp("act", q_s[0:P, 0:512], pq[0:P, 0:512])
        self.cp("dve", q_s[0:P, 512:768], pq[0:P, 512:768])
        self.act(sq[0:P, 0:768], q_s[0:P, :], AF.Square)
        sq3 = sq.re("p (h d) -> p h d", d=96)
        st2, rs2 = p["st2"], p["rs2"]
        self.red("dve", st2[0:P, 0:8], sq3[0:P, :, 0:64])
        self.red("dve", st2[0:P, 8:16], sq3[0:P, :, 64:96])
        self.rstd(st2[0:P, 0:16], self.invn2[0:P, 0:16], p["tm2"][0:P, 0:16], rs2[0:P, 0:16])
        q3 = q_s.re("p (h d) -> p h d", d=96)
        gq96 = self.grow[0:P, 160:256]
        self.tt("dve", q3[0:P, :, :], q3[0:P, :, :], gq96.bc(1, [P, 8, 96]), ALU.mult)
        qa = p["qa"]
        self.tt("dve", qa[0:P, :, 0:64], q3[0:P, :, 0:64], rs2[0:P, 0:8].bc(2, [P, 8, 64]), ALU.mult)
        trp, t1, t2 = p["trp"], p["t1"], p["t2"]
        self.tt("dve", trp[0:P], q3[0:P, :, 64:96], rs2[0:P, 8:16].bc(2, [P, 8, 32]), ALU.mult)
        self.tt("dve", t1[0:P], trp[0:P], rp[0:P, 0:32].bc(1, [P, 8, 32]), ALU.mult)
        self.tt("dve", t2[0:P, :, 0:16], trp[0:P, :, 16:32], rp[0:P, 32:48].bc(1, [P, 8, 16]), ALU.mult)
        self.tt("dve", t2[0:P, :, 16:32], trp[0:P, :, 0:16], rp[0:P, 48:64].bc(1, [P, 8, 16]), ALU.mult)
        self.tt("dve", qa[0:P, :, 64:96], t1[0:P], t2[0:P], ALU.add)
        pt = self.bkb()
        pt3 = pt.re("p (h t) -> p h t", t=128)
        for h in range(8):
            self.tr(pt3[0:96, h, 0:P], qa[0:P, h, :], P, mark=(h == 7))
        self.cp("act", p["qT_s"][:, :, 0:P], pt3[0:96, :, 0:P])
        S.dma(seq["QT"][:, :, tok0:tok0 + P].rearrange("h d t -> d h t"), p["qT_s"][:, :, 0:P])
        yield
        self.kv_build(P, kt * 128, kt, seq["KT"], seq["VA"], p["ckvb"], kro)
        yield
        first = (t == 0 and not seq["hist"])
        pp_ = self.bk()
        pp3 = pp_.re("p (g t) -> p g t", t=128)
        for g in range(4):
            bc_ = self.bands[0:P, (8 + g) if first else g, 0:P]
            self.mm(pp3[0:64, g, 0:P], ub[0:P, g * 64:(g + 1) * 64], bc_, start=True, stop=first)
            if not first:
                self.mm(pp3[0:64, g, 0:P], ubp[:, g * 64:(g + 1) * 64], self.bands[:, 4 + g, 0:P], start=False, stop=True)
        self.cp("dve", p["pldT"][:, :, 0:P], pp3[0:64, :, 0:P])
        py = self.bk()
        py3 = py.re("p (g t) -> p g t", t=128)
        for g in range(4):
            self.mm(py3[0:64, g, 0:P], self.pool_w[:, g, :], p["pldT"][:, g, 0:P])
        self.tt("dve", p["yT"][:, :, 0:P], py3[0:64, :, 0:P], self.plsc.bc(2, [64, 4, P]), ALU.mult)
        S.dma(seq["OT"][512:768, tok0:tok0 + P].rearrange("(g d) t -> d g t", d=64), p["yT"][:, :, 0:P])
        yield
        pt = self.bkb()
        pt3 = pt.re("p (h t) -> p h t", t=128)
        for h in range(8):
            self.tr(pt3[0:64, h, 0:P], qkb[0:P, h * 64:(h + 1) * 64], P, mark=(h == 7))
        self.cp("act", QcT[:, :, 0:P], pt3[0:64, 0:4, 0:P])
        self.cp("act", p["KcT"][slot][:, :, 0:P], pt3[0:64, 4:8, 0:P])
        yield
        us = [u for u in range(u_cur - 4, u_cur + 1) if u >= 0]
        PTc = p["PTc"]
        for ui, u in enumerate(us):
            nk = P if u == u_cur else 128
            du = u - u_cur + 4
            pS = self.bk()
            pS3 = pS.re("p (h t) -> p h t", t=128)
            for h in range(4):
                self.mm(pS3[0:nk, h, 0:P], p["KcT"][u % 6][:, h, 0:nk], QcT[:, h, 0:P], start=True, stop=False, mark=False)
                self.mm(pS3[0:nk, h, 0:P], self.identb[0:nk, 0:nk], self.biasT[0:nk, h * 5 + du, 0:P], start=False, stop=True, mark=(h == 3))
            self.act(PTc[0:nk, ui, :, 0:P], pS3[0:nk, :, 0:P], AF.Exp)
        po = self.bk()
        po3 = po.re("p (h t) -> p h t", t=128)
        for h in range(4):
            for ui, u in enumerate(us):
                nk = P if u == u_cur else 128
                self.mm(po3[0:65, h, 0:P], p["Vc"][u % 6][0:nk, h, :], PTc[0:nk, ui, h, 0:P], start=(ui == 0), stop=(ui == len(us) - 1),
                        mark=(ui == len(us) - 1 and h == 3))
        rd = p["rd"]
        rd3 = rd.re("p (h t) -> p h t", t=128)
        self.act(rd3[64:65, :, 0:P], po3[64:65, :, 0:P], AF.Ln)
        self.act(rd3[64:65, :, 0:P], rd3[64:65, :, 0:P], AF.Exp, scale=-1.0)
        pb = self.bk()
        pb3 = pb.re("p (h t) -> p h t", t=128)
        for h in range(4):
            self.mm(pb3[0:64, h, 0:P], self.ones_f[64:65, 0:64], rd3[64:65, h, 0:P], mark=(h == 3))
        bcs3 = p["bc_s"].re("p (h t) -> p h t", t=128)
        self.cp("act", bcs3[:, :, 0:P], pb3[0:64, :, 0:P])
        self.tt("dve", p["ocT"][:, :, 0:P], po3[0:64, :, 0:P], bcs3[:, :, 0:P], ALU.mult)
        S.dma(seq["OT"][768:1024, tok0:tok0 + P].rearrange("(h d) t -> d h t", d=64), p["ocT"][:, :, 0:P])

    def run_interleaved(self, gens, lag):
        active = []
        idx = 0
        while idx < len(gens) or active:
            if idx < len(gens) and (not active or active[-1][1] >= lag):
                active.append([gens[idx], 0])
                idx += 1
            for a in list(active):
                try:
                    next(a[0])
                    a[1] += 1
                except StopIteration:
                    active.remove(a)

    def p1_hist(self, l, seq):
        p = self.p1
        S = self.S
        d = self.din
        nh = self.PAST // 128
        for t in range(nh):
            S.dma(p["ckv_s"], d["c_ckv"][l][t * 128:(t + 1) * 128, :])
            S.dma(p["kro"], d["c_kr"][l][t * 128:(t + 1) * 128, :])
            self.cp("act", p["ckvb"], p["ckv_s"])
            self.kv_build(128, t * 128, t, seq["KT"], seq["VA"], p["ckvb"], p["kro"])
        for u in range(4):
            S.dma(p["cst"], d["c_cak"][l][u * 128:(u + 1) * 128, :])
            self.cp("act", p["cstb"], p["cst"])
            pt = self.bkb()
            pt3 = pt.re("p (h t) -> p h t", t=128)
            for h in range(4):
                self.tr(pt3[0:64, h, :], p["cstb"][:, h * 64:(h + 1) * 64], 128, mark=(h == 3))
            self.cp("act", p["KcT"][u % 6], pt3[0:64, 0:4, :])
            S.dma(p["cst"], d["c_cav"][l][u * 128:(u + 1) * 128, :])
            self.cp("dve", p["Vc"][u % 6][:, :, 0:64], p["cst"].re("p (h d) -> p h d", d=64))
        self.memset("dve", p["ubf"], 0.0)
        S.dma(p["ubf"][113:128, :], d["st_pool"][l])
        self.cp("dve", p["ub"][2], p["ubf"])

    def alloc_p2(self):
        self.p2 = p = {}
        p["KT"] = [self.av(0, 2, [96, 8192]), self.av(5, 2, [96, 8192])]
        p["QT"] = [self.av(2, 2, [96, 8192]), self.av(7, 2, [96, 8192])]
        p["VA"] = [self.av(4, 1, [128, 64, 65]), self.av(9, 1, [128, 64, 65])]
        p["PT"] = [self.av(10, 1, [128, 512], off=i * 1024) for i in range(4)]
        for i, t in enumerate(p["PT"]):
            t.bufs = [self.slab[10]] if False else [Buf("p2pt%d" % i)]
        def sb(name, shape, dt=F32):
            nb = int(np.prod(shape[1:])) * (4 if dt == F32 else 2)
            return self.ab(name, shape, dt) if nb >= 512 else self.sb(name, shape, dt)
        self.ab_cur = 0
        p["rd"] = sb("p2_rd", [128, 512]); p["bc_s"] = [sb("p2_bcs%d" % i, [64, 512]) for i in range(2)]
        p["oT"] = [sb("p2_oT%d" % i, [64, 512], BF16) for i in range(2)]
        self.pt_i = 0
        self.p2_o = 0
        self.p2_s = 0
        self.p2slabdep = self.slab[10]

    def p2_seq(self, seq, causal):
        p = self.p2
        S = self.S
        NQ = seq["nq"]
        NKT = seq["nkt_total"]
        nk_last = seq["nk_last"]
        sc = 96 ** -0.5
        for i in range(4):
            self.memset("pool", T(p["PT"][i].ap, p["PT"][i].bufs + [self.p2slabdep]), 0.0)
        NK = (NKT - 1) * 128 + nk_last

        def load_head(h):
            s = h % 2
            KT, QT, VA = p["KT"][s], p["QT"][s], p["VA"][s]
            S.dma(KT[:, 0:NK], seq["KT"][h][:, 0:NK])
            S.dma(QT[:, 0:NQ], seq["QT"][h][:, 0:NQ])
            if nk_last == 128:
                S.dma(VA[:, 0:NKT, :], seq["VA"][h][:, 0:NKT, :])
            else:
                S.dma(VA[:, 0:NKT - 1, :], seq["VA"][h][:, 0:NKT - 1, :])
                S.dma(VA[0:nk_last, NKT - 1, :], seq["VA"][h][0:nk_last, NKT - 1, :])

        blocks = []
        for h in range(NH):
            s = h % 2
            KT, QT, VA = p["KT"][s], p["QT"][s], p["VA"][s]
            gi = 0
            first_of_head = True
            for q0 in range(0, NQ, 512):
                W = min(512, NQ - q0)
                kts = list(range(0, (q0 + W) // 128)) if causal else list(range(NKT))
                grp = dict(pO=None)
                for ki, kt in enumerate(kts):
                    nk = nk_last if kt == NKT - 1 else 128
                    diag = causal and kt * 128 >= q0
                    c0 = kt * 128 - q0 if diag else 0
                    blk = {}

                    def A(h=h, kt=kt, nk=nk, diag=diag, c0=c0, W=W, q0=q0, KT=KT, QT=QT, blk=blk, pre=first_of_head):
                        pS = self.bank(2 + self.p2_s % 5)
                        self.p2_s += 1
                        self.mm(pS[0:nk, c0:W], KT[:, kt * 128:kt * 128 + nk], QT[:, q0 + c0:q0 + W])
                        PT = p["PT"][self.pt_i]
                        self.pt_i = (self.pt_i + 1) % 4
                        self.act(PT[0:nk, c0:W], pS[0:nk, c0:W], AF.Exp, scale=sc)
                        if diag:
                            self.memset("pool", PT[64:128, c0:c0 + 64], 0.0)
                        blk["PT"] = PT

                    def B(h=h, kt=kt, nk=nk, c0=c0, W=W, q0=q0, VA=VA, blk=blk, ki=ki, nkts=len(kts), grp=grp, gi=gi, pre=first_of_head):
                        if pre and h + 1 < NH:
                            load_head(h + 1)
                        if ki == 0:
                            grp["pO"] = self.bank(self.p2_o % 2)
                            self.p2_o += 1
                        pO = grp["pO"]
                        self.mm(pO[0:65, c0:W], VA[0:nk, kt, :], blk["PT"][0:nk, c0:W], start=(ki == 0), stop=(ki == nkts - 1), mark=True)
                        if ki == nkts - 1:
                            rd = p["rd"]
                            self.act(rd[64:65, 0:W], pO[64:65, 0:W], AF.Ln)
                            self.act(rd[64:65, 0:W], rd[64:65, 0:W], AF.Exp, scale=-1.0)
                            pb = self.bank(7)
                            self.mm(pb[0:64, 0:W], self.ones_f[64:65, 0:64], rd[64:65, 0:W])
                            bcs = p["bc_s"][gi % 2]
                            oT = p["oT"][gi % 2]
                            self.cp("act", bcs[:, 0:W], pb[0:64, 0:W])
                            self.tt("dve", oT[:, 0:W], pO[0:64, 0:W], bcs[:, 0:W], ALU.mult)
                            S.dma(seq["OT"][h * 64:(h + 1) * 64, q0:q0 + W], oT[:, 0:W])

                    blocks.append((A, B))
                    first_of_head = False
                gi += 1
        LA = 3
        load_head(0)
        for i in range(len(blocks) + LA):
            if i < len(blocks):
                blocks[i][0]()
            if i - LA >= 0:
                blocks[i - LA][1]()
        for i in range(4):
            self.memset("pool", T(p["PT"][i].ap, p["PT"][i].bufs + [self.p2slabdep]), 0.0)

    def alloc_p3(self):
        def sb(name, shape, dt=F32):
            nb = int(np.prod(shape[1:])) * (4 if dt == F32 else 2)
            return self.ab(name, shape, dt) if nb >= 512 else self.sb(name, shape, dt)
        self.ab_cur = 0
        self.p3 = p = {}
        G = 256
        p["xt"] = [sb("p3_xt%d" % i, [128, 1024]) for i in range(2)]
        p["xr"] = [sb("p3_xr%d" % i, [128, 1024]) for i in range(2)]
        p["hb"] = sb("p3_hb", [128, 1024], BF16); p["junk"] = sb("p3_junk", [128, 1024], BF16)
        p["st"] = sb("p3_st", [128, 8])
        p["hT"] = [sb("p3_hT%d" % i, [128, 8, G], BF16) for i in range(2)]
        p["oT"] = sb("p3_oT", [128, 8, G], BF16)
        p["ed"] = [sb("p3_ed%d" % i, [128, G]) for i in range(4)]
        p["tt"] = [sb("p3_tt%d" % i, [128, G]) for i in range(4)]
        p["mT"] = sb("p3_mT", [128, 8, G], BF16)
        p["hxT"] = sb("p3_hxT", [128, 8, 128], BF16)
        p["pt"] = [sb("p3_pt%d" % i, [128, 256]) for i in range(2)]; p["pb"] = sb("p3_pb", [128, 256], BF16); p["pT"] = sb("p3_pT", [128, 2, 128], BF16)
        p["sig"] = sb("p3_sig", [128, 1024])
        self.ed_i = 0
        self.tt_i = 0
        self.gd_i = 0

    def p3_head(self, l, seq, g0, tiles, gi):
        p = self.p3
        S = self.S
        hT, st = p["hT"][gi % 2], p["st"]
        for j, (tok0, P) in enumerate(tiles):
            S.dma(p["xt"][j % 2][0:P, :], seq["x"][tok0:tok0 + P, :])
        for j, (tok0, P) in enumerate(tiles):
            xt = p["xt"][j % 2]
            jo = tok0 - g0
            self.act(p["junk"][0:P, :], xt[0:P, :], AF.Square, accum=st[0:P, 0:1])
            self.rstd(st[0:P, 0:1], self.invD[0:P, :], st[0:P, 1:2], st[0:P, 2:3])
            self.act(p["hb"][0:P, :], xt[0:P, :], AF.Copy, scale=st[0:P, 2:3])
            pt = self.bkb()
            pt3 = pt.re("p (k t) -> p k t", t=128)
            for k in range(8):
                self.tr(pt3[:, k, 0:P], p["hb"][0:P, k * 128:(k + 1) * 128], P, mark=(k == 7))
            self.cp("dve", hT[:, :, jo:jo + P], pt3[:, :, 0:P])

    def p3_group(self, l, seq, g0, tiles, gi=0, hook=None):
        p = self.p3
        S = self.S
        W = sum(P for _, P in tiles)
        hT, oT, mT, st = p["hT"][gi % 2], p["oT"], p["mT"], p["st"]
        S.dma(oT[:, :, 0:W], seq["OT"][:, g0:g0 + W].rearrange("(c p) t -> p c t", p=128))
        for j, (tok0, P) in enumerate(tiles):
            S.dma(p["xr"][j % 2][0:P, :], seq["x"][tok0:tok0 + P, :])
            S.dma(p["pt"][j % 2][0:P, :], seq["p"][l][tok0:tok0 + P, :])

        def ntt():
            t = p["tt"][self.tt_i]
            self.tt_i = (self.tt_i + 1) % 4
            return t

        def gate_items(items):
            for i in range(0, len(items), 2):
                grp = items[i:i + 2]
                st_ = []
                for it in grp:
                    ppd = it["pre"]() if it.get("pre") else None
                    pg = self.bk()
                    for k in range(8):
                        self.mm(pg[:, 0:W], self.w_fm[:, k, it["col0"]:it["col0"] + 128], hT[:, k, 0:W], start=(k == 0), stop=(k == 7))
                    ed = p["ed"][self.ed_i]
                    self.ed_i = (self.ed_i + 1) % 4
                    st_.append((pg, ppd, ed))
                for (pg, ppd, ed) in st_:
                    self.act(ed[:, 0:W], pg[:, 0:W], AF.Exp, scale=-1.0)
                for (pg, ppd, ed) in st_:
                    self.act(ed[:, 0:W], ed[:, 0:W], AF.Ln, bias=self.one_t[:, :])
                for (pg, ppd, ed) in st_:
                    self.act(ed[:, 0:W], ed[:, 0:W], AF.Exp, scale=-1.0)
                for it, (pg, ppd, ed) in zip(grp, st_):
                    it["post"](pg, ppd, ed)

        def silu_post(c):
            def f(pg, ppd, ed):
                t = ntt()
                self.tt("dve", t[:, 0:W], pg[:, 0:W], ed[:, 0:W], ALU.mult)
                self.tt("dve", oT[:, c, 0:W], t[:, 0:W], oT[:, c, 0:W], ALU.mult)
            return f

        gate_items([dict(col0=c * 128, post=silu_post(c)) for c in range(8)])
        accs = {}

        def mk_pre(oc, kc0, nkc):
            def f():
                ppd = self.bk()
                for kc in range(nkc):
                    self.mm(ppd[:, 0:W], self.w_o[:, kc0 + kc, oc * 128:(oc + 1) * 128], oT[:, kc0 + kc, 0:W], start=(kc == 0), stop=(kc == nkc - 1))
                return ppd
            return f

        def mk_post(oc, br):
            def f(pg, ppd, ed):
                t = ntt()
                self.tt("dve", t[:, 0:W], ppd[:, 0:W], ed[:, 0:W], ALU.mult)
                if br == 0:
                    accs[oc] = t
                elif br == 1:
                    self.tt("dve", accs[oc][:, 0:W], accs[oc][:, 0:W], t[:, 0:W], ALU.add)
                else:
                    self.tt("dve", mT[:, oc, 0:W], accs[oc][:, 0:W], t[:, 0:W], ALU.add)
            return f

        items = []
        for oc in range(8):
            for br, (kc0, nkc) in enumerate(((0, 4), (4, 2), (6, 2))):
                items.append(dict(col0=1024 + br * 1024 + oc * 128, pre=mk_pre(oc, kc0, nkc), post=mk_post(oc, br)))
        gate_items(items)
        if hook is not None:
            hook()
        for j, (tok0, P) in enumerate(tiles):
            jo = tok0 - g0
            xr, sig = p["xr"][j % 2], p["sig"]
            x2 = xr
            ptl = p["pt"][j % 2]
            px = self.bk2()
            for c in range(2):
                for k in range(8):
                    self.mm(px[0:P, c * 512:(c + 1) * 512], mT[:, k, jo:jo + P], self.w_out[:, k, c * 512:(c + 1) * 512], start=(k == 0), stop=(k == 7))
            for c in range(2):
                self.tt("dve", x2[0:P, c * 512:(c + 1) * 512], px[0:P, c * 512:(c + 1) * 512], xr[0:P, c * 512:(c + 1) * 512], ALU.add)
            self.act(p["junk"][0:P, :], x2[0:P, :], AF.Square, accum=st[0:P, 4:5])
            self.rstd(st[0:P, 4:5], self.invD[0:P, :], st[0:P, 5:6], st[0:P, 6:7])
            self.act(p["hb"][0:P, :], x2[0:P, :], AF.Copy, scale=st[0:P, 6:7])
            pt = self.bkb()
            pt3 = pt.re("p (k t) -> p k t", t=128)
            for k in range(8):
                self.tr(pt3[:, k, 0:P], p["hb"][0:P, k * 128:(k + 1) * 128], P, mark=(k == 7))
            self.cp("dve", p["hxT"][:, :, 0:P], pt3[:, :, 0:P])
            self.cp("act", p["pb"][0:P, :], ptl[0:P, :])
            pt = self.bkb()
            pt3 = pt.re("p (k t) -> p k t", t=128)
            for k in range(2):
                self.tr(pt3[:, k, 0:P], p["pb"][0:P, k * 128:(k + 1) * 128], P, mark=(k == 1))
            self.cp("dve", p["pT"][:, :, 0:P], pt3[:, 0:2, 0:P])
            pgt = self.bk2()
            for c in range(2):
                for k in range(8):
                    self.mm(pgt[0:P, c * 512:(c + 1) * 512], p["hxT"][:, k, 0:P], self.w_pg[:, k, c * 512:(c + 1) * 512], start=(k == 0), stop=(k == 7))
            for c in range(2):
                self.act(sig[0:P, c * 512:(c + 1) * 512], pgt[0:P, c * 512:(c + 1) * 512], AF.Exp, scale=-1.0)
            for c in range(2):
                self.act(sig[0:P, c * 512:(c + 1) * 512], sig[0:P, c * 512:(c + 1) * 512], AF.Ln, bias=self.one_t[0:P, :])
            for c in range(2):
                self.act(sig[0:P, c * 512:(c + 1) * 512], sig[0:P, c * 512:(c + 1) * 512], AF.Exp, scale=-1.0)
            ppp = self.bk2()
            for c in range(2):
                for k in range(2):
                    self.mm(ppp[0:P, c * 512:(c + 1) * 512], p["pT"][:, k, 0:P], self.w_pp[:, k, c * 512:(c + 1) * 512], start=(k == 0), stop=(k == 1))
            for c in range(2):
                self.tt("dve", sig[0:P, c * 512:(c + 1) * 512], ppp[0:P, c * 512:(c + 1) * 512], sig[0:P, c * 512:(c + 1) * 512], ALU.mult)
            self.tt("dve", x2[0:P, :], x2[0:P, :], sig[0:P, :], ALU.add)
            S.dma(seq["xo"][tok0:tok0 + P, :], x2[0:P, :])

    def build(self):
        S = self.S
        d, o = self.din, self.dout
        LP, LS, PAST = self.LP, self.LS, self.PAST
        self.phase_log = []
        plog = lambda name: self.phase_log.append((name, dict(S.nm)))
        self.plog = plog
        self.setup_consts()
        self.alloc_p1()
        self.alloc_p2()
        self.alloc_p3()
        import os
        STOP = int(os.environ.get("KSTOP", "99"))
        for l in range(NL):
            if STOP < 99 and l > 0:
                break
            seqP = dict(P=128, nt=LP // 128, kt0=0, ca0=0, hist=False,
                        x=(d["xp"] if l == 0 else self.x1p), xo=(self.x1p if l < NL - 1 else o["y_p"]),
                        rope=d["ropeP"], p=d["pp"], QT=self.QTp, KT=self.KTp, VA=self.VAp, OT=self.OTp,
                        o_ckv=o["ckv_p"], o_kr=o["kr_p"], o_cak=o["cak_p"], o_cav=o["cav_p"], o_pool=[o["pool_p"][i] for i in range(NL)],
                        nq=LP, nkt_total=LP // 128, nk_last=128)
            seqS = dict(P=LS, nt=1, kt0=PAST // 128, ca0=4, hist=True,
                        x=(d["xs"] if l == 0 else self.x1s), xo=(self.x1s if l < NL - 1 else o["y_s"]),
                        rope=d["ropeS"], p=d["ps"], QT=self.QTs, KT=self.KTs, VA=self.VAs, OT=self.OTs,
                        o_ckv=o["ckv_s"], o_kr=o["kr_s"], o_cak=o["cak_s"], o_cav=o["cav_s"], o_pool=[o["pool_s"][i] for i in range(NL)],
                        nq=LS, nkt_total=PAST // 128 + 1, nk_last=LS)
            S.dma_fence()
            plog("L%d start" % l)
            self.load_layer_small(l)
            if STOP <= 0:
                break
            self.load_p1_weights(l)
            if STOP <= 1:
                break
            plog("L%d p1w done" % l)
            self.run_interleaved([self.p1_tile(l, seqP, t) for t in range(seqP["nt"])], 4)
            plog("L%d p1 prompt done" % l)
            if STOP <= 2:
                break
            self.p1_hist(l, seqS)
            self.run_interleaved([self.p1_tile(l, seqS, 0)], 4)
            S.dma_fence()
            if STOP <= 3:
                break
            plog("L%d p1 sample done" % l)
            self.p2_seq(seqP, True)
            plog("L%d p2 prompt done" % l)
            self.p2_seq(seqS, False)
            plog("L%d p2 sample done" % l)
            S.dma_fence()
            if STOP <= 4:
                break
            self.load_p3_weights(l)
            if STOP <= 5:
                break
            plog("L%d p3w done" % l)
            G = 256
            groups = [(g0, [(g0 + j * 128, 128) for j in range(G // 128)]) for g0 in range(0, LP, G)]
            groups.append(None)
            self.p3_head(l, seqP, groups[0][0], groups[0][1], 0)
            for gi in range(len(groups) - 1):
                g0, tl = groups[gi]
                if groups[gi + 1] is not None:
                    nxt = groups[gi + 1]
                    hook = (lambda nxt=nxt, gi=gi: self.p3_head(l, seqP, nxt[0], nxt[1], gi + 1))
                else:
                    hook = (lambda gi=gi: self.p3_head(l, seqS, 0, [(0, LS)], gi + 1))
                self.p3_group(l, seqP, g0, tl, gi, hook)
            plog("L%d p3 prompt done" % l)
            self.p3_group(l, seqS, 0, [(0, LS)], len(groups) - 1, None)
            plog("L%d p3 sample done" % l)
        S.dma_fence()
        nc = self.nc
        with nc.Block() as block:
            @block.tensor
            def _(e):
                S.replay("pe", e)

            @block.scalar
            def _(e):
                S.replay("act", e)

            @block.vector
            def _(e):
                S.replay("dve", e)

            @block.gpsimd
            def _(e):
                S.replay("pool", e)

            @block.sync
            def _(e):
                S.replay("sp", e)
        return nc


def _rope_table(pos):
    half = 16
    inv = 10000.0 ** (-np.arange(half, dtype=np.float64) / half)
    ang = (pos.astype(np.float32)[:, None] * inv.astype(np.float32)[None, :]).astype(np.float32)
    c = np.cos(ang.astype(np.float64)); s = np.sin(ang.astype(np.float64))
    return np.concatenate([c, c, -s, s], axis=1).astype(np.float32)


def _bands():
    out = np.zeros((3, 4, 128, 128), np.float32)
    i = np.arange(128)[:, None]; j = np.arange(128)[None, :]
    for g, w in enumerate((2, 4, 8, 16)):
        inw = ((j - i) >= 0) & ((j - i) < w)
        out[0, g] = inw / w - (i == j)
        out[1, g] = (((j + 128 - i) >= 0) & ((j + 128 - i) < w)) / w
        cnt = np.minimum(j + 1, w)
        out[2, g] = inw / cnt - (i == j)
    return np.ascontiguousarray(out.reshape(12, 128, 128).transpose(1, 0, 2))


def _bias_idx():
    idx = np.zeros((5, 128, 128), np.int64); msk = np.zeros((5, 128, 128), bool)
    r = np.arange(128)[:, None]; c = np.arange(128)[None, :]
    for du in range(5):
        rel = (du - 4) * 128 + r - c
        idx[du] = np.clip(rel, -128, 128) + 128
        kc = ((du - 4) * 128 + r) // 64
        qc = c // 64
        msk[du] = (kc > qc) | (kc < qc - 8)
    return idx, msk


def _prep(inp, c, LP, PAST, LS):
    f = np.float32
    m = {}
    m["xp"] = inp["x_prompt"][c]; m["xs"] = inp["x_sample"][c]
    m["c_ckv"] = inp["cache_mla_ckv"][:, c]; m["c_kr"] = inp["cache_mla_krope"][:, c]
    m["c_cak"] = inp["cache_ca_k"][:, c].reshape(NL, 512, 256); m["c_cav"] = inp["cache_ca_v"][:, c].reshape(NL, 512, 256)
    m["st_pool"] = inp["state_pool"][:, c]
    m["pp"] = inp["p_prompt"][:, c]; m["ps"] = inp["p_sample"][:, c]
    return {k: np.ascontiguousarray(v, dtype=f) for k, v in m.items()}


def _shared(inp, LP, PAST, LS):
    f = np.float32
    m = {}
    for k_, n in (("w_in", "w_in"), ("w_uq", "mla_w_uq"), ("w_ukv", "mla_w_ukv"), ("w_o_mla", "w_o_mla"), ("pool_w", "pool_w"),
                  ("w_o_pool", "w_o_pool"), ("w_o_ca", "w_o_ca"), ("w_out", "w_out"), ("w_pg", "w_ple_gate"), ("w_pp", "w_ple_proj")):
        m[k_] = inp[n]
    gcol = np.concatenate([inp["norm_in"].reshape(NL, 8, 128).transpose(0, 2, 1), inp["ple_norm"].reshape(NL, 8, 128).transpose(0, 2, 1),
                           inp["mla_q_norm"].reshape(NL, 2, 128).transpose(0, 2, 1)], axis=2)
    m["gcol"] = gcol
    caq = np.tile(inp["ca_qn"][:, None, :], (1, 4, 1)).reshape(NL, 256)
    cak = np.tile(inp["ca_kn"][:, None, :], (1, 4, 1)).reshape(NL, 256)
    m["grow"] = np.concatenate([inp["mla_kv_norm"], inp["mla_kn_rope"], inp["mla_qn_nope"], inp["mla_qn_rope"], inp["mla_kn_nope"],
                                caq, cak], axis=1)[:, None, :]
    m["plsc"] = inp["pool_scale"].reshape(NL, 4, 64).transpose(0, 2, 1)
    idx, msk = _bias_idx()
    tab = inp["ca_rel_bias"]
    bt = tab[:, :, idx]
    bt = np.where(msk[None, None], f(-30000.0), bt)
    m["biasT"] = bt.transpose(0, 3, 1, 2, 4).reshape(NL, 128, 20, 128)
    m["ident"] = np.eye(128, dtype=f)
    m["bands"] = _bands()
    m["ropeP"] = _rope_table(np.arange(LP)); m["ropeS"] = _rope_table(PAST + np.arange(LS))
    return {k: np.ascontiguousarray(v, dtype=f) for k, v in m.items()}


_CACHE = {}


def run(inp, ncores, LP, PAST, LS=64):
    key = (LP, PAST, LS)
    if key not in _CACHE:
        _CACHE[key] = K(LP, PAST, LS).build()
    nc = _CACHE[key]
    sh = _shared(inp, LP, PAST, LS)
    in_maps = []
    for c in range(ncores):
        mm_ = dict(sh)
        mm_.update(_prep(inp, c, LP, PAST, LS))
        in_maps.append(mm_)
    res = run_bass_kernel_spmd(nc, in_maps, core_ids=list(range(ncores)))
    R = res.results
    st = lambda k, ax: np.stack([r[k] for r in R], axis=ax)
    y_p = st("y_p", 0); y_s = st("y_s", 0)
    outs = (y_p, y_s, st("ckv_p", 1), st("kr_p", 1),
            st("cak_p", 1).reshape(NL, ncores, -1, 4, 64), st("cav_p", 1).reshape(NL, ncores, -1, 4, 64), st("pool_p", 1),
            st("ckv_s", 1), st("kr_s", 1), st("cak_s", 1).reshape(NL, ncores, -1, 4, 64), st("cav_s", 1).reshape(NL, ncores, -1, 4, 64),
            st("pool_s", 1))
    return tuple(np.ascontiguousarray(o, dtype=np.float32) for o in outs)


def kernel(**inputs):
    inp = {k: np.asarray(v) for k, v in inputs.items()}
    return run(inp, 8, 8192, 4096, 64)
```

```python
import numpy as np
import concourse.bass as bass
import concourse.mybir as mybir
from concourse.bass_utils import run_bass_kernel_spmd

F32 = mybir.dt.float32
BF16 = mybir.dt.bfloat16
AF = mybir.ActivationFunctionType
ALU = mybir.AluOpType
AX = mybir.AxisListType

D = 1024
EPS = 1e-6
NL = 2
NH = 8
CA_H = 4
SLAB = 8320
NSLAB = 14
NSLABB = 56


class Buf:
    def __init__(self, name):
        self.name = name
        self.w = None
        self.r = {}
        self.dsem = None
        self.dcnt = 0


class T:
    def __init__(self, ap, bufs):
        self.ap = ap
        self.bufs = list(bufs)

    def __getitem__(self, k):
        return T(self.ap[k], self.bufs)

    def re(self, s, **kw):
        return T(self.ap.rearrange(s, **kw), self.bufs)

    def bc(self, axis, shape):
        return T(self.ap.unsqueeze(axis).to_broadcast(shape), self.bufs)

    def bit(self, dt):
        return T(self.ap.bitcast(dt), self.bufs)


def _ap(x):
    return x.ap if isinstance(x, T) else x


class Sched:
    ENG = ["pe", "act", "dve", "pool", "sp"]

    def __init__(self, nc):
        self.nc = nc
        self.ops = {e: [] for e in self.ENG}
        self.nm = {e: 0 for e in self.ENG}
        self.sem = {e: nc.alloc_semaphore("s_" + e) for e in self.ENG}
        self.waited = {e: {} for e in self.ENG}
        self.pend = {e: ([], []) for e in self.ENG}
        self.dmarecs = {}
        self.nsem = 5

    def _need(self, e, rec):
        if rec is None:
            return
        key, sem, val = rec
        if key == "pe" and e == "pe":
            return
        if self.waited[e].get(key, 0) >= val:
            return
        self.waited[e][key] = val
        self.ops[e].append(("wait", sem, val))

    def op(self, e, fn, r=(), w=(), mark=True):
        rb = [b for t in r if isinstance(t, T) for b in t.bufs]
        wb = [b for t in w if isinstance(t, T) for b in t.bufs]
        for b in rb:
            self._need(e, b.w)
        for b in wb:
            self._need(e, b.w)
            for rec in list(b.r.values()):
                self._need(e, rec)
        pr, pw = self.pend[e]
        if mark:
            self.nm[e] += 1
            rec = (e, self.sem[e], self.nm[e])
            self.ops[e].append(("op", fn, self.sem[e]))
            for b in pw + wb:
                b.w = rec
                b.r = {}
            for b in pr + rb:
                if b.w is not rec:
                    b.r[e] = rec
            self.pend[e] = ([], [])
        else:
            self.ops[e].append(("op", fn, None))
            pr.extend(rb)
            pw.extend(wb)

    def dma(self, out, in_, q=None):
        e = q if q is not None else ("sp" if isinstance(out, T) else "pool")
        sb = out if isinstance(out, T) else in_
        rb = in_.bufs if isinstance(in_, T) else []
        wb = out.bufs if isinstance(out, T) else []
        for b in rb:
            self._need(e, b.w)
        for b in wb:
            self._need(e, b.w)
            for rec in list(b.r.values()):
                self._need(e, rec)
        b0 = sb.bufs[0]
        if b0.dsem is None:
            b0.dsem = {}
            b0.dcnt = {}
        if e not in b0.dsem:
            b0.dsem[e] = self.nc.alloc_semaphore("d%s_%s" % (e, b0.name))
            b0.dcnt[e] = 0
            self.nsem += 1
        b0.dcnt[e] += 16
        key = "d%s:%s" % (e, b0.name)
        rec = (key, b0.dsem[e], b0.dcnt[e])
        self.ops[e].append(("dma", _ap(out), _ap(in_), b0.dsem[e]))
        for b in wb:
            b.w = rec
            b.r = {}
        for b in rb:
            b.r[key] = rec
        self.dmarecs[key] = rec

    def dma_fence(self):
        for e in ("sp", "pool"):
            for rec in list(self.dmarecs.values()):
                self._need(e, rec)

    def replay(self, e, eng):
        for it in self.ops[e]:
            if it[0] == "wait":
                eng.wait_ge(it[1], it[2])
            elif it[0] == "op":
                ins = it[1](eng)
                if it[2] is not None:
                    ins.then_inc(it[2], 1)
            else:
                eng.dma_start(out=it[1], in_=it[2]).then_inc(it[3], 16)


class K:
    def __init__(self, LP, PAST, LS=64):
        self.LP, self.PAST, self.LS = LP, PAST, LS
        nc = self.nc = bass.Bass("TRN2", target_bir_lowering=False)
        self.S = Sched(nc)
        import os
        self.pstop = float(os.environ.get("PSTOP", "99"))
        self.nbuf = 0
        self.din = {}
        self.dout = {}
        self._decl_io()
        self._alloc()

    def _in(self, name, shape):
        self.din[name] = self.nc.dram_tensor(name, list(shape), F32, kind="ExternalInput").ap()
        return self.din[name]

    def _out(self, name, shape):
        self.dout[name] = self.nc.dram_tensor(name, list(shape), F32, kind="ExternalOutput").ap()
        return self.dout[name]

    def _scr(self, name, shape, dt=BF16):
        return self.nc.dram_tensor(name, list(shape), dt, kind="Internal").ap()

    def _decl_io(self):
        LP, PAST, LS = self.LP, self.PAST, self.LS
        i = self._in
        i("xp", [LP, D]); i("xs", [LS, D])
        i("c_ckv", [NL, PAST, 128]); i("c_kr", [NL, PAST, 32])
        i("c_cak", [NL, 512, 256]); i("c_cav", [NL, 512, 256])
        i("st_pool", [NL, 15, 256])
        i("pp", [NL, LP, 256]); i("ps", [NL, LS, 256])
        i("w_in", [NL, D, 5536]); i("w_uq", [NL, 256, 768]); i("w_ukv", [NL, 128, 1024])
        i("w_o_mla", [NL, 512, D]); i("pool_w", [NL, 4, 64, 64]); i("w_o_pool", [NL, 256, D])
        i("w_o_ca", [NL, 256, D]); i("w_out", [NL, D, D]); i("w_pg", [NL, D, D]); i("w_pp", [NL, 256, D])
        i("gcol", [NL, 128, 18])
        i("grow", [NL, 1, 128 + 32 + 96 + 64 + 512])
        i("plsc", [NL, 64, 4])
        i("biasT", [NL, 128, 20, 128])
        i("ident", [128, 128])
        i("bands", [128, 12, 128])
        i("ropeP", [LP, 64]); i("ropeS", [LS, 64])
        o = self._out
        o("y_p", [LP, D]); o("y_s", [LS, D])
        o("ckv_p", [NL, LP, 128]); o("kr_p", [NL, LP, 32])
        o("cak_p", [NL, 512, 256]); o("cav_p", [NL, 512, 256]); o("pool_p", [NL, 15, 256])
        o("ckv_s", [NL, LS, 128]); o("kr_s", [NL, LS, 32])
        o("cak_s", [NL, LS, 256]); o("cav_s", [NL, LS, 256]); o("pool_s", [NL, 15, 256])
        s = self._scr
        self.nktP = LP // 128
        self.nktS = PAST // 128 + 1
        self.x1p = s("x1p", [LP, D], F32); self.x1s = s("x1s", [LS, D], F32)
        self.QTp = s("QTp", [NH, 96, LP]); self.KTp = s("KTp", [NH, 96, LP])
        self.VAp = s("VAp", [NH, 128, self.nktP, 65])
        self.OTp = s("OTp", [D, LP])
        LK = self.nktS * 128
        self.QTs = s("QTs", [NH, 96, LS]); self.KTs = s("KTs", [NH, 96, LK])
        self.VAs = s("VAs", [NH, 128, self.nktS, 65])
        self.OTs = s("OTs", [D, LS])

    def sb(self, name, shape, dt=F32, bufs=None):
        self.nbuf += 1
        ap = self.nc.alloc_sbuf_tensor("sb_" + name, list(shape), dt).ap()
        return T(ap, bufs if bufs is not None else [Buf(name)])

    def _alloc(self):
        nc = self.nc
        self.pp_t = [nc.alloc_psum_tensor("pp%d" % i, [128, 1024], F32).ap() for i in range(4)]
        self.pbuf = [Buf("pb%d" % i) for i in range(8)]
        self.bk_i = 0
        self.bk2_i = 0
        self.arena = nc.alloc_sbuf_tensor("arena", [128, NSLAB * SLAB // 2], BF16).ap()
        self.slab = [Buf("slab%d" % i) for i in range(NSLAB)]
        self.arenaB = nc.alloc_sbuf_tensor("arenaB", [128, NSLABB * 512], BF16).ap()
        self.slabB = [Buf("slabB%d" % i) for i in range(NSLABB)]
        self.ab_cur = 0
        sb = self.sb
        self.identf = sb("identf", [128, 128]); self.identb = sb("identb", [128, 128], BF16)
        self.ones_f = sb("ones_f", [128, 64])
        self.bandf = self.av(3, 1, [128, 12 * 128], F32)
        self.bands = sb("bands", [128, 12, 128], BF16)
        self.gcol = sb("gcol", [128, 18])
        self.grow = sb("grow", [128, 832])
        self.plsc = sb("plsc", [64, 4])
        self.biasT = sb("biasTb", [128, 20, 128], BF16)
        self.invn1 = sb("invn1", [128, 11]); self.invn2 = sb("invn2", [128, 24])
        self.w_uq = sb("w_uq", [128, 2, 768], BF16); self.w_ukv = sb("w_ukv", [128, 1024], BF16)
        self.pool_w = sb("pool_w", [64, 4, 64], BF16); self.w_pp = sb("w_ppb", [128, 2, 1024], BF16)
        self.stg = [sb("stg%d" % i, [128, 1024]) for i in range(2)]
        self.stg += [self.av(NSLABB - 8 + 4 * i, 4, [128, 1024], F32, arena="B") for i in range(2)]
        self.stg_i = 0

    def ab(self, name, shape, dt=F32):
        esz = 4 if dt == F32 else 2
        nb = int(np.prod(shape[1:])) * esz
        ns = (nb + 1023) // 1024
        t = self.av(self.ab_cur, ns, shape, dt, arena="B")
        self.ab_cur += ns
        assert self.ab_cur <= NSLABB, (name, self.ab_cur)
        return t

    def av(self, s0, n, shape, dt=BF16, off=0, arena="A"):
        SLAB_ = SLAB if arena == "A" else 1024
        ar = self.arena if arena == "A" else self.arenaB
        slabs = self.slab if arena == "A" else self.slabB
        base = ar[:, s0 * SLAB_ // 2 + off // 2: (s0 + n) * SLAB_ // 2]
        esz = 4 if dt == F32 else 2
        nel = int(np.prod(shape[1:]))
        assert off + nel * esz <= n * SLAB_, (shape, n)
        ap = base[:, 0:nel * esz // 2]
        if dt == F32:
            ap = ap.bitcast(F32)
        ap = ap[0:shape[0], :]
        if len(shape) == 3:
            ap = ap.rearrange("p (a b) -> p a b", b=shape[2])
        elif len(shape) == 4:
            ap = ap.rearrange("p (a b c) -> p a b c", b=shape[2], c=shape[3])
        return T(ap, slabs[s0:s0 + n])

    def bk(self):
        i = self.bk_i
        self.bk_i = (i + 1) % 8
        return T(self.pp_t[i // 2][:, (i % 2) * 512:(i % 2 + 1) * 512], [self.pbuf[i]])

    def bank(self, i):
        return T(self.pp_t[i // 2][:, (i % 2) * 512:(i % 2 + 1) * 512], [self.pbuf[i]])

    def bk2(self):
        i = self.bk2_i
        self.bk2_i = (i + 1) % 4
        return T(self.pp_t[i], [self.pbuf[2 * i], self.pbuf[2 * i + 1]])

    def act(self, out, in_, func, scale=1.0, bias=0.0, accum=None):
        r = [in_] + [x for x in (scale, bias) if isinstance(x, T)]
        w = [out] + ([accum] if accum is not None else [])
        kw = dict(out=out.ap, in_=in_.ap, func=func, bias=_ap(bias), scale=_ap(scale))
        if accum is not None:
            kw["accum_out"] = accum.ap
        self.S.op("act", lambda e: e.activation(**kw), r, w)

    def tt(self, eng, out, a, b, op):
        self.S.op(eng, lambda e: e.tensor_tensor(out=out.ap, in0=a.ap, in1=b.ap, op=op), [a, b], [out])

    def ts(self, eng, out, a, s1, op0, s2=None, op1=None):
        r = [a] + [x for x in (s1, s2) if isinstance(x, T)]
        if op1 is None:
            fn = lambda e: e.tensor_scalar(out=out.ap, in0=a.ap, scalar1=_ap(s1), scalar2=None, op0=op0)
        else:
            fn = lambda e: e.tensor_scalar(out=out.ap, in0=a.ap, scalar1=_ap(s1), scalar2=_ap(s2), op0=op0, op1=op1)
        self.S.op(eng, fn, r, [out])

    def stt(self, eng, out, a, s, b, op0, op1):
        r = [a, b] + ([s] if isinstance(s, T) else [])
        self.S.op(eng, lambda e: e.scalar_tensor_tensor(out=out.ap, in0=a.ap, scalar=_ap(s), in1=b.ap, op0=op0, op1=op1), r, [out])

    def cp(self, eng, out, in_):
        if eng == "act":
            self.act(out, in_, AF.Copy)
        else:
            self.S.op(eng, lambda e: e.tensor_copy(out=out.ap, in_=in_.ap), [in_], [out])

    def red(self, eng, out, in_):
        self.S.op(eng, lambda e: e.tensor_reduce(out=out.ap, in_=in_.ap, axis=AX.X, op=ALU.add), [in_], [out])

    def memset(self, eng, out, val):
        self.S.op(eng, lambda e: e.memset(out.ap, val), [], [out])

    def recip(self, out, in_):
        self.S.op("dve", lambda e: e.reciprocal(out=out.ap, in_=in_.ap), [in_], [out])

    def mm(self, out, lhsT, rhs, start=True, stop=True, mark=None):
        if mark is None:
            mark = stop
        self.S.op("pe", lambda e: e.matmul(out.ap, lhsT=lhsT.ap, rhs=rhs.ap, start=start, stop=stop), [lhsT, rhs], [out], mark=mark)

    def tr(self, out, in_, P, mark=True):
        idn = self.identb[0:P, 0:P]
        self.S.op("pe", lambda e: e.transpose(out=out.ap, in_=in_.ap, identity=idn.ap), [in_, idn], [out], mark=mark)

    def rstd(self, st_in, invn, tmp, out):
        self.tt("dve", tmp, st_in, invn, ALU.mult)
        self.act(tmp, tmp, AF.Ln, bias=self.eps_t[0:tmp.ap.shape[0], :])
        self.act(out, tmp, AF.Exp, scale=-0.5)

    def load_w(self, dst, src, scale=None, eng="pool"):
        n = dst.ap.shape[-1]
        rows = dst.ap.shape[0]
        for c0 in range(0, n, 1024):
            c1 = min(n, c0 + 1024)
            st = self.stg[self.stg_i]
            self.stg_i = (self.stg_i + 1) % len(self.stg)
            self.S.dma(st[0:rows, 0:c1 - c0], src[:, c0:c1])
            if scale is not None:
                self.act(dst[:, c0:c1], st[0:rows, 0:c1 - c0], AF.Copy, scale=scale)
            else:
                self.cp(eng, dst[:, c0:c1], st[0:rows, 0:c1 - c0])

    def setup_consts(self):
        S = self.S
        d = self.din
        S.dma(self.identf, d["ident"])
        self.cp("dve", self.identb, self.identf)
        self.memset("dve", self.ones_f, 1.0)
        self.eps_t = self.sb("eps_t", [128, 1])
        self.memset("dve", self.eps_t, EPS)
        self.one_t = self.sb("one_t", [128, 1])
        self.memset("dve", self.one_t, 1.0)
        S.dma(self.bandf, d["bands"].rearrange("p a b -> p (a b)"))
        self.cp("dve", self.bands.re("p a b -> p (a b)"), self.bandf)
        for c0, c1, v in ((0, 1, 1 / 256), (1, 2, 1 / 128), (2, 3, 1 / 32), (3, 11, 1 / 64)):
            self.memset("dve", self.invn1[:, c0:c1], v)
        for c0, c1, v in ((0, 8, 1 / 64), (8, 16, 1 / 32), (16, 24, 1 / 64)):
            self.memset("dve", self.invn2[:, c0:c1], v)
        self.invD = self.sb("invD", [128, 1])
        self.memset("dve", self.invD, 1.0 / D)

    def load_layer_small(self, l):
        S = self.S
        d = self.din
        S.dma(self.gcol, d["gcol"][l])
        S.dma(self.grow, d["grow"][l].partition_broadcast(128))
        S.dma(self.plsc, d["plsc"][l])
        self.ts("dve", self.grow[:, 320:576], self.grow[:, 320:576], 0.125, ALU.mult)
        bst = self.av(3, 2, [128, 20 * 128], F32)
        S.dma(bst, d["biasT"][l].rearrange("p a b -> p (a b)"))
        self.cp("pool", self.biasT.re("p a b -> p (a b)"), bst)
        for k in range(2):
            self.load_w(self.w_uq[:, k, :], d["w_uq"][l][k * 128:(k + 1) * 128, :], scale=self.gcol[:, 16 + k:17 + k])
            self.load_w(self.w_pp[:, k, :], d["w_pp"][l][k * 128:(k + 1) * 128, :])
        self.load_w(self.w_ukv, d["w_ukv"][l])
        for g in range(4):
            self.load_w(self.pool_w[:, g, :], d["pool_w"][l][g])

    def load_p1_weights(self, l):
        w_in = self.din["w_in"][l]
        self.w_tm = self.av(0, 3, [128, 8, 1440])
        pieces = [(0, 256, 0), (928, 1184, 256), (1440, 1952, 512), (1952, 2208, 1024), (256, 416, 1280)]
        import os
        KK = int(os.environ.get("KK", "8")); NP_ = int(os.environ.get("KNP", "5"))
        for k in range(KK):
            for (a, b, o) in pieces[:NP_]:
                self.load_w(self.w_tm[:, k, o:o + b - a], w_in[k * 128:(k + 1) * 128, a:b], scale=self.gcol[:, k:k + 1])

    def load_pg(self, l):
        d = self.din
        self.w_pg = self.av(12, 2, [128, 8, 1024])
        for k in range(8):
            self.load_w(self.w_pg[:, k, :], d["w_pg"][l][k * 128:(k + 1) * 128, :], scale=self.gcol[:, 8 + k:9 + k])

    def load_p3_weights(self, l):
        d = self.din
        w_in = d["w_in"][l]
        self.w_fm = self.av(0, 8, [128, 8, 4096])
        self.w_o = self.av(8, 2, [128, 8, 1024])
        self.w_out = self.av(10, 2, [128, 8, 1024])
        self.w_pg = self.av(12, 2, [128, 8, 1024])
        pieces = [(416, 928, 0), (1184, 1440, 512), (2208, 2464, 768), (2464, 3488, 1024), (3488, 4512, 2048), (4512, 5536, 3072)]
        for k in range(8):
            for (a, b, o) in pieces:
                self.load_w(self.w_fm[:, k, o:o + b - a], w_in[k * 128:(k + 1) * 128, a:b], scale=self.gcol[:, k:k + 1])
            self.load_w(self.w_out[:, k, :], d["w_out"][l][k * 128:(k + 1) * 128, :], eng="dve")
        for k in range(4):
            self.load_w(self.w_o[:, k, :], d["w_o_mla"][l][k * 128:(k + 1) * 128, :], eng="dve")
        for k in range(2):
            self.load_w(self.w_o[:, 4 + k, :], d["w_o_pool"][l][k * 128:(k + 1) * 128, :])
            self.load_w(self.w_o[:, 6 + k, :], d["w_o_ca"][l][k * 128:(k + 1) * 128, :])

    def alloc_p1(self):
        av = self.av
        o = [0]

        def A(shape, dt=BF16, slab=None):
            esz = 4 if dt == F32 else 2
            nb = int(np.prod(shape[1:])) * esz
            ns = (nb + SLAB - 1) // SLAB
            t = av(o[0], ns, shape, dt)
            o[0] += ns
            return t

        o[0] = 3
        self.p1 = p = {}
        p["xt"] = [A([128, 1024], F32), A([128, 1024], F32)]
        p["kv_s"] = A([128, 1024], F32)
        p["zA"] = A([128, 512], F32)
        p["zB"] = A([128, 512], F32)
        p["zC"] = A([128, 416], F32)
        p["q_s"] = A([128, 768], F32)
        p["sq"] = A([128, 768], F32)
        p["hT"] = A([128, 8, 128])
        assert o[0] <= 12
        def sb(name, shape, dt=F32):
            nb = int(np.prod(shape[1:])) * (4 if dt == F32 else 2)
            return self.ab(name, shape, dt) if nb >= 512 else self.sb(name, shape, dt)
        self.ab_cur = 0
        p["hb"] = sb("p1_hb", [128, 1024], BF16)
        p["junk"] = sb("p1_junk", [128, 1024], BF16)
        p["st0"] = sb("p1_st0", [128, 4])
        p["st1"] = sb("p1_st1", [128, 11]); p["tm1"] = sb("p1_tm1", [128, 11]); p["rs1"] = sb("p1_rs1", [128, 11])
        p["st2"] = sb("p1_st2", [128, 24]); p["tm2"] = sb("p1_tm2", [128, 24]); p["rs2"] = sb("p1_rs2", [128, 24])
        p["cq"] = sb("p1_cq", [128, 256], BF16); p["cqT"] = sb("p1_cqT", [128, 2, 128], BF16)
        p["ckv_s"] = sb("p1_ckv", [128, 128]); p["ckvb"] = sb("p1_ckvb", [128, 128], BF16); p["ckvT"] = sb("p1_ckvT", [128, 128], BF16)
        p["kr"] = sb("p1_kr", [128, 32]); p["kt1"] = sb("p1_kt1", [128, 32]); p["kt2"] = sb("p1_kt2", [128, 32]); p["kro"] = sb("p1_kro", [128, 32])
        p["rp"] = [sb("p1_rp%d" % i, [128, 64]) for i in range(3)]
        p["qkn"] = sb("p1_qkn", [128, 512]); p["qkb"] = sb("p1_qkb", [128, 512], BF16)
        p["qa"] = sb("p1_qa", [128, 8, 96], BF16); p["ka"] = sb("p1_ka", [128, 8, 96], BF16)
        p["va"] = self.sb("p1_va", [128, 8, 65], BF16)
        p["trp"] = sb("p1_trp", [128, 8, 32]); p["t1"] = sb("p1_t1", [128, 8, 32]); p["t2"] = sb("p1_t2", [128, 8, 32])
        p["qT_s"] = sb("p1_qTs", [96, 8, 128], BF16); p["kT_s"] = sb("p1_kTs", [96, 8, 128], BF16)
        p["ub"] = [sb("p1_ub%d" % i, [128, 256], BF16) for i in range(3)]
        p["ubf"] = sb("p1_ubf", [128, 256])
        p["pldT"] = sb("p1_pldT", [64, 4, 128], BF16); p["yT"] = sb("p1_yT", [64, 4, 128], BF16)
        p["QcT"] = [sb("p1_QcT%d" % i, [64, 4, 128], BF16) for i in range(2)]
        p["KcT"] = [sb("p1_KcT%d" % i, [64, 4, 128], BF16) for i in range(6)]
        p["Vc"] = [self.sb("p1_Vc%d" % i, [128, 4, 65], BF16) for i in range(6)]
        p["PTc"] = sb("p1_PTc", [128, 5, 4, 128], BF16)
        p["rd"] = sb("p1_rd", [128, 512]); p["bc_s"] = sb("p1_bcs", [64, 512]); p["ocT"] = sb("p1_ocT", [64, 4, 128], BF16)
        p["cst"] = sb("p1_cst", [128, 256]); p["cstb"] = sb("p1_cstb", [128, 256], BF16)
        for t in p["Vc"]:
            self.memset("pool", t, 1.0)
        self.memset("pool", p["va"], 1.0)

    def kv_build(self, P, tok0, t, KT, VA, ckvb, kro):
        p = self.p1
        self.tr(self.bkb()[:, 0:P], ckvb[0:P, :], P)
        ps = self.last_bkb
        self.cp("dve", p["ckvT"][:, 0:P], ps[:, 0:P])
        pk = self.bk2()
        for c in range(2):
            self.mm(pk[0:P, c * 512:(c + 1) * 512], p["ckvT"][:, 0:P], self.w_ukv[:, c * 512:(c + 1) * 512])
        kv = p["kv_s"]
        self.cp("act", kv[0:P, 0:512], pk[0:P, 0:512])
        self.cp("dve", kv[0:P, 512:1024], pk[0:P, 512:1024])
        kv3 = kv.re("p (h d) -> p h d", d=128)
        sq3 = p["sq"].re("p (h d) -> p h d", d=96)
        self.act(sq3[0:P, :, 0:64], kv3[0:P, :, 0:64], AF.Square)
        self.red("dve", p["st2"][0:P, 16:24], sq3[0:P, :, 0:64])
        self.rstd(p["st2"][0:P, 16:24], self.invn2[0:P, 16:24], p["tm2"][0:P, 16:24], p["rs2"][0:P, 16:24])
        gkn = self.grow[0:P, 256:320]
        ka = p["ka"]
        self.tt("dve", sq3[0:P, :, 0:64], kv3[0:P, :, 0:64], gkn.bc(1, [P, 8, 64]), ALU.mult)
        self.tt("dve", ka[0:P, :, 0:64], sq3[0:P, :, 0:64], p["rs2"][0:P, 16:24].bc(2, [P, 8, 64]), ALU.mult)
        self.cp("dve", ka[0:P, :, 64:96], kro[0:P, :].bc(1, [P, 8, 32]))
        self.cp("act", p["va"][0:P, :, 0:64], kv3[0:P, :, 64:128])
        pt = self.bkb()
        pt3 = pt.re("p (h t) -> p h t", t=128)
        for h in range(8):
            self.tr(pt3[0:96, h, 0:P], ka[0:P, h, :], P, mark=(h == 7))
        self.cp("act", p["kT_s"][:, :, 0:P], pt3[0:96, :, 0:P])
        self.S.dma(KT[:, :, tok0:tok0 + P].rearrange("h d t -> d h t"), p["kT_s"][:, :, 0:P])
        self.S.dma(VA[:, 0:P, t, :].rearrange("h p c -> p h c"), p["va"][0:P, :, :])

    def bkb(self):
        b = self.bk()
        self.last_bkb = b.bit(BF16)
        return self.last_bkb

    def p1_tile(self, l, seq, t):
        p = self.p1
        S = self.S
        P = seq["P"]
        tok0 = t * 128
        kt = seq["kt0"] + t
        xt = p["xt"][t % 2]
        rp = p["rp"][t % 3]
        QcT = p["QcT"][t % 2]
        S.dma(xt[0:P, :], seq["x"][tok0:tok0 + P, :])
        S.dma(rp[0:P, :], seq["rope"][tok0:tok0 + P, :])
        st0 = p["st0"]
        self.act(p["junk"][0:P, :], xt[0:P, :], AF.Square, accum=st0[0:P, 0:1])
        self.rstd(st0[0:P, 0:1], self.invD[0:P, :], st0[0:P, 1:2], st0[0:P, 2:3])
        self.act(p["hb"][0:P, :], xt[0:P, :], AF.Copy, scale=st0[0:P, 2:3])
        pt = self.bkb()
        pt3 = pt.re("p (k t) -> p k t", t=128)
        for k in range(8):
            self.tr(pt3[:, k, 0:P], p["hb"][0:P, k * 128:(k + 1) * 128], P, mark=(k == 7))
        hT = p["hT"]
        self.cp("dve", hT[:, :, 0:P], pt3[:, :, 0:P])
        yield
        zs = [p["zA"], p["zB"], p["zC"]]
        for c, wc in enumerate((512, 512, 416)):
            pz = self.bk()
            for k in range(8):
                self.mm(pz[0:P, 0:wc], hT[:, k, 0:P], self.w_tm[:, k, c * 512:c * 512 + wc], start=(k == 0), stop=(k == 7))
            self.cp("act" if c != 1 else "dve", zs[c][0:P, 0:wc], pz[0:P, 0:wc])
        zA, zB, zC = zs
        st1, rs1 = p["st1"], p["rs1"]
        sq = p["sq"]
        self.act(sq[0:P, 0:256], zA[0:P, 0:256], AF.Square, accum=st1[0:P, 0:1])
        self.act(sq[0:P, 256:384], zC[0:P, 256:384], AF.Square, accum=st1[0:P, 1:2])
        self.act(sq[0:P, 384:416], zC[0:P, 384:416], AF.Square, accum=st1[0:P, 2:3])
        self.act(sq[0:P, 0:512], zB[0:P, :], AF.Square)
        self.red("dve", st1[0:P, 3:11], sq[0:P, 0:512].re("p (h d) -> p h d", d=64))
        self.rstd(st1[0:P, :], self.invn1[0:P, :], p["tm1"][0:P, :], rs1[0:P, :])
        yield
        self.act(p["cq"][0:P, :], zA[0:P, 0:256], AF.Copy, scale=rs1[0:P, 0:1])
        ckv_s = p["ckv_s"]
        self.stt("dve", ckv_s[0:P, :], zC[0:P, 256:384], rs1[0:P, 1:2], self.grow[0:P, 0:128], ALU.mult, ALU.mult)
        S.dma(seq["o_ckv"][l][tok0:tok0 + P, :], ckv_s[0:P, :])
        self.cp("act", p["ckvb"][0:P, :], ckv_s[0:P, :])
        kr, kt1, kt2, kro = p["kr"], p["kt1"], p["kt2"], p["kro"]
        self.stt("dve", kr[0:P, :], zC[0:P, 384:416], rs1[0:P, 2:3], self.grow[0:P, 128:160], ALU.mult, ALU.mult)
        self.tt("dve", kt1[0:P, :], kr[0:P, :], rp[0:P, 0:32], ALU.mult)
        self.tt("dve", kt2[0:P, 0:16], kr[0:P, 16:32], rp[0:P, 32:48], ALU.mult)
        self.tt("dve", kt2[0:P, 16:32], kr[0:P, 0:16], rp[0:P, 48:64], ALU.mult)
        self.tt("dve", kro[0:P, :], kt1[0:P, :], kt2[0:P, :], ALU.add)
        S.dma(seq["o_kr"][l][tok0:tok0 + P, :], kro[0:P, :])
        ub = p["ub"][t % 3]
        ubp = p["ub"][(t - 1) % 3]
        self.cp("act", ub[0:P, :], zA[0:P, 256:512])
        if t == seq["nt"] - 1:
            S.dma(seq["o_pool"][l], zA[P - 15:P, 256:512])
        qkn, qkb = p["qkn"], p["qkb"]
        zB3 = zB.re("p (h d) -> p h d", d=64)
        qk3 = qkn.re("p (h d) -> p h d", d=64)
        self.tt("dve", qk3[0:P], zB3[0:P], self.grow[0:P, 320:832].re("p (h d) -> p h d", d=64), ALU.mult)
        self.tt("dve", qk3[0:P], qk3[0:P], rs1[0:P, 3:11].bc(2, [P, 8, 64]), ALU.mult)
        self.cp("act", qkb[0:P, :], qkn[0:P, :])
        lo = max(0, seq["nt"] * 128 - 512) if not seq["hist"] else 0
        if tok0 >= lo:
            S.dma(seq["o_cak"][l][tok0 - lo:tok0 - lo + P, :], qkn[0:P, 256:512])
            S.dma(seq["o_cav"][l][tok0 - lo:tok0 - lo + P, :], zC[0:P, 0:256])
        u_cur = seq["ca0"] + t
        slot = u_cur % 6
        self.cp("act", p["Vc"][slot][0:P, :, 0:64], zC[0:P, 0:256].re("p (h d) -> p h d", d=64))
        yield
        pt = self.bkb()
        pt3 = pt.re("p (k t) -> p k t", t=128)
        for k in range(2):
            self.tr(pt3[:, k, 0:P], p["cq"][0:P, k * 128:(k + 1) * 128], P, mark=(k == 1))
        self.cp("dve", p["cqT"][:, :, 0:P], pt3[:, 0:2, 0:P])
        pq = self.bk2()
        for c, (c0, c1) in enumerate(((0, 512), (512, 768))):
            for k in range(2):
                self.mm(pq[0:P, c * 512:c * 512 + c1 - c0], p["cqT"][:, k, 0:P], self.w_uq[:, k, c0:c1], start=(k == 0), stop=(k == 1))
        q_s = p["q_s"]
        self.cp("act", q_s[0:P, 0:512], pq[0:P, 0:512])
        self.cp("dve", q_s[0:P, 512:768], pq[0:P, 512:768])
        self.act(sq[0:P, 0:768], q_s[0:P, :], AF.Square)
        sq3 = sq.re("p (h d) -> p h d", d=96)
        st2, rs2 = p["st2"], p["rs2"]
        self.red("dve", st2[0:P, 0:8], sq3[0:P, :, 0:64])
        self.red("dve", st2[0:P, 8:16], sq3[0:P, :, 64:96])
        self.rstd(st2[0:P, 0:16], self.invn2[0:P, 0:16], p["tm2"][0:P, 0:16], rs2[0:P, 0:16])
        q3 = q_s.re("p (h d) -> p h d", d=96)
        gq96 = self.grow[0:P, 160:256]
        self.tt("dve", q3[0:P, :, :], q3[0:P, :, :], gq96.bc(1, [P, 8, 96]), ALU.mult)
        qa = p["qa"]
        self.tt("dve", qa[0:P, :, 0:64], q3[0:P, :, 0:64], rs2[0:P, 0:8].bc(2, [P, 8, 64]), ALU.mult)
        trp, t1, t2 = p["trp"], p["t1"], p["t2"]
        self.tt("dve", trp[0:P], q3[0:P, :, 64:96], rs2[0:P, 8:16].bc(2, [P, 8, 32]), ALU.mult)
        self.tt("dve", t1[0:P], trp[0:P], rp[0:P, 0:32].bc(1, [P, 8, 32]), ALU.mult)
        self.tt("dve", t2[0:P, :, 0:16], trp[0:P, :, 16:32], rp[0:P, 32:48].bc(1, [P, 8, 16]), ALU.mult)
        self.tt("dve", t2[0:P, :, 16:32], trp[0:P, :, 0:16], rp[0:P, 48:64].bc(1, [P, 8, 16]), ALU.mult)
        self.tt("dve", qa[0:P, :, 64:96], t1[0:P], t2[0:P], ALU.add)
        pt = self.bkb()
        pt3 = pt.re("p (h t) -> p h t", t=128)
        for h in range(8):
            self.tr(pt3[0:96, h, 0:P], qa[0:P, h, :], P, mark=(h == 7))
        self.cp("act", p["qT_s"][:, :, 0:P], pt3[0:96, :, 0:P])
        S.dma(seq["QT"][:, :, tok0:tok0 + P].rearrange("h d t -> d h t"), p["qT_s"][:, :, 0:P])
        yield
        self.kv_build(P, kt * 128, kt, seq["KT"], seq["VA"], p["ckvb"], kro)
        yield
        first = (t == 0 and not seq["hist"])
        pp_ = self.bk()
        pp3 = pp_.re("p (g t) -> p g t", t=128)
        for g in range(4):
            bc_ = self.bands[0:P, (8 + g) if first else g, 0:P]
            self.mm(pp3[0:64, g, 0:P], ub[0:P, g * 64:(g + 1) * 64], bc_, start=True, stop=first)
            if not first:
                self.mm(pp3[0:64, g, 0:P], ubp[:, g * 64:(g + 1) * 64], self.bands[:, 4 + g, 0:P], start=False, stop=True)
        self.cp("dve", p["pldT"][:, :, 0:P], pp3[0:64, :, 0:P])
        py = self.bk()
        py3 = py.re("p (g t) -> p g t", t=128)
        for g in range(4):
            self.mm(py3[0:64, g, 0:P], self.pool_w[:, g, :], p["pldT"][:, g, 0:P])
        self.tt("dve", p["yT"][:, :, 0:P], py3[0:64, :, 0:P], self.plsc.bc(2, [64, 4, P]), ALU.mult)
        S.dma(seq["OT"][512:768, tok0:tok0 + P].rearrange("(g d) t -> d g t", d=64), p["yT"][:, :, 0:P])
        yield
        pt = self.bkb()
        pt3 = pt.re("p (h t) -> p h t", t=128)
        for h in range(8):
            self.tr(pt3[0:64, h, 0:P], qkb[0:P, h * 64:(h + 1) * 64], P, mark=(h == 7))
        self.cp("act", QcT[:, :, 0:P], pt3[0:64, 0:4, 0:P])
        self.cp("act", p["KcT"][slot][:, :, 0:P], pt3[0:64, 4:8, 0:P])
        yield
        us = [u for u in range(u_cur - 4, u_cur + 1) if u >= 0]
        PTc = p["PTc"]
        for ui, u in enumerate(us):
            nk = P if u == u_cur else 128
            du = u - u_cur + 4
            pS = self.bk()
            pS3 = pS.re("p (h t) -> p h t", t=128)
            for h in range(4):
                self.mm(pS3[0:nk, h, 0:P], p["KcT"][u % 6][:, h, 0:nk], QcT[:, h, 0:P], start=True, stop=False, mark=False)
                self.mm(pS3[0:nk, h, 0:P], self.identb[0:nk, 0:nk], self.biasT[0:nk, h * 5 + du, 0:P], start=False, stop=True, mark=(h == 3))
            self.act(PTc[0:nk, ui, :, 0:P], pS3[0:nk, :, 0:P], AF.Exp)
        po = self.bk()
        po3 = po.re("p (h t) -> p h t", t=128)
        for h in range(4):
            for ui, u in enumerate(us):
                nk = P if u == u_cur else 128
                self.mm(po3[0:65, h, 0:P], p["Vc"][u % 6][0:nk, h, :], PTc[0:nk, ui, h, 0:P], start=(ui == 0), stop=(ui == len(us) - 1),
                        mark=(ui == len(us) - 1 and h == 3))
        rd = p["rd"]
        rd3 = rd.re("p (h t) -> p h t", t=128)
        self.act(rd3[64:65, :, 0:P], po3[64:65, :, 0:P], AF.Ln)
        self.act(rd3[64:65, :, 0:P], rd3[64:65, :, 0:P], AF.Exp, scale=-1.0)
        pb = self.bk()
        pb3 = pb.re("p (h t) -> p h t", t=128)
        for h in range(4):
            self.mm(pb3[0:64, h, 0:P], self.ones_f[64:65, 0:64], rd3[64:65, h, 0:P], mark=(h == 3))
        bcs3 = p["bc_s"].re("p (h t) -> p h t", t=128)
        self.cp("act", bcs3[:, :, 0:P], pb3[0:64, :, 0:P])
        self.tt("dve", p["ocT"][:, :, 0:P], po3[0:64, :, 0:P], bcs3[:, :, 0:P], ALU.mult)
        S.dma(seq["OT"][768:1024, tok0:tok0 + P].rearrange("(h d) t -> d h t", d=64), p["ocT"][:, :, 0:P])

    def run_interleaved(self, gens, lag):
        active = []
        idx = 0
        while idx < len(gens) or active:
            if idx < len(gens) and (not active or active[-1][1] >= lag):
                active.append([gens[idx], 0])
                idx += 1
            for a in list(active):
                try:
                    next(a[0])
                    a[1] += 1
                except StopIteration:
                    active.remove(a)

    def p1_hist(self, l, seq):
        p = self.p1
        S = self.S
        d = self.din
        nh = self.PAST // 128
        for t in range(nh):
            S.dma(p["ckv_s"], d["c_ckv"][l][t * 128:(t + 1) * 128, :])
            S.dma(p["kro"], d["c_kr"][l][t * 128:(t + 1) * 128, :])
            self.cp("act", p["ckvb"], p["ckv_s"])
            self.kv_build(128, t * 128, t, seq["KT"], seq["VA"], p["ckvb"], p["kro"])
        for u in range(4):
            S.dma(p["cst"], d["c_cak"][l][u * 128:(u + 1) * 128, :])
            self.cp("act", p["cstb"], p["cst"])
            pt = self.bkb()
            pt3 = pt.re("p (h t) -> p h t", t=128)
            for h in range(4):
                self.tr(pt3[0:64, h, :], p["cstb"][:, h * 64:(h + 1) * 64], 128, mark=(h == 3))
            self.cp("act", p["KcT"][u % 6], pt3[0:64, 0:4, :])
            S.dma(p["cst"], d["c_cav"][l][u * 128:(u + 1) * 128, :])
            self.cp("dve", p["Vc"][u % 6][:, :, 0:64], p["cst"].re("p (h d) -> p h d", d=64))
        self.memset("dve", p["ubf"], 0.0)
        S.dma(p["ubf"][113:128, :], d["st_pool"][l])
        self.cp("dve", p["ub"][2], p["ubf"])

    def alloc_p2(self):
        self.p2 = p = {}
        p["KT"] = [self.av(0, 2, [96, 8192]), self.av(5, 2, [96, 8192])]
        p["QT"] = [self.av(2, 2, [96, 8192]), self.av(7, 2, [96, 8192])]
        p["VA"] = [self.av(4, 1, [128, 64, 65]), self.av(9, 1, [128, 64, 65])]
        p["PT"] = [self.av(10, 1, [128, 512], off=i * 1024) for i in range(6)]
        for i, t in enumerate(p["PT"]):
            t.bufs = [self.slab[10]] if False else [Buf("p2pt%d" % i)]
        def sb(name, shape, dt=F32):
            nb = int(np.prod(shape[1:])) * (4 if dt == F32 else 2)
            return self.ab(name, shape, dt) if nb >= 512 else self.sb(name, shape, dt)
        self.ab_cur = 0
        p["rd"] = sb("p2_rd", [128, 512]); p["bc_s"] = [sb("p2_bcs%d" % i, [64, 512]) for i in range(2)]
        p["oT"] = [sb("p2_oT%d" % i, [64, 512], BF16) for i in range(2)]
        self.pt_i = 0
        self.p2_o = 0
        self.p2_s = 0
        self.p2slabdep = self.slab[10]

    def p2_seq(self, seq, causal):
        p = self.p2
        S = self.S
        NQ = seq["nq"]
        NKT = seq["nkt_total"]
        nk_last = seq["nk_last"]
        sc = 96 ** -0.5
        for i in range(6):
            self.memset("pool", T(p["PT"][i].ap, p["PT"][i].bufs + [self.p2slabdep]), 0.0)
        NK = (NKT - 1) * 128 + nk_last

        def load_head(h):
            s = h % 2
            KT, QT, VA = p["KT"][s], p["QT"][s], p["VA"][s]
            S.dma(KT[:, 0:NK], seq["KT"][h][:, 0:NK])
            S.dma(QT[:, 0:NQ], seq["QT"][h][:, 0:NQ])
            if nk_last == 128:
                S.dma(VA[:, 0:NKT, :], seq["VA"][h][:, 0:NKT, :])
            else:
                S.dma(VA[:, 0:NKT - 1, :], seq["VA"][h][:, 0:NKT - 1, :])
                S.dma(VA[0:nk_last, NKT - 1, :], seq["VA"][h][0:nk_last, NKT - 1, :])

        blocks = []
        for h in range(NH):
            s = h % 2
            KT, QT, VA = p["KT"][s], p["QT"][s], p["VA"][s]
            gi = 0
            first_of_head = True
            for q0 in range(0, NQ, 512):
                W = min(512, NQ - q0)
                kts = list(range(0, (q0 + W) // 128)) if causal else list(range(NKT))
                grp = dict(pO=None)
                for ki, kt in enumerate(kts):
                    nk = nk_last if kt == NKT - 1 else 128
                    diag = causal and kt * 128 >= q0
                    c0 = kt * 128 - q0 if diag else 0
                    blk = {}

                    def A(h=h, kt=kt, nk=nk, diag=diag, c0=c0, W=W, q0=q0, KT=KT, QT=QT, blk=blk, pre=first_of_head):
                        pS = self.bank(2 + self.p2_s % 5)
                        self.p2_s += 1
                        self.mm(pS[0:nk, c0:W], KT[:, kt * 128:kt * 128 + nk], QT[:, q0 + c0:q0 + W])
                        PT = p["PT"][self.pt_i]
                        self.pt_i = (self.pt_i + 1) % 6
                        self.act(PT[0:nk, c0:W], pS[0:nk, c0:W], AF.Exp, scale=sc)
                        if diag:
                            self.memset("pool", PT[64:128, c0:c0 + 64], 0.0)
                        blk["PT"] = PT

                    def B(h=h, kt=kt, nk=nk, c0=c0, W=W, q0=q0, VA=VA, blk=blk, ki=ki, nkts=len(kts), grp=grp, gi=gi, pre=first_of_head):
                        if pre and h + 1 < NH:
                            load_head(h + 1)
                        if ki == 0:
                            grp["pO"] = self.bank(self.p2_o % 2)
                            self.p2_o += 1
                        pO = grp["pO"]
                        self.mm(pO[0:65, c0:W], VA[0:nk, kt, :], blk["PT"][0:nk, c0:W], start=(ki == 0), stop=(ki == nkts - 1), mark=True)
                        if ki == nkts - 1:
                            rd = p["rd"]
                            self.act(rd[64:65, 0:W], pO[64:65, 0:W], AF.Ln)
                            self.act(rd[64:65, 0:W], rd[64:65, 0:W], AF.Exp, scale=-1.0)
                            pb = self.bank(7)
                            self.mm(pb[0:64, 0:W], self.ones_f[64:65, 0:64], rd[64:65, 0:W])
                            bcs = p["bc_s"][gi % 2]
                            oT = p["oT"][gi % 2]
                            self.cp("act", bcs[:, 0:W], pb[0:64, 0:W])
                            self.tt("dve", oT[:, 0:W], pO[0:64, 0:W], bcs[:, 0:W], ALU.mult)
                            S.dma(seq["OT"][h * 64:(h + 1) * 64, q0:q0 + W], oT[:, 0:W])

                    blocks.append((A, B))
                    first_of_head = False
                gi += 1
        LA = 4
        load_head(0)
        for i in range(len(blocks) + LA):
            if i < len(blocks):
                blocks[i][0]()
            if i - LA >= 0:
                blocks[i - LA][1]()
        for i in range(6):
            self.memset("pool", T(p["PT"][i].ap, p["PT"][i].bufs + [self.p2slabdep]), 0.0)

    def alloc_p3(self):
        def sb(name, shape, dt=F32):
            nb = int(np.prod(shape[1:])) * (4 if dt == F32 else 2)
            return self.ab(name, shape, dt) if nb >= 512 else self.sb(name, shape, dt)
        self.ab_cur = 0
        self.p3 = p = {}
        G = 256
        p["xt"] = [sb("p3_xt%d" % i, [128, 1024]) for i in range(2)]
        p["xr"] = [sb("p3_xr%d" % i, [128, 1024]) for i in range(2)]
        p["hb"] = sb("p3_hb", [128, 1024], BF16); p["junk"] = sb("p3_junk", [128, 1024], BF16)
        p["st"] = sb("p3_st", [128, 8])
        p["hT"] = [sb("p3_hT%d" % i, [128, 8, G], BF16) for i in range(2)]
        p["oT"] = sb("p3_oT", [128, 8, G], BF16)
        p["ed"] = [sb("p3_ed%d" % i, [128, G]) for i in range(4)]
        p["tt"] = [sb("p3_tt%d" % i, [128, G]) for i in range(4)]
        p["mT"] = sb("p3_mT", [128, 8, G], BF16)
        p["hxT"] = sb("p3_hxT", [128, 8, 128], BF16)
        p["pt"] = [sb("p3_pt%d" % i, [128, 256]) for i in range(2)]; p["pb"] = sb("p3_pb", [128, 256], BF16); p["pT"] = sb("p3_pT", [128, 2, 128], BF16)
        p["sig"] = sb("p3_sig", [128, 1024])
        self.ed_i = 0
        self.tt_i = 0
        self.gd_i = 0

    def p3_head(self, l, seq, g0, tiles, gi):
        p = self.p3
        S = self.S
        hT, st = p["hT"][gi % 2], p["st"]
        for j, (tok0, P) in enumerate(tiles):
            S.dma(p["xt"][j % 2][0:P, :], seq["x"][tok0:tok0 + P, :])
        for j, (tok0, P) in enumerate(tiles):
            xt = p["xt"][j % 2]
            jo = tok0 - g0
            self.act(p["junk"][0:P, :], xt[0:P, :], AF.Square, accum=st[0:P, 0:1])
            self.rstd(st[0:P, 0:1], self.invD[0:P, :], st[0:P, 1:2], st[0:P, 2:3])
            self.act(p["hb"][0:P, :], xt[0:P, :], AF.Copy, scale=st[0:P, 2:3])
            pt = self.bkb()
            pt3 = pt.re("p (k t) -> p k t", t=128)
            for k in range(8):
                self.tr(pt3[:, k, 0:P], p["hb"][0:P, k * 128:(k + 1) * 128], P, mark=(k == 7))
            self.cp("dve", hT[:, :, jo:jo + P], pt3[:, :, 0:P])

    def p3_group(self, l, seq, g0, tiles, gi=0, hook=None):
        p = self.p3
        S = self.S
        W = sum(P for _, P in tiles)
        hT, oT, mT, st = p["hT"][gi % 2], p["oT"], p["mT"], p["st"]
        S.dma(oT[:, :, 0:W], seq["OT"][:, g0:g0 + W].rearrange("(c p) t -> p c t", p=128))
        for j, (tok0, P) in enumerate(tiles):
            S.dma(p["xr"][j % 2][0:P, :], seq["x"][tok0:tok0 + P, :])
            S.dma(p["pt"][j % 2][0:P, :], seq["p"][l][tok0:tok0 + P, :])

        def ntt():
            t = p["tt"][self.tt_i]
            self.tt_i = (self.tt_i + 1) % 4
            return t

        def gate_items(items):
            for i in range(0, len(items), 2):
                grp = items[i:i + 2]
                st_ = []
                for it in grp:
                    ppd = it["pre"]() if it.get("pre") else None
                    pg = self.bk()
                    for k in range(8):
                        self.mm(pg[:, 0:W], self.w_fm[:, k, it["col0"]:it["col0"] + 128], hT[:, k, 0:W], start=(k == 0), stop=(k == 7))
                    ed = p["ed"][self.ed_i]
                    self.ed_i = (self.ed_i + 1) % 4
                    st_.append((pg, ppd, ed))
                for (pg, ppd, ed) in st_:
                    self.act(ed[:, 0:W], pg[:, 0:W], AF.Exp, scale=-1.0)
                for (pg, ppd, ed) in st_:
                    self.act(ed[:, 0:W], ed[:, 0:W], AF.Ln, bias=self.one_t[:, :])
                for (pg, ppd, ed) in st_:
                    self.act(ed[:, 0:W], ed[:, 0:W], AF.Exp, scale=-1.0)
                for it, (pg, ppd, ed) in zip(grp, st_):
                    it["post"](pg, ppd, ed)

        def silu_post(c):
            def f(pg, ppd, ed):
                t = ntt()
                self.tt("dve", t[:, 0:W], pg[:, 0:W], ed[:, 0:W], ALU.mult)
                self.tt("dve", oT[:, c, 0:W], t[:, 0:W], oT[:, c, 0:W], ALU.mult)
            return f

        gate_items([dict(col0=c * 128, post=silu_post(c)) for c in range(8)])
        accs = {}

        def mk_pre(oc, kc0, nkc):
            def f():
                ppd = self.bk()
                for kc in range(nkc):
                    self.mm(ppd[:, 0:W], self.w_o[:, kc0 + kc, oc * 128:(oc + 1) * 128], oT[:, kc0 + kc, 0:W], start=(kc == 0), stop=(kc == nkc - 1))
                return ppd
            return f

        def mk_post(oc, br):
            def f(pg, ppd, ed):
                t = ntt()
                self.tt("dve", t[:, 0:W], ppd[:, 0:W], ed[:, 0:W], ALU.mult)
                if br == 0:
                    accs[oc] = t
                elif br == 1:
                    self.tt("dve", accs[oc][:, 0:W], accs[oc][:, 0:W], t[:, 0:W], ALU.add)
                else:
                    self.tt("dve", mT[:, oc, 0:W], accs[oc][:, 0:W], t[:, 0:W], ALU.add)
            return f

        items = []
        for oc in range(8):
            for br, (kc0, nkc) in enumerate(((0, 4), (4, 2), (6, 2))):
                items.append(dict(col0=1024 + br * 1024 + oc * 128, pre=mk_pre(oc, kc0, nkc), post=mk_post(oc, br)))
        gate_items(items)
        if hook is not None:
            hook()
        for j, (tok0, P) in enumerate(tiles):
            jo = tok0 - g0
            xr, sig = p["xr"][j % 2], p["sig"]
            x2 = xr
            ptl = p["pt"][j % 2]
            px = self.bk2()
            for c in range(2):
                for k in range(8):
                    self.mm(px[0:P, c * 512:(c + 1) * 512], mT[:, k, jo:jo + P], self.w_out[:, k, c * 512:(c + 1) * 512], start=(k == 0), stop=(k == 7))
            for c in range(2):
                self.tt("dve", x2[0:P, c * 512:(c + 1) * 512], px[0:P, c * 512:(c + 1) * 512], xr[0:P, c * 512:(c + 1) * 512], ALU.add)
            self.act(p["junk"][0:P, :], x2[0:P, :], AF.Square, accum=st[0:P, 4:5])
            self.rstd(st[0:P, 4:5], self.invD[0:P, :], st[0:P, 5:6], st[0:P, 6:7])
            self.act(p["hb"][0:P, :], x2[0:P, :], AF.Copy, scale=st[0:P, 6:7])
            pt = self.bkb()
            pt3 = pt.re("p (k t) -> p k t", t=128)
            for k in range(8):
                self.tr(pt3[:, k, 0:P], p["hb"][0:P, k * 128:(k + 1) * 128], P, mark=(k == 7))
            self.cp("dve", p["hxT"][:, :, 0:P], pt3[:, :, 0:P])
            self.cp("act", p["pb"][0:P, :], ptl[0:P, :])
            pt = self.bkb()
            pt3 = pt.re("p (k t) -> p k t", t=128)
            for k in range(2):
                self.tr(pt3[:, k, 0:P], p["pb"][0:P, k * 128:(k + 1) * 128], P, mark=(k == 1))
            self.cp("dve", p["pT"][:, :, 0:P], pt3[:, 0:2, 0:P])
            pgt = self.bk2()
            for c in range(2):
                for k in range(8):
                    self.mm(pgt[0:P, c * 512:(c + 1) * 512], p["hxT"][:, k, 0:P], self.w_pg[:, k, c * 512:(c + 1) * 512], start=(k == 0), stop=(k == 7))
            for c in range(2):
                self.act(sig[0:P, c * 512:(c + 1) * 512], pgt[0:P, c * 512:(c + 1) * 512], AF.Exp, scale=-1.0)
            for c in range(2):
                self.act(sig[0:P, c * 512:(c + 1) * 512], sig[0:P, c * 512:(c + 1) * 512], AF.Ln, bias=self.one_t[0:P, :])
            for c in range(2):
                self.act(sig[0:P, c * 512:(c + 1) * 512], sig[0:P, c * 512:(c + 1) * 512], AF.Exp, scale=-1.0)
            ppp = self.bk2()
            for c in range(2):
                for k in range(2):
                    self.mm(ppp[0:P, c * 512:(c + 1) * 512], p["pT"][:, k, 0:P], self.w_pp[:, k, c * 512:(c + 1) * 512], start=(k == 0), stop=(k == 1))
            for c in range(2):
                self.tt("dve", sig[0:P, c * 512:(c + 1) * 512], ppp[0:P, c * 512:(c + 1) * 512], sig[0:P, c * 512:(c + 1) * 512], ALU.mult)
            self.tt("dve", x2[0:P, :], x2[0:P, :], sig[0:P, :], ALU.add)
            S.dma(seq["xo"][tok0:tok0 + P, :], x2[0:P, :])

    def build(self):
        S = self.S
        d, o = self.din, self.dout
        LP, LS, PAST = self.LP, self.LS, self.PAST
        self.phase_log = []
        plog = lambda name: self.phase_log.append((name, dict(S.nm)))
        self.plog = plog
        self.setup_consts()
        self.alloc_p1()
        self.alloc_p2()
        self.alloc_p3()
        import os
        STOP = int(os.environ.get("KSTOP", "99"))
        for l in range(NL):
            if STOP < 99 and l > 0:
                break
            seqP = dict(P=128, nt=LP // 128, kt0=0, ca0=0, hist=False,
                        x=(d["xp"] if l == 0 else self.x1p), xo=(self.x1p if l < NL - 1 else o["y_p"]),
                        rope=d["ropeP"], p=d["pp"], QT=self.QTp, KT=self.KTp, VA=self.VAp, OT=self.OTp,
                        o_ckv=o["ckv_p"], o_kr=o["kr_p"], o_cak=o["cak_p"], o_cav=o["cav_p"], o_pool=[o["pool_p"][i] for i in range(NL)],
                        nq=LP, nkt_total=LP // 128, nk_last=128)
            seqS = dict(P=LS, nt=1, kt0=PAST // 128, ca0=4, hist=True,
                        x=(d["xs"] if l == 0 else self.x1s), xo=(self.x1s if l < NL - 1 else o["y_s"]),
                        rope=d["ropeS"], p=d["ps"], QT=self.QTs, KT=self.KTs, VA=self.VAs, OT=self.OTs,
                        o_ckv=o["ckv_s"], o_kr=o["kr_s"], o_cak=o["cak_s"], o_cav=o["cav_s"], o_pool=[o["pool_s"][i] for i in range(NL)],
                        nq=LS, nkt_total=PAST // 128 + 1, nk_last=LS)
            S.dma_fence()
            plog("L%d start" % l)
            self.load_layer_small(l)
            if STOP <= 0:
                break
            self.load_p1_weights(l)
            if STOP <= 1:
                break
            plog("L%d p1w done" % l)
            self.run_interleaved([self.p1_tile(l, seqP, t) for t in range(seqP["nt"])], 4)
            plog("L%d p1 prompt done" % l)
            self.load_pg(l)
            if STOP <= 2:
                break
            self.p1_hist(l, seqS)
            self.run_interleaved([self.p1_tile(l, seqS, 0)], 4)
            S.dma_fence()
            if STOP <= 3:
                break
            plog("L%d p1 sample done" % l)
            self.p2_seq(seqP, True)
            plog("L%d p2 prompt done" % l)
            self.p2_seq(seqS, False)
            plog("L%d p2 sample done" % l)
            S.dma_fence()
            if STOP <= 4:
                break
            self.load_p3_weights(l)
            if STOP <= 5:
                break
            plog("L%d p3w done" % l)
            G = 256
            groups = [(g0, [(g0 + j * 128, 128) for j in range(G // 128)]) for g0 in range(0, LP, G)]
            groups.append(None)
            self.p3_head(l, seqP, groups[0][0], groups[0][1], 0)
            for gi in range(len(groups) - 1):
                g0, tl = groups[gi]
                if groups[gi + 1] is not None:
                    nxt = groups[gi + 1]
                    hook = (lambda nxt=nxt, gi=gi: self.p3_head(l, seqP, nxt[0], nxt[1], gi + 1))
                else:
                    hook = (lambda gi=gi: self.p3_head(l, seqS, 0, [(0, LS)], gi + 1))
                self.p3_group(l, seqP, g0, tl, gi, hook)
            plog("L%d p3 prompt done" % l)
            self.p3_group(l, seqS, 0, [(0, LS)], len(groups) - 1, None)
            plog("L%d p3 sample done" % l)
        S.dma_fence()
        nc = self.nc
        with nc.Block() as block:
            @block.tensor
            def _(e):
                S.replay("pe", e)

            @block.scalar
            def _(e):
                S.replay("act", e)

            @block.vector
            def _(e):
                S.replay("dve", e)

            @block.gpsimd
            def _(e):
                S.replay("pool", e)

            @block.sync
            def _(e):
                S.replay("sp", e)
        return nc


def _rope_table(pos):
    half = 16
    inv = 10000.0 ** (-np.arange(half, dtype=np.float64) / half)
    ang = (pos.astype(np.float32)[:, None] * inv.astype(np.float32)[None, :]).astype(np.float32)
    c = np.cos(ang.astype(np.float64)); s = np.sin(ang.astype(np.float64))
    return np.concatenate([c, c, -s, s], axis=1).astype(np.float32)


def _bands():
    out = np.zeros((3, 4, 128, 128), np.float32)
    i = np.arange(128)[:, None]; j = np.arange(128)[None, :]
    for g, w in enumerate((2, 4, 8, 16)):
        inw = ((j - i) >= 0) & ((j - i) < w)
        out[0, g] = inw / w - (i == j)
        out[1, g] = (((j + 128 - i) >= 0) & ((j + 128 - i) < w)) / w
        cnt = np.minimum(j + 1, w)
        out[2, g] = inw / cnt - (i == j)
    return np.ascontiguousarray(out.reshape(12, 128, 128).transpose(1, 0, 2))


def _bias_idx():
    idx = np.zeros((5, 128, 128), np.int64); msk = np.zeros((5, 128, 128), bool)
    r = np.arange(128)[:, None]; c = np.arange(128)[None, :]
    for du in range(5):
        rel = (du - 4) * 128 + r - c
        idx[du] = np.clip(rel, -128, 128) + 128
        kc = ((du - 4) * 128 + r) // 64
        qc = c // 64
        msk[du] = (kc > qc) | (kc < qc - 8)
    return idx, msk


def _prep(inp, c, LP, PAST, LS):
    f = np.float32
    m = {}
    m["xp"] = inp["x_prompt"][c]; m["xs"] = inp["x_sample"][c]
    m["c_ckv"] = inp["cache_mla_ckv"][:, c]; m["c_kr"] = inp["cache_mla_krope"][:, c]
    m["c_cak"] = inp["cache_ca_k"][:, c].reshape(NL, 512, 256); m["c_cav"] = inp["cache_ca_v"][:, c].reshape(NL, 512, 256)
    m["st_pool"] = inp["state_pool"][:, c]
    m["pp"] = inp["p_prompt"][:, c]; m["ps"] = inp["p_sample"][:, c]
    return {k: np.ascontiguousarray(v, dtype=f) for k, v in m.items()}


def _shared(inp, LP, PAST, LS):
    f = np.float32
    m = {}
    for k_, n in (("w_in", "w_in"), ("w_uq", "mla_w_uq"), ("w_ukv", "mla_w_ukv"), ("w_o_mla", "w_o_mla"), ("pool_w", "pool_w"),
                  ("w_o_pool", "w_o_pool"), ("w_o_ca", "w_o_ca"), ("w_out", "w_out"), ("w_pg", "w_ple_gate"), ("w_pp", "w_ple_proj")):
        m[k_] = inp[n]
    gcol = np.concatenate([inp["norm_in"].reshape(NL, 8, 128).transpose(0, 2, 1), inp["ple_norm"].reshape(NL, 8, 128).transpose(0, 2, 1),
                           inp["mla_q_norm"].reshape(NL, 2, 128).transpose(0, 2, 1)], axis=2)
    m["gcol"] = gcol
    caq = np.tile(inp["ca_qn"][:, None, :], (1, 4, 1)).reshape(NL, 256)
    cak = np.tile(inp["ca_kn"][:, None, :], (1, 4, 1)).reshape(NL, 256)
    m["grow"] = np.concatenate([inp["mla_kv_norm"], inp["mla_kn_rope"], inp["mla_qn_nope"], inp["mla_qn_rope"], inp["mla_kn_nope"],
                                caq, cak], axis=1)[:, None, :]
    m["plsc"] = inp["pool_scale"].reshape(NL, 4, 64).transpose(0, 2, 1)
    idx, msk = _bias_idx()
    tab = inp["ca_rel_bias"]
    bt = tab[:, :, idx]
    bt = np.where(msk[None, None], f(-30000.0), bt)
    m["biasT"] = bt.transpose(0, 3, 1, 2, 4).reshape(NL, 128, 20, 128)
    m["ident"] = np.eye(128, dtype=f)
    m["bands"] = _bands()
    m["ropeP"] = _rope_table(np.arange(LP)); m["ropeS"] = _rope_table(PAST + np.arange(LS))
    return {k: np.ascontiguousarray(v, dtype=f) for k, v in m.items()}


_CACHE = {}


def run(inp, ncores, LP, PAST, LS=64):
    key = (LP, PAST, LS)
    if key not in _CACHE:
        _CACHE[key] = K(LP, PAST, LS).build()
    nc = _CACHE[key]
    sh = _shared(inp, LP, PAST, LS)
    in_maps = []
    for c in range(ncores):
        mm_ = dict(sh)
        mm_.update(_prep(inp, c, LP, PAST, LS))
        in_maps.append(mm_)
    res = run_bass_kernel_spmd(nc, in_maps, core_ids=list(range(ncores)))
    R = res.results
    st = lambda k, ax: np.stack([r[k] for r in R], axis=ax)
    y_p = st("y_p", 0); y_s = st("y_s", 0)
    outs = (y_p, y_s, st("ckv_p", 1), st("kr_p", 1),
            st("cak_p", 1).reshape(NL, ncores, -1, 4, 64), st("cav_p", 1).reshape(NL, ncores, -1, 4, 64), st("pool_p", 1),
            st("ckv_s", 1), st("kr_s", 1), st("cak_s", 1).reshape(NL, ncores, -1, 4, 64), st("cav_s", 1).reshape(NL, ncores, -1, 4, 64),
            st("pool_s", 1))
    return tuple(np.ascontiguousarray(o, dtype=np.float32) for o in outs)


def kernel(**inputs):
    inp = {k: np.asarray(v) for k, v in inputs.items()}
    return run(inp, 8, 8192, 4096, 64)
```
